# Optimizing a Trainium2 kernel written in Bass

```python
import math
import jax, jax.numpy as jnp
from jax import lax
import numpy as np

D_MODEL = 1024
BATCH = 8
SEQ = 4096
DEPTH = 4

D_FF = 2816
GDN_HEADS = 8
GDN_DK = 128
GDN_DV = 128
CONV_K = 4
CHUNK = 64
MLA_HEADS = 8
Q_LORA = 384
KV_LORA = 256
QK_NOPE = 128
QK_ROPE = 64
V_HEAD = 128
ROPE_THETA = 10000.0
Q_BLOCK = 128
EPS = 1e-6
N_MOD = 9

GDN_QK = GDN_HEADS * GDN_DK
GDN_V = GDN_HEADS * GDN_DV
GDN_CONV_C = 2 * GDN_QK + GDN_V
MLA_QK = QK_NOPE + QK_ROPE
IN_SIZES = (GDN_CONV_C, GDN_V, GDN_HEADS, GDN_HEADS, Q_LORA, KV_LORA, QK_ROPE, D_MODEL, D_MODEL)
N_IN = GDN_CONV_C + GDN_V + 2 * GDN_HEADS + Q_LORA + KV_LORA + QK_ROPE + 2 * D_MODEL

kernel_name = 'hybrid_gdn_mla_macaron_adaln'


def rms_norm(x, gain):
    xf = x.astype(jnp.float32)
    y = xf * lax.rsqrt(jnp.mean(xf * xf, axis=-1, keepdims=True) + EPS)
    return (y * gain.astype(jnp.float32)).astype(x.dtype)


def l2_norm(x):
    xf = x.astype(jnp.float32)
    return xf * lax.rsqrt(jnp.sum(xf * xf, axis=-1, keepdims=True) + EPS)


def modulate(x, gain, shift, scale):
    return rms_norm(x, gain) * (1.0 + scale[:, None]) + shift[:, None]


def swiglu(h, w1, w3, w2):
    return (jax.nn.silu(h @ w1) * (h @ w3)) @ w2


def causal_dwconv(x, w):
    S = x.shape[1]
    xp = jnp.pad(x, ((0, 0), (CONV_K - 1, 0), (0, 0)))
    return sum(xp[:, j:j + S] * w[j] for j in range(CONV_K))


def rope_tables(positions):
    half = QK_ROPE // 2
    inv_freq = ROPE_THETA ** (-jnp.arange(half, dtype=jnp.float32) / half)
    ang = positions.astype(jnp.float32)[..., None] * inv_freq
    return jnp.cos(ang), jnp.sin(ang)


def apply_rope(x, cos, sin):
    x1, x2 = jnp.split(x, 2, axis=-1)
    c = cos[:, :, None].astype(x.dtype)
    s = sin[:, :, None].astype(x.dtype)
    return jnp.concatenate([x1 * c - x2 * s, x2 * c + x1 * s], axis=-1)


def gated_delta_rule(q, k, v, g, beta):
    B, S, H, DK = q.shape
    DV = v.shape[-1]
    N = S // CHUNK
    f32 = jnp.float32

    def chunks(t):
        t = t.astype(f32).reshape((B, N, CHUNK, H) + t.shape[3:])
        return jnp.moveaxis(t, (1, 3), (0, 2))

    qc = chunks(q) * (DK ** -0.5)
    kc, vc = chunks(k), chunks(v)
    gc, bc = chunks(g), chunks(beta)
    gcum = jnp.cumsum(gc, axis=-1)
    causal = jnp.tril(jnp.ones((CHUNK, CHUNK), bool))
    strict = jnp.tril(jnp.ones((CHUNK, CHUNK), bool), -1)
    decay = jnp.exp(jnp.where(causal, gcum[..., :, None] - gcum[..., None, :], -jnp.inf))
    kb = kc * bc[..., None]
    m = jnp.where(strict, jnp.einsum('nbhid,nbhjd->nbhij', kb, kc) * decay, 0.0)
    eye = jnp.eye(CHUNK, dtype=f32)
    t_inv = lax.linalg.triangular_solve(eye + m, jnp.broadcast_to(eye, m.shape), left_side=True, lower=True)
    u = t_inv @ (vc * bc[..., None])
    w = t_inv @ (kb * jnp.exp(gcum)[..., None])
    qk = jnp.einsum('nbhid,nbhjd->nbhij', qc, kc) * decay
    q_dec = qc * jnp.exp(gcum)[..., None]
    k_dec = kc * jnp.exp(gcum[..., -1:] - gcum)[..., None]
    g_last = jnp.exp(gcum[..., -1])

    def step(state, inp):
        u_i, w_i, qk_i, qd_i, kd_i, gl_i = inp
        v_new = u_i - w_i @ state
        o_i = qd_i @ state + qk_i @ v_new
        state = state * gl_i[..., None, None] + jnp.swapaxes(kd_i, -1, -2) @ v_new
        return state, o_i

    s0 = jnp.zeros((B, H, DK, DV), f32)
    _, o = lax.scan(step, s0, (u, w, qk, q_dec, k_dec, g_last))
    return jnp.moveaxis(o, (0, 2), (1, 3)).reshape(B, S, H, DV)


def gated_deltanet(qkv, z, b_raw, a_raw, conv_w, a_log, dt_bias, out_gain):
    B, S, _ = qkv.shape
    qkv_c = jax.nn.silu(causal_dwconv(qkv, conv_w))
    q, k, v = jnp.split(qkv_c, [GDN_QK, 2 * GDN_QK], axis=-1)
    q = l2_norm(q.reshape(B, S, GDN_HEADS, GDN_DK))
    k = l2_norm(k.reshape(B, S, GDN_HEADS, GDN_DK))
    v = v.reshape(B, S, GDN_HEADS, GDN_DV)
    beta = jax.nn.sigmoid(b_raw.astype(jnp.float32))
    g = -jnp.exp(a_log.astype(jnp.float32)) * jax.nn.softplus(a_raw.astype(jnp.float32) + dt_bias.astype(jnp.float32))
    o = gated_delta_rule(q, k, v, g, beta)
    o = rms_norm(o, out_gain) * jax.nn.silu(z.astype(jnp.float32).reshape(B, S, GDN_HEADS, GDN_DV))
    return o.reshape(B, S, GDN_V).astype(qkv.dtype)


def causal_block_attention(q, k, v):
    B, S, H, Dqk = q.shape
    nb = S // Q_BLOCK
    scale = Dqk ** -0.5
    qb = jnp.moveaxis(q.reshape(B, nb, Q_BLOCK, H, Dqk), 1, 0)
    kpos = jnp.arange(S)

    def one_block(args):
        i, q_i = args
        s = jnp.einsum('bqhd,bkhd->bhqk', q_i, k, preferred_element_type=jnp.float32) * scale
        qpos = i * Q_BLOCK + jnp.arange(Q_BLOCK)
        s = jnp.where(kpos[None, :] <= qpos[:, None], s, -jnp.inf)
        p = jax.nn.softmax(s, axis=-1)
        return jnp.einsum('bhqk,bkhv->bqhv', p.astype(v.dtype), v)

    o = lax.map(one_block, (jnp.arange(nb), qb))
    return jnp.moveaxis(o, 0, 1).reshape(B, S, H, v.shape[-1])


def mla(q_lat, kv_lat, k_pe, cos, sin, q_lat_gain, kv_lat_gain, w_q_up, w_kv_up, q_norm, k_norm):
    B, S, _ = q_lat.shape
    q = (rms_norm(q_lat, q_lat_gain) @ w_q_up).reshape(B, S, MLA_HEADS, MLA_QK)
    kv = (rms_norm(kv_lat, kv_lat_gain) @ w_kv_up).reshape(B, S, MLA_HEADS, QK_NOPE + V_HEAD)
    k_nope, v = jnp.split(kv, [QK_NOPE], axis=-1)
    k = jnp.concatenate([k_nope, jnp.broadcast_to(k_pe[:, :, None, :], (B, S, MLA_HEADS, QK_ROPE))], axis=-1)
    q = rms_norm(q, q_norm)
    k = rms_norm(k, k_norm)
    q = jnp.concatenate([q[..., :QK_NOPE], apply_rope(q[..., QK_NOPE:], cos, sin)], axis=-1)
    k = jnp.concatenate([k[..., :QK_NOPE], apply_rope(k[..., QK_NOPE:], cos, sin)], axis=-1)
    o = causal_block_attention(q, k, v)
    return o.reshape(B, S, MLA_HEADS * V_HEAD)


def token_mix(h, cos, sin, w_in, gdn_conv, gdn_a_log, gdn_dt_bias, gdn_out_gain, q_lat_gain, kv_lat_gain,
              w_q_up, w_kv_up, q_norm, k_norm, w_branch_a, w_branch_b, w_out):
    offsets = np.cumsum(IN_SIZES)[:-1].tolist()
    qkv, z, b_raw, a_raw, q_lat, kv_lat, k_pe, gate_a, gate_b = jnp.split(h @ w_in, offsets, axis=-1)
    y_a = gated_deltanet(qkv, z, b_raw, a_raw, gdn_conv, gdn_a_log, gdn_dt_bias, gdn_out_gain) @ w_branch_a
    y_b = mla(q_lat, kv_lat, k_pe, cos, sin, q_lat_gain, kv_lat_gain, w_q_up, w_kv_up, q_norm, k_norm) @ w_branch_b
    y = jax.nn.sigmoid(gate_a) * y_a + jax.nn.sigmoid(gate_b) * y_b
    return y @ w_out


def setup_inputs(seed: int = 0) -> dict:
    key = jax.random.key(seed)
    ks = iter(jax.random.split(key, 40))
    L, D = DEPTH, D_MODEL

    def normal(shape, fan_in, gain=1.0):
        return jax.random.normal(next(ks), shape, jnp.float32) * (gain * fan_in ** -0.5)

    def ones_noise(shape):
        return 1.0 + 0.05 * jax.random.normal(next(ks), shape, jnp.float32)

    x = jax.random.normal(next(ks), (BATCH, SEQ, D), jnp.float32)
    c = jax.random.normal(next(ks), (BATCH, D), jnp.float32)
    positions = (jax.random.randint(next(ks), (BATCH, 1), 0, 1024, jnp.int32)
                 + jnp.arange(SEQ, dtype=jnp.int32)[None, :])
    ada_w = normal((L, D, N_MOD * D), D, 0.25)
    ada_b = 0.01 * jax.random.normal(next(ks), (L, N_MOD * D), jnp.float32)
    norm_ffn1 = ones_noise((L, D))
    ffn1_w1 = normal((L, D, D_FF), D)
    ffn1_w3 = normal((L, D, D_FF), D)
    ffn1_w2 = normal((L, D_FF, D), D_FF)
    norm_mix = ones_noise((L, D))
    w_in = normal((L, D, N_IN), D)
    gdn_conv = normal((L, CONV_K, GDN_CONV_C), CONV_K)
    gdn_a_log = jnp.log(jax.random.uniform(next(ks), (L, GDN_HEADS), jnp.float32, 1.0, 16.0))
    dt = jnp.exp(jax.random.uniform(next(ks), (L, GDN_HEADS), jnp.float32, math.log(1e-3), math.log(1e-1)))
    gdn_dt_bias = dt + jnp.log(-jnp.expm1(-dt))
    gdn_out_gain = ones_noise((L, GDN_DV))
    mla_q_lat_gain = ones_noise((L, Q_LORA))
    mla_kv_lat_gain = ones_noise((L, KV_LORA))
    mla_w_q_up = normal((L, Q_LORA, MLA_HEADS * MLA_QK), Q_LORA)
    mla_w_kv_up = normal((L, KV_LORA, MLA_HEADS * (QK_NOPE + V_HEAD)), KV_LORA)
    mla_q_norm = ones_noise((L, MLA_QK))
    mla_k_norm = ones_noise((L, MLA_QK))
    w_branch_a = normal((L, GDN_V, D), GDN_V)
    w_branch_b = normal((L, MLA_HEADS * V_HEAD, D), MLA_HEADS * V_HEAD)
    w_out = normal((L, D, D), D)
    norm_ffn2 = ones_noise((L, D))
    ffn2_w1 = normal((L, D, D_FF), D)
    ffn2_w3 = normal((L, D, D_FF), D)
    ffn2_w2 = normal((L, D_FF, D), D_FF)
    return {'x': x, 'c': c, 'positions': positions, 'ada_w': ada_w, 'ada_b': ada_b,
            'norm_ffn1': norm_ffn1, 'ffn1_w1': ffn1_w1, 'ffn1_w3': ffn1_w3, 'ffn1_w2': ffn1_w2,
            'norm_mix': norm_mix, 'w_in': w_in, 'gdn_conv': gdn_conv, 'gdn_a_log': gdn_a_log,
            'gdn_dt_bias': gdn_dt_bias, 'gdn_out_gain': gdn_out_gain, 'mla_q_lat_gain': mla_q_lat_gain,
            'mla_kv_lat_gain': mla_kv_lat_gain, 'mla_w_q_up': mla_w_q_up, 'mla_w_kv_up': mla_w_kv_up,
            'mla_q_norm': mla_q_norm, 'mla_k_norm': mla_k_norm, 'w_branch_a': w_branch_a,
            'w_branch_b': w_branch_b, 'w_out': w_out, 'norm_ffn2': norm_ffn2, 'ffn2_w1': ffn2_w1,
            'ffn2_w3': ffn2_w3, 'ffn2_w2': ffn2_w2}


def reference(x, c, positions, ada_w, ada_b, norm_ffn1, ffn1_w1, ffn1_w3, ffn1_w2, norm_mix, w_in, gdn_conv,
              gdn_a_log, gdn_dt_bias, gdn_out_gain, mla_q_lat_gain, mla_kv_lat_gain, mla_w_q_up, mla_w_kv_up,
              mla_q_norm, mla_k_norm, w_branch_a, w_branch_b, w_out, norm_ffn2, ffn2_w1, ffn2_w3, ffn2_w2):
    cos, sin = rope_tables(positions)
    cond = jax.nn.silu(c)
    for l in range(DEPTH):
        mod = cond @ ada_w[l] + ada_b[l]
        sh1, sc1, g1, sh2, sc2, g2, sh3, sc3, g3 = jnp.split(mod, N_MOD, axis=-1)
        h = modulate(x, norm_ffn1[l], sh1, sc1)
        x = x + 0.5 * (1.0 + g1)[:, None] * swiglu(h, ffn1_w1[l], ffn1_w3[l], ffn1_w2[l])
        h = modulate(x, norm_mix[l], sh2, sc2)
        y = token_mix(h, cos, sin, w_in[l], gdn_conv[l], gdn_a_log[l], gdn_dt_bias[l], gdn_out_gain[l],
                      mla_q_lat_gain[l], mla_kv_lat_gain[l], mla_w_q_up[l], mla_w_kv_up[l], mla_q_norm[l],
                      mla_k_norm[l], w_branch_a[l], w_branch_b[l], w_out[l])
        x = x + (1.0 + g2)[:, None] * y
        h = modulate(x, norm_ffn2[l], sh3, sc3)
        x = x + 0.5 * (1.0 + g3)[:, None] * swiglu(h, ffn2_w1[l], ffn2_w3[l], ffn2_w2[l])
    return x
```

```python
import numpy as np
from contextlib import ExitStack
import concourse.bass as bass
import concourse.mybir as mybir
from concourse.bass_utils import run_bass_kernel_spmd

F32 = mybir.dt.float32
BF16 = mybir.dt.bfloat16
I32 = mybir.dt.int32
AF = mybir.ActivationFunctionType
ALU = mybir.AluOpType
AX = mybir.AxisListType

D = 1024
KD = 8
DFF = 2816
KF = 22
NIN = 6864
EPS = 1e-6
TT = 512


class Res:
    __slots__ = ("w", "r", "name", "excl")

    def __init__(self, name="", excl=False):
        self.w = None
        self.r = {}
        self.name = name
        self.excl = excl


class Instr:
    __slots__ = ("eng", "fn", "deps", "idx", "signal", "sem", "semval", "dma", "gidx")

    def __init__(self, eng, fn, dma):
        self.eng = eng
        self.fn = fn
        self.dma = dma
        self.deps = set()
        self.signal = False
        self.sem = None
        self.semval = 0


NDMA_SEM = 12


class Prog:
    ENGS = ("pe", "act", "dve", "pool", "sp")

    def __init__(self, nc):
        self.nc = nc
        self.instrs = []
        self.cnt = {e: 0 for e in self.ENGS}
        self.last = {e: None for e in self.ENGS}
        self.dma_hist = {e: [] for e in self.ENGS}
        self.out_dmas = []

    def op(self, eng, fn, r=(), w=(), dma=False):
        ins = Instr(eng, fn, dma)
        ins.idx = self.cnt[eng]
        self.cnt[eng] += 1
        ins.gidx = len(self.instrs)
        if any(x.excl for x in r):
            w = list(w) + [x for x in r if x.excl]
            r = [x for x in r if not x.excl]
        deps = ins.deps
        for x in r:
            if x.w is not None:
                deps.add(x.w)
        for x in w:
            if x.w is not None:
                deps.add(x.w)
            for v in x.r.values():
                if isinstance(v, list):
                    deps.update(v)
                else:
                    deps.add(v)
        for x in r:
            if dma:
                x.r.setdefault("dma", []).append(ins)
            else:
                x.r[eng] = ins
        for x in w:
            x.w = ins
            x.r = {}
        deps.discard(ins)
        if dma:
            h = self.dma_hist[eng]
            if len(h) >= NDMA_SEM:
                deps.add(h[-NDMA_SEM])
            h.append(ins)
        self.instrs.append(ins)
        self.last[eng] = ins
        return ins

    def barrier(self):
        lasts = [v for v in self.last.values() if v is not None]
        alld = []
        for e in self.ENGS:
            alld.extend(self.dma_hist[e][-NDMA_SEM:])
        for e in self.ENGS:
            eobj = e
            ins = self.op(e, (lambda en: en.nop()), dma=False)
            for d in lasts + alld:
                if d is not ins:
                    ins.deps.add(d)

    def emit(self):
        nc = self.nc
        eobj = {"pe": nc.tensor, "act": nc.scalar, "dve": nc.vector, "pool": nc.gpsimd, "sp": nc.sync}
        esem = {e: nc.alloc_semaphore("cs_" + e) for e in self.ENGS}
        dsem = {e: [nc.alloc_semaphore("ds_%s_%d" % (e, i)) for i in range(NDMA_SEM)] for e in ("sp", "pool", "act")}
        for ins in self.instrs:
            keep = set()
            for d in ins.deps:
                if (not d.dma) and d.eng == ins.eng:
                    if ins.eng == "pe":
                        continue
                    if ins.dma:
                        continue
                    if ins.idx - d.idx > 3:
                        continue
                keep.add(d)
            if ins.dma:
                for d in ins.deps:
                    if (not d.dma) and d.eng == ins.eng:
                        keep.add(d)
            ins.deps = keep
            for d in keep:
                d.signal = True
        cnt = {e: 0 for e in self.ENGS}
        dcnt = {e: 0 for e in self.ENGS}
        for ins in self.instrs:
            if ins.dma:
                k = dcnt[ins.eng]
                dcnt[ins.eng] += 1
                ins.sem = dsem[ins.eng][k % NDMA_SEM]
                ins.semval = 16 * (k // NDMA_SEM + 1)
            elif ins.signal:
                cnt[ins.eng] += 1
                ins.sem = esem[ins.eng]
                ins.semval = cnt[ins.eng]
        waited = {e: {} for e in self.ENGS}
        nwait = 0
        for ins in self.instrs:
            e = eobj[ins.eng]
            wd = waited[ins.eng]
            waits = {}
            for d in ins.deps:
                key = id(d.sem)
                if wd.get(key, 0) >= d.semval:
                    continue
                cur = waits.get(key)
                if cur is None or cur[1] < d.semval:
                    waits[key] = (d.sem, d.semval)
            wl = list(waits.values())
            for (s, v) in wl[:-1]:
                e.wait_ge(s, v)
                nwait += 1
            bi = ins.fn(e)
            if wl:
                bi._wait_ge(wl[-1][0], wl[-1][1])
            for key, (s, v) in waits.items():
                wd[key] = v
            if ins.dma:
                bi.then_inc(ins.sem, 16)
            elif ins.signal:
                bi.then_inc(ins.sem, 1)
        fin = {}
        for d in self.out_dmas:
            key = id(d.sem)
            if key not in fin or fin[key][1] < d.semval:
                fin[key] = (d.sem, d.semval)
        for (s, v) in fin.values():
            nc.sync.wait_ge(s, v)
        return nwait


NCONST = 640 + 4 * 512
BIG = 30000.0
QSCALE = 128.0 ** -0.5
ASCALE = 192.0 ** -0.5


def make_consts():
    c = np.zeros((128, NCONST), np.float32)
    c[:, 0:128] = np.eye(128, dtype=np.float32)
    k = np.arange(128)[:, None]
    i = np.arange(128)[None, :]
    c[:, 128:256] = (k <= i).astype(np.float32)
    c[:, 256:384] = np.where(i >= k, 0.0, -BIG)
    c[:, 384:512] = np.where(i > k, 0.0, -BIG)
    rot = np.zeros((128, 64), np.float32)
    for m in range(32):
        rot[m + 32, m] = -1.0
        rot[m, m + 32] = 1.0
    c[:, 512:576] = rot
    half = 32
    invf = (10000.0 ** (-(np.arange(half, dtype=np.float32)) / half)).astype(np.float32)
    c[0:64, 576] = np.concatenate([invf, invf])
    q = np.arange(512)[None, :]
    for d in range(4):
        c[:, 640 + d * 512:640 + (d + 1) * 512] = np.where(d * 128 + k <= q, 0.0, -BIG)
    return c


def build_program(S, L, stage="full", dbg=()):
    NT = S // TT
    NCH = S // 128
    nc = bass.Bass("TRN2", target_bir_lowering=False)
    P = Prog(nc)
    es = ExitStack()

    def dram_in(name, shape, dt=F32):
        return nc.dram_tensor(name, list(shape), dt, kind="ExternalInput").ap()

    def scratch(name, shape, dt):
        kind = "ExternalOutput" if name in dbg else "Internal"
        return nc.dram_tensor(name, list(shape), dt, kind=kind).ap()

    xT_d = dram_in("xT", [D, S])
    cT_d = dram_in("cT", [128, KD])
    pos_d = dram_in("pos", [1, S], I32)
    consts_d = dram_in("consts", [128, NCONST])
    adaw_d = dram_in("ada_w", [4, D, 9 * D])
    adab_d = dram_in("ada_bT", [4, 128, 72])
    gains_d = dram_in("gainsT", [4, 3, 128, KD])
    w1_d = [dram_in("ffn1_w1", [4, D, DFF]), dram_in("ffn2_w1", [4, D, DFF])]
    w3_d = [dram_in("ffn1_w3", [4, D, DFF]), dram_in("ffn2_w3", [4, D, DFF])]
    w2_d = [dram_in("ffn1_w2", [4, DFF, D]), dram_in("ffn2_w2", [4, DFF, D])]
    win_d = dram_in("w_in", [4, D, NIN])
    convT_d = dram_in("convT", [4, 128, 24, 4])
    hv_d = dram_in("headvec", [4, 128, 16])
    pv_d = dram_in("partvec", [4, 128, 16])
    wq_d = dram_in("mla_w_q_up", [4, 384, 1536])
    wkv_d = dram_in("mla_w_kv_up", [4, 256, 2048])
    wa_d = dram_in("w_branch_a", [4, D, D])
    wb_d = dram_in("w_branch_b", [4, D, D])
    wo_d = dram_in("w_out", [4, D, D])
    out_d = nc.dram_tensor("outT", [D, S], F32, kind="ExternalOutput").ap()

    qT_s = scratch("qT_s", [128, KD, S], BF16)
    kT_s = scratch("kT_s", [128, KD, S], BF16)
    vT_s = scratch("vT_s", [128, KD, S], BF16)
    zs_s = scratch("zs_s", [128, KD, S], BF16)
    ga_s = scratch("ga_s", [128, KD, S], BF16)
    gb_s = scratch("gb_s", [128, KD, S], BF16)
    bg_s = scratch("bg_s", [S, 24], F32)
    qm_s = scratch("qm_s", [8, 192, S], BF16)
    km_s = scratch("km_s", [8, 192, S], BF16)
    vm_s = scratch("vm_s", [S, D], BF16)
    og_s = scratch("og_s", [128, KD, S], BF16)
    om_s = scratch("om_s", [128, KD, S], BF16)
    import os as _os0
    if _os0.environ.get('GDN_DBG'):
        dbgU = nc.dram_tensor("dbgU", [NCH, 128, 2048], F32, kind="ExternalOutput").ap()
        dbgG = nc.dram_tensor("dbgG", [NCH, 128, 16], F32, kind="ExternalOutput").ap()
    r_scr = {n: Res(n) for n in ("q", "k", "v", "z", "ga", "gb", "bg", "qm", "km", "vm", "og", "om")}

    uid = [0]

    def un(name):
        uid[0] += 1
        return "%s_u%d" % (name, uid[0])

    def sb(name, shape, dt):
        return es.enter_context(nc.sbuf_tensor(name, list(shape), dt))

    def mm(out, lhsT, rhs, start, stop, r, w):
        return P.op("pe", lambda e: e.matmul(out, lhsT, rhs, start=start, stop=stop), r=r, w=w)

    def tr(out, in_, ident, r, w):
        return P.op("pe", lambda e: e.transpose(out, in_, ident), r=r, w=w)

    def act(out, in_, func, r, w, bias=None, scale=1.0, accum=None):
        def f(e):
            kw = {}
            if bias is not None:
                kw["bias"] = bias
            if accum is not None:
                kw["accum_out"] = accum
            return e.activation(out=out, in_=in_, func=func, scale=scale, **kw)
        return P.op("act", f, r=r, w=w)

    def tt(eng, out, in0, in1, op, r, w):
        return P.op(eng, lambda e: e.tensor_tensor(out, in0, in1, op), r=r, w=w)

    def stt(eng, out, in0, scalar, in1, op0, op1, r, w):
        return P.op(eng, lambda e: e.scalar_tensor_tensor(out=out, in0=in0, scalar=scalar, in1=in1, op0=op0, op1=op1), r=r, w=w)

    def tsc(eng, out, in0, s1, s2, op0, op1, r, w):
        if s2 is None:
            return P.op(eng, lambda e: e.tensor_scalar(out, in0, s1, None, op0), r=r, w=w)
        return P.op(eng, lambda e: e.tensor_scalar(out, in0, s1, s2, op0, op1), r=r, w=w)

    def cp(eng, out, in_, r, w):
        return P.op(eng, lambda e: e.tensor_copy(out, in_), r=r, w=w)

    def dma(eng, out, in_, r, w, final=False):
        ins = P.op(eng, lambda e: e.dma_start(out=out, in_=in_), r=r, w=w, dma=True)
        if final:
            P.out_dmas.append(ins)
        return ins

    class Ring:
        def __init__(self, ctx, name, shape, dt, n):
            self.t = [ctx.enter_context(nc.sbuf_tensor(un("%s_%d" % (name, i)), list(shape), dt)) for i in range(n)]
            self.r = [Res("%s_%d" % (name, i)) for i in range(n)]
            self.i = 0
            self.n = n

        def get(self):
            k = self.i % self.n
            self.i += 1
            return self.t[k], self.r[k]

    cst = sb("cst", [128, 640], F32)
    r_cst = Res("cst")
    dma("sp", cst[:], consts_d[:, 0:640], r=[], w=[r_cst])
    ident_f = cst[:, 0:128]
    tri_f = cst[:, 128:256]
    maski = cst[:, 256:384]
    masks = cst[:, 384:512]
    cbf = sb("cbf", [128, 192 + 2048], BF16)
    r_cbf = Res("cbf")
    dma("pool", cbf[:, 0:128], consts_d[:, 0:128], r=[], w=[r_cbf])
    dma("pool", cbf[:, 128:192], consts_d[:, 512:576], r=[], w=[r_cbf])
    dma("pool", cbf[:, 192:192 + 2048], consts_d[:, 640:640 + 2048], r=[], w=[r_cbf])
    ident_b = cbf[:, 0:128]
    rot_b = cbf[0:64, 128:192]
    ones_bf = sb("ones_bf", [128, 128], BF16)
    ones_f = sb("ones_f", [128, 128], F32)
    r_ones = Res("ones")
    P.op("dve", lambda e: e.memset(ones_bf[:], 1.0), w=[r_ones])
    P.op("dve", lambda e: e.memset(ones_f[:], 1.0), w=[r_ones])
    eps_t = sb("eps_t", [128, 1], F32)
    r_eps = Res("eps")
    P.op("dve", lambda e: e.memset(eps_t[:], EPS), w=[r_eps])

    NBANK = 6
    banks = [es.enter_context(nc.psum_tensor("bank%d" % i, [128, 512], F32)) for i in range(NBANK)]
    r_bank = [Res("bank%d" % i, excl=True) for i in range(NBANK)]
    bbank = [es.enter_context(nc.psum_tensor("bbank%d" % i, [128, 1024], BF16)) for i in range(2)]
    r_bb = [Res("bb0", excl=True), Res("bb1", excl=True)]
    bstate = {"i": 0}

    def nb():
        k = bstate["i"] % NBANK
        bstate["i"] += 1
        return k

    cs_s = scratch("cs_s", [2, 64, S], F32)
    r_rope = Res("rope")
    r_cs = Res("cs")
    with ExitStack() as es2:
        cos2 = es2.enter_context(nc.sbuf_tensor("cos2", [64, S], F32))
        sin2 = es2.enter_context(nc.sbuf_tensor("sin2", [64, S], F32))
        posi = es2.enter_context(nc.sbuf_tensor("posi", [64, S], I32))
        ang = es2.enter_context(nc.sbuf_tensor("ang", [64, S], F32))
        nn = es2.enter_context(nc.sbuf_tensor("nn", [64, S], F32))
        r_p = Res("posi")
        r_a = Res("ang")
        r_n = Res("nn")
        dma("sp", posi[:], pos_d[0:1, :].partition_broadcast(64), r=[], w=[r_p])
        cp("dve", ang[:], posi[:], r=[r_p], w=[r_a])
        tsc("dve", ang[:], ang[:], cst[0:64, 576:577], None, ALU.mult, ALU.bypass, r=[r_a, r_cst], w=[r_a])
        TWO_PI = 2.0 * np.pi
        C1 = 6.28125
        C2 = float(TWO_PI - C1)
        MAGIC = 12582912.0
        tsc("dve", nn[:], ang[:], float(1.0 / TWO_PI), MAGIC, ALU.mult, ALU.add, r=[r_a], w=[r_n])
        tsc("dve", nn[:], nn[:], -MAGIC, None, ALU.add, ALU.bypass, r=[r_n], w=[r_n])
        stt("dve", ang[:], nn[:], -C1, ang[:], ALU.mult, ALU.add, r=[r_n, r_a], w=[r_a])
        stt("dve", ang[:], nn[:], -C2, ang[:], ALU.mult, ALU.add, r=[r_n, r_a], w=[r_a])
        tsc("dve", ang[:], ang[:], float(np.pi), float(-np.pi), ALU.min, ALU.max, r=[r_a], w=[r_a])
        act(sin2[:], ang[:], AF.Sin, r=[r_a], w=[r_rope])
        tsc("dve", nn[:], ang[:], float(np.pi / 2), None, ALU.add, ALU.bypass, r=[r_a], w=[r_n])
        tsc("dve", ang[:], nn[:], float(np.pi), float(-TWO_PI), ALU.is_gt, ALU.mult, r=[r_n], w=[r_a])
        tt("dve", nn[:], nn[:], ang[:], ALU.add, r=[r_n, r_a], w=[r_n])
        tsc("dve", nn[:], nn[:], float(np.pi), float(-np.pi), ALU.min, ALU.max, r=[r_n], w=[r_n])
        act(cos2[:], nn[:], AF.Sin, r=[r_n], w=[r_rope])
        dma("sp", cs_s[0, :, :], cos2[:], r=[r_rope], w=[r_cs])
        dma("sp", cs_s[1, :, :], sin2[:], r=[r_rope], w=[r_cs])
        P.barrier()

    condT = sb("condT", [128, KD], F32)
    r_cond = Res("cond")
    dma("sp", condT[:], cT_d[:, :], r=[], w=[r_cond])
    act(condT[:], condT[:], AF.Silu, r=[r_cond], w=[r_cond])
    modT = sb("modT", [128, 4, 72], F32)
    r_mod = Res("mod")
    adab = sb("adab", [128, 4, 72], F32)
    r_adab = Res("adab")
    for l in range(4):
        dma("sp", adab[:, l, :], adab_d[l, :, :], r=[], w=[r_adab])
    gains = sb("gains", [128, 4, 3, KD], F32)
    r_gains = Res("gains")
    for l in range(4):
        for j in range(3):
            dma("sp", gains[:, l, j, :], gains_d[l, j, :, :], r=[], w=[r_gains])
    hvec = sb("hvec", [128, 4, 16], F32)
    pvec = sb("pvec", [128, 4, 16], F32)
    r_hv = Res("hv")
    for l in range(4):
        dma("sp", hvec[:, l, :], hv_d[l, :, :], r=[], w=[r_hv])
        dma("sp", pvec[:, l, :], pv_d[l, :, :], r=[], w=[r_hv])
    for l in range(4):
        act(hvec[:, l, 0:8], hvec[:, l, 0:8], AF.Exp, r=[r_hv], w=[r_hv])
        tsc("dve", hvec[:, l, 0:8], hvec[:, l, 0:8], -1.0, None, ALU.mult, ALU.bypass, r=[r_hv], w=[r_hv])
    NG = 8
    GW = 9 * D // NG
    with ExitStack() as es2:
        awb = [es2.enter_context(nc.sbuf_tensor(un("awb%d" % i), [128, KD, GW], F32)) for i in range(2)]
        r_awb = [Res("awb0"), Res("awb1")]
        gi = 0
        for l in range(L):
            for g in range(NG):
                bsel = gi % 2
                for k in range(KD):
                    dma("sp", awb[bsel][:, k, :], adaw_d[l, k * 128:(k + 1) * 128, g * GW:(g + 1) * GW], r=[], w=[r_awb[bsel]])
                bk = nb()
                nj = GW // 128
                for jj in range(nj):
                    for k in range(KD):
                        mm(banks[bk][:, jj:jj + 1], awb[bsel][:, k, jj * 128:(jj + 1) * 128], condT[:, k:k + 1],
                           k == 0, k == KD - 1, r=[r_awb[bsel], r_cond], w=[r_bank[bk]])
                tt("dve", modT[:, l, g * nj:(g + 1) * nj], banks[bk][:, 0:nj], adab[:, l, g * nj:(g + 1) * nj], ALU.add,
                   r=[r_bank[bk], r_adab], w=[r_mod])
                gi += 1
        P.barrier()
    vecA = sb("vecA", [128, 4, 3, KD], F32)
    vecG = sb("vecG", [128, 4, 3, KD], F32)
    r_vec = Res("vec")
    for l in range(L):
        for j in range(3):
            sc = modT[:, l, (3 * j + 1) * KD:(3 * j + 2) * KD]
            gg = modT[:, l, (3 * j + 2) * KD:(3 * j + 3) * KD]
            stt("dve", vecA[:, l, j, :], sc, 1.0, gains[:, l, j, :], ALU.add, ALU.mult, r=[r_mod, r_gains], w=[r_vec])
            gs = 1.0 if j == 1 else 0.5
            tsc("dve", vecG[:, l, j, :], gg, 1.0, gs, ALU.add, ALU.mult, r=[r_mod], w=[r_vec])

    def vA(l, j, k):
        return vecA[:, l, j, k:k + 1]

    def vB(l, j, k):
        return modT[:, l, 3 * j * KD + k:3 * j * KD + k + 1]

    def vG(l, j, k):
        return vecG[:, l, j, k:k + 1]

    P.barrier()

    r_res = [Res("res%d" % t) for t in range(NT)]
    state = {"first": True}

    def norm_tile(l, j, xt, r_xt, hb, r_hb, sq, r_sq, rstd, r_rstd, tmpring):
        act(sq[:], xt[:], AF.Square, r=r_xt, w=[r_sq])
        bk = nb()
        for k in range(KD):
            mm(banks[bk][:], ones_bf[:], sq[:, k, :], k == 0, k == KD - 1, r=[r_sq, r_ones], w=[r_bank[bk]])
        act(rstd[:], banks[bk][:], AF.Ln, r=[r_bank[bk], r_eps], w=[r_rstd], bias=eps_t[:], scale=1.0 / D)
        act(rstd[:], rstd[:], AF.Exp, r=[r_rstd], w=[r_rstd], scale=-0.5)
        for k in range(KD):
            tm, r_tm = tmpring.get()
            stt("dve", tm[:], xt[:, k, :], vA(l, j, k), rstd[:], ALU.mult, ALU.mult, r=[r_xt[k], r_rstd, r_vec], w=[r_tm])
            act(hb[:, k, :], tm[:], AF.Identity, r=[r_tm, r_mod], w=[r_hb[k]], bias=vB(l, j, k))

    def load_xt(t, xt, r_xt, eng="sp"):
        src = xT_d if state["first"] else out_d
        ts_ = slice(t * TT, (t + 1) * TT)
        for k in range(KD):
            dma(eng, xt[:, k, :], src[k * 128:(k + 1) * 128, ts_], r=[r_res[t]], w=[r_xt[k]])

    def ffn(l, j, which):
        with ExitStack() as fs:
            def fsb(name, shape, dt):
                return fs.enter_context(nc.sbuf_tensor(un(name), list(shape), dt))
            w1 = fsb("w1", [128, KD, DFF], BF16)
            w3 = fsb("w3", [128, KD, DFF], BF16)
            w2 = fsb("w2", [128, KF, D], BF16)
            xt = fsb("xt", [128, KD, TT], F32)
            hb = fsb("hb", [128, KD, TT], BF16)
            ub = fsb("ub", [128, KF, TT], BF16)
            sq = fsb("sq", [128, KD, TT], BF16)
            rstd = fsb("rstd", [128, TT], F32)
            tmpring = Ring(fs, "tmp", [128, TT], F32, 2)
            r_w1 = [Res() for _ in range(KD)]
            r_w3 = [Res() for _ in range(KD)]
            r_w2 = [Res() for _ in range(KF)]
            r_xt = [Res() for _ in range(KD)]
            r_hb = [Res() for _ in range(KD)]
            r_ub = [Res() for _ in range(KF)]
            r_sq = Res()
            r_rstd = Res()
            for k in range(KD):
                dma("pool", w1[:, k, :], w1_d[which][l, k * 128:(k + 1) * 128, :], r=[], w=[r_w1[k]])
                dma("pool", w3[:, k, :], w3_d[which][l, k * 128:(k + 1) * 128, :], r=[], w=[r_w3[k]])
            for f in range(KF):
                dma("pool", w2[:, f, :], w2_d[which][l, f * 128:(f + 1) * 128, :], r=[], w=[r_w2[f]])
            for t in range(NT):
                ts_ = slice(t * TT, (t + 1) * TT)
                load_xt(t, xt, r_xt)
                norm_tile(l, j, xt, r_xt, hb, r_hb, sq, r_sq, rstd, r_rstd, tmpring)
                for f in range(KF):
                    b1 = nb()
                    b3 = nb()
                    fs_ = slice(f * 128, (f + 1) * 128)
                    for k in range(KD):
                        mm(banks[b1][:], w1[:, k, fs_], hb[:, k, :], k == 0, k == KD - 1, r=[r_w1[k], r_hb[k]], w=[r_bank[b1]])
                    for k in range(KD):
                        mm(banks[b3][:], w3[:, k, fs_], hb[:, k, :], k == 0, k == KD - 1, r=[r_w3[k], r_hb[k]], w=[r_bank[b3]])
                    tm, r_tm = tmpring.get()
                    act(tm[:], banks[b1][:], AF.Silu, r=[r_bank[b1]], w=[r_tm])
                    tt("dve", ub[:, f, :], tm[:], banks[b3][:], ALU.mult, r=[r_tm, r_bank[b3]], w=[r_ub[f]])
                for d in range(KD):
                    bk = nb()
                    ds_ = slice(d * 128, (d + 1) * 128)
                    for f in range(KF):
                        mm(banks[bk][:], w2[:, f, ds_], ub[:, f, :], f == 0, f == KF - 1, r=[r_w2[f], r_ub[f]], w=[r_bank[bk]])
                    stt("dve", xt[:, d, :], banks[bk][:], vG(l, j, d), xt[:, d, :], ALU.mult, ALU.add,
                        r=[r_bank[bk], r_vec, r_xt[d]], w=[r_xt[d]])
                    dma("sp", out_d[ds_, ts_], xt[:, d, :], r=[r_xt[d]], w=[r_res[t]], final=True)
            P.barrier()
        state["first"] = False

    def mix_proj(l, part):
        with ExitStack() as fs:
            def fsb(name, shape, dt):
                return fs.enter_context(nc.sbuf_tensor(un(name), list(shape), dt))
            cbase = 0 if part == 0 else 4112
            NC_ = 4112 if part == 0 else NIN - 4112
            win = fsb("win", [128, KD, NC_], BF16)
            cstr = Ring(fs, "cst_t", [64, 2, TT], F32, 2)
            wq = fsb("wq", [128, 3, 1536 if part == 1 else 2], BF16)
            wkv = fsb("wkv", [128, 2, 2048 if part == 1 else 2], BF16)
            cw = fsb("cw", [128, 24, 4], F32)
            xt = fsb("xt", [128, KD, TT], F32)
            hb = fsb("hb", [128, KD, TT], BF16)
            sq = fsb("sq", [128, KD, TT], BF16)
            rstd = fsb("rstd", [128, TT], F32)
            halo = fsb("halo", [128, 24, 3], F32)
            ql = fsb("ql", [128, 3, TT], F32)
            qn = fsb("qn", [128, 3, TT], BF16)
            kvl = fsb("kvl", [128, 2, TT], F32)
            kvn = fsb("kvn", [128, 2, TT], BF16)
            kpe = fsb("kpe", [64, TT], F32)
            sqpe = fsb("sqpe", [64, TT], BF16)
            tmpring = Ring(fs, "tmp", [128, TT], F32, 3)
            prering = Ring(fs, "pre", [128, TT + 3], F32, 3)
            accring = Ring(fs, "acc", [128, TT], F32, 3)
            sring = Ring(fs, "s", [128, TT], F32, 8)
            sqring = Ring(fs, "sqb", [128, TT], BF16, 8)
            rsring = Ring(fs, "rs", [128, TT], F32, 3)
            obring = Ring(fs, "ob", [128, TT], BF16, 8)
            pending = []
            pendA = []
            pendB = []
            smring = Ring(fs, "sm", [128, 32], F32, 4)
            bgring = Ring(fs, "bgt", [128, 24], F32, 2)
            vtring = Ring(fs, "vt", [128, 512], BF16, 2)
            r_win = [Res() for _ in range(KD)]
            r_wq, r_wkv, r_cw, r_halo = Res(), Res(), Res(), Res()
            r_xt = [Res() for _ in range(KD)]
            r_hb = [Res() for _ in range(KD)]
            r_sq, r_rstd = Res(), Res()
            r_ql, r_qn, r_kvl, r_kvn, r_kpe, r_sqpe = Res(), Res(), Res(), Res(), Res(), Res()
            for k in range(KD):
                dma("pool", win[:, k, :], win_d[l, k * 128:(k + 1) * 128, cbase:cbase + NC_], r=[], w=[r_win[k]])
            for c in (range(3) if part == 1 else ()):
                dma("pool", wq[:, c, :], wq_d[l, c * 128:(c + 1) * 128, :], r=[], w=[r_wq])
            for c in (range(2) if part == 1 else ()):
                dma("pool", wkv[:, c, :], wkv_d[l, c * 128:(c + 1) * 128, :], r=[], w=[r_wkv])
            dma("sp", cw[:], convT_d[l, :, :, :], r=[], w=[r_cw])
            P.op("dve", lambda e: e.memset(halo[:], 0.0), w=[r_halo])
            pv = lambda c: pvec[:, l, c:c + 1]

            def proj(col0, M, bk, rows=128):
                for k in range(KD):
                    mm(banks[bk][0:M, :], win[:, k, col0 - cbase:col0 - cbase + M], hb[:, k, :], k == 0, k == KD - 1,
                       r=[r_win[k], r_hb[k]], w=[r_bank[bk]])

            def rsq_from_bank(bk, scale, rows=128):
                rs, r_rs = rsring.get()
                act(rs[0:rows, :], banks[bk][0:rows, :], AF.Ln, r=[r_bank[bk], r_eps], w=[r_rs], bias=eps_t[0:rows, :], scale=scale)
                act(rs[0:rows, :], rs[0:rows, :], AF.Exp, r=[r_rs], w=[r_rs], scale=-0.5)
                return rs, r_rs

            def rope_out(src, r_src, dst_dram, r_dst):
                bk = 5
                mm(banks[bk][0:64, :], rot_b, src, True, True, r=[r_src, r_cbf], w=[r_bank[bk]])
                t1, r_t1 = tmpring.get()
                tt("dve", t1[0:64, :], banks[bk][0:64, :], cst_t[:, 1, :], ALU.mult, r=[r_bank[bk], r_cst_t], w=[r_t1])
                t2, r_t2 = tmpring.get()
                tt("dve", t2[0:64, :], src, cst_t[:, 0, :], ALU.mult, r=[r_src, r_cst_t], w=[r_t2])
                ob, r_ob = obring.get()
                tt("dve", ob[0:64, :], t1[0:64, :], t2[0:64, :], ALU.add, r=[r_t1, r_t2], w=[r_ob])
                dma("sp", dst_dram, ob[0:64, :], r=[r_ob], w=[r_dst])

            load_xt(0, xt, r_xt, "pool")
            for t in range(NT):
                tsl = slice(t * TT, (t + 1) * TT)
                norm_tile(l, 1, xt, r_xt, hb, r_hb, sq, r_sq, rstd, r_rstd, tmpring)
                if t + 1 < NT:
                    load_xt(t + 1, xt, r_xt, "pool")
                if part == 1:
                    cst_t, r_cst_t = cstr.get()
                    dma("sp", cst_t[:], cs_s[:, :, tsl].rearrange("a p t -> p a t"), r=[r_cs], w=[r_cst_t])
                def stage1(c):
                    bk = nb()
                    proj(c * 128, 128, bk)
                    pre, r_pre = prering.get()
                    cp("dve", pre[:, 0:3], halo[:, c, :], r=[r_halo], w=[r_pre])
                    act(pre[:, 3:TT + 3], banks[bk][:], AF.Copy, r=[r_bank[bk]], w=[r_pre])
                    cp("dve", halo[:, c, :], pre[:, TT:TT + 3], r=[r_pre], w=[r_halo])
                    acc, r_acc = accring.get()
                    tsc("dve", acc[:], pre[:, 3:TT + 3], cw[:, c, 3:4], None, ALU.mult, ALU.bypass, r=[r_pre, r_cw], w=[r_acc])
                    for jj in (2, 1, 0):
                        stt("dve", acc[:], pre[:, jj:jj + TT], cw[:, c, jj:jj + 1], acc[:], ALU.mult, ALU.add,
                            r=[r_pre, r_cw, r_acc], w=[r_acc])
                    return acc, r_acc

                def stage2(c, acc, r_acc):
                    if c >= 16:
                        ob, r_ob = obring.get()
                        act(ob[:], acc[:], AF.Silu, r=[r_acc], w=[r_ob])
                        dma("sp", vT_s[:, c - 16, tsl], ob[:], r=[r_ob], w=[r_scr["v"]])
                    else:
                        s_, r_s = sring.get()
                        act(s_[:], acc[:], AF.Silu, r=[r_acc], w=[r_s])
                        sqb, r_sqb = sqring.get()
                        act(sqb[:], s_[:], AF.Square, r=[r_s], w=[r_sqb])

                        def tail(c=c, s_=s_, r_s=r_s, sqb=sqb, r_sqb=r_sqb, tsl=tsl):
                            b2 = nb()
                            mm(banks[b2][:], ones_bf[:], sqb[:], True, True, r=[r_sqb, r_ones], w=[r_bank[b2]])
                            rs, r_rs = rsq_from_bank(b2, 1.0)
                            ob, r_ob = obring.get()
                            tt("dve", ob[:], s_[:], rs[:], ALU.mult, r=[r_s, r_rs], w=[r_ob])
                            if c < 8:
                                dma("sp", qT_s[:, c, tsl], ob[:], r=[r_ob], w=[r_scr["q"]])
                            else:
                                dma("sp", kT_s[:, c - 8, tsl], ob[:], r=[r_ob], w=[r_scr["k"]])
                        pending.append(tail)
                    if len(pending) >= 6:
                        for _ in range(4):
                            pending.pop(0)()

                if part == 0:
                    prev = None
                    for c in range(24):
                        cur_ = stage1(c)
                        if prev is not None:
                            stage2(c - 1, *prev)
                        prev = cur_
                    stage2(23, *prev)
                while pending:
                    pending.pop(0)()
                for c in (range(8) if part == 0 else ()):
                    bk = nb()
                    proj(3072 + c * 128, 128, bk)
                    ob, r_ob = obring.get()
                    act(ob[:], banks[bk][:], AF.Silu, r=[r_bank[bk]], w=[r_ob])
                    dma("sp", zs_s[:, c, tsl], ob[:], r=[r_ob], w=[r_scr["z"]])
                for c in (range(16) if part == 1 else ()):
                    bk = nb()
                    proj(4816 + c * 128, 128, bk)
                    ob, r_ob = obring.get()
                    act(ob[:], banks[bk][:], AF.Sigmoid, r=[r_bank[bk]], w=[r_ob])
                    if c < 8:
                        dma("sp", ga_s[:, c, tsl], ob[:], r=[r_ob], w=[r_scr["ga"]])
                    else:
                        dma("sp", gb_s[:, c - 8, tsl], ob[:], r=[r_ob], w=[r_scr["gb"]])
                for sub in (range(4) if part == 0 else ()):
                    bk = nb()
                    ssl = slice(sub * 128, (sub + 1) * 128)
                    for k in range(KD):
                        mm(banks[bk][:, 0:16], hb[:, k, ssl], win[:, k, 4096:4112], k == 0, k == KD - 1,
                           r=[r_win[k], r_hb[k]], w=[r_bank[bk]])
                    sm, r_sm = smring.get()
                    bgt, r_bgt = bgring.get()
                    act(sm[:, 0:8], banks[bk][:, 0:8], AF.Exp, r=[r_bank[bk]], w=[r_sm], scale=-1.0)
                    act(sm[:, 8:16], sm[:, 0:8], AF.Ln, r=[r_sm], w=[r_sm], bias=1.0)
                    tsc("dve", bgt[:, 16:24], sm[:, 8:16], -1.0, None, ALU.mult, ALU.bypass, r=[r_sm], w=[r_bgt])
                    act(bgt[:, 0:8], bgt[:, 16:24], AF.Exp, r=[r_bgt], w=[r_bgt])
                    tt("dve", sm[:, 16:24], banks[bk][:, 8:16], hvec[:, l, 8:16], ALU.add, r=[r_bank[bk], r_hv, r_sm], w=[r_sm])
                    act(sm[:, 24:32], sm[:, 16:24], AF.Exp, r=[r_sm], w=[r_sm])
                    act(sm[:, 16:24], sm[:, 24:32], AF.Ln, r=[r_sm], w=[r_sm], bias=1.0)
                    tt("dve", bgt[:, 8:16], sm[:, 16:24], hvec[:, l, 0:8], ALU.mult, r=[r_sm, r_hv, r_bgt], w=[r_bgt])
                    dma("sp", bg_s[t * TT + sub * 128:t * TT + (sub + 1) * 128, :], bgt[:], r=[r_bgt], w=[r_scr["bg"]])
                if part == 0:
                    continue
                for c in range(3):
                    bk = nb()
                    proj(4112 + c * 128, 128, bk)
                    act(ql[:, c, :], banks[bk][:], AF.Copy, r=[r_bank[bk]], w=[r_ql])
                for c in range(2):
                    bk = nb()
                    proj(4496 + c * 128, 128, bk)
                    act(kvl[:, c, :], banks[bk][:], AF.Copy, r=[r_bank[bk]], w=[r_kvl])
                bk = nb()
                proj(4752, 64, bk)
                act(kpe[:], banks[bk][0:64, :], AF.Copy, r=[r_bank[bk]], w=[r_kpe])
                act(sqpe[:], kpe[:], AF.Square, r=[r_kpe], w=[r_sqpe])
                act(sq[:, 0:3, :], ql[:], AF.Square, r=[r_ql], w=[r_sq])
                bk = nb()
                for c in range(3):
                    mm(banks[bk][:], ones_bf[:], sq[:, c, :], c == 0, c == 2, r=[r_sq, r_ones], w=[r_bank[bk]])
                rs, r_rs = rsq_from_bank(bk, 1.0 / 384)
                for c in range(3):
                    stt("dve", qn[:, c, :], ql[:, c, :], pv(1 + c), rs[:], ALU.mult, ALU.mult, r=[r_ql, r_rs, r_hv], w=[r_qn])
                act(sq[:, 4:6, :], kvl[:], AF.Square, r=[r_kvl], w=[r_sq])
                bk = nb()
                for c in range(2):
                    mm(banks[bk][:], ones_bf[:], sq[:, 4 + c, :], c == 0, c == 1, r=[r_sq, r_ones], w=[r_bank[bk]])
                rs, r_rs = rsq_from_bank(bk, 1.0 / 256)
                for c in range(2):
                    stt("dve", kvn[:, c, :], kvl[:, c, :], pv(4 + c), rs[:], ALU.mult, ALU.mult, r=[r_kvl, r_rs, r_hv], w=[r_kvn])
                hcount = 0
                for isk in (0, 1):
                    for h in range(8):
                        bn = hcount % 2
                        br = 2 + hcount % 2
                        hcount += 1
                        if isk == 0:
                            for c in range(3):
                                mm(banks[bn][:], wq[:, c, h * 192:h * 192 + 128], qn[:, c, :], c == 0, c == 2, r=[r_wq, r_qn], w=[r_bank[bn]])
                            for c in range(3):
                                mm(banks[br][0:64, :], wq[:, c, h * 192 + 128:h * 192 + 192], qn[:, c, :], c == 0, c == 2, r=[r_wq, r_qn], w=[r_bank[br]])
                            sqr, r_sqr = sqring.get()
                            act(sqr[0:64, :], banks[br][0:64, :], AF.Square, r=[r_bank[br]], w=[r_sqr])
                            ropesrc, r_ropesrc = banks[br][0:64, :], r_bank[br]
                            gcol = 6
                        else:
                            for c in range(2):
                                mm(banks[bn][:], wkv[:, c, h * 256:h * 256 + 128], kvn[:, c, :], c == 0, c == 1, r=[r_wkv, r_kvn], w=[r_bank[bn]])
                            sqr, r_sqr = sqpe, r_sqpe
                            ropesrc, r_ropesrc = kpe[:], r_kpe
                            gcol = 8
                        sqn, r_sqn = sqring.get()
                        act(sqn[:], banks[bn][:], AF.Square, r=[r_bank[bn]], w=[r_sqn])

                        def tailA(h=h, isk=isk, bn=bn, sqn=sqn, r_sqn=r_sqn, sqr=sqr, r_sqr=r_sqr, ropesrc=ropesrc,
                                  r_ropesrc=r_ropesrc, gcol=gcol, tsl=tsl):
                            b2 = 4
                            mm(banks[b2][:], ones_bf[:], sqn[:], True, False, r=[r_sqn, r_ones], w=[r_bank[b2]])
                            mm(banks[b2][:], ones_bf[0:64, :], sqr[0:64, :], False, True, r=[r_sqr, r_ones], w=[r_bank[b2]])
                            rs, r_rs = rsq_from_bank(b2, 1.0 / 192)
                            ob, r_ob = obring.get()
                            stt("dve", ob[:], banks[bn][:], pv(gcol), rs[:], ALU.mult, ALU.mult, r=[r_bank[bn], r_rs, r_hv], w=[r_ob])
                            dst = qm_s if isk == 0 else km_s
                            r_dst = r_scr["qm"] if isk == 0 else r_scr["km"]
                            dma("sp", dst[h, 0:128, tsl], ob[:], r=[r_ob], w=[r_dst])
                            rb, r_rb = obring.get()
                            stt("dve", rb[0:64, :], ropesrc, pvec[0:64, l, gcol + 1:gcol + 2], rs[0:64, :], ALU.mult, ALU.mult,
                                r=[r_ropesrc, r_rs, r_hv], w=[r_rb])

                            def tailB():
                                rope_out(rb[0:64, :], r_rb, dst[h, 128:192, tsl], r_dst)
                            pendB.append(tailB)
                        pendA.append(tailA)
                        while len(pendA) > 1:
                            pendA.pop(0)()
                        while len(pendB) > 1:
                            pendB.pop(0)()
                while pendA:
                    pendA.pop(0)()
                while pendB:
                    pendB.pop(0)()
                for sub in range(4):
                    ssl = slice(sub * 128, (sub + 1) * 128)
                    for g in range(2):
                        bk = nb()
                        for c in range(2):
                            rhs = wkv[:, c, g * 1024:(g + 1) * 1024].rearrange("p (h x) -> p h x", x=256)[:, :, 128:256]
                            mm(banks[bk][:].rearrange("p (h x) -> p h x", x=128), kvn[:, c, ssl], rhs, c == 0, c == 1,
                               r=[r_wkv, r_kvn], w=[r_bank[bk]])
                        vt, r_vt = vtring.get()
                        act(vt[:], banks[bk][:], AF.Copy, r=[r_bank[bk]], w=[r_vt])
                        dma("sp", vm_s[t * TT + sub * 128:t * TT + (sub + 1) * 128, g * 512:(g + 1) * 512], vt[:], r=[r_vt], w=[r_scr["vm"]])
            P.barrier()
    def bc_h(ap2d):
        return ap2d.unsqueeze(1).to_broadcast([128, 8, ap2d.shape[1]])

    def bc_h4(ap2d):
        return ap2d.unsqueeze(1).to_broadcast([128, 4, ap2d.shape[1]])

    def bc_x(ap2d, n=128):
        return ap2d.unsqueeze(2).to_broadcast([128, ap2d.shape[1], n])

    def b4(bk):
        return banks[bk][:].rearrange("p (h x) -> p h x", x=128)

    def mix_gdn(l):
        with ExitStack() as fs:
            def fsb(name, shape, dt):
                return fs.enter_context(nc.sbuf_tensor(un(name), list(shape), dt))
            S_f = fsb("S_f", [128, 8, 128], F32)
            S_b = fsb("S_b", [128, 8, 128], BF16)
            r_S = [Res(), Res()]
            r_Sb = [Res(), Res()]
            P.op("dve", lambda e: e.memset(S_f[:], 0.0), w=r_S)
            P.op("dve", lambda e: e.memset(S_b[:], 0.0), w=r_Sb)
            qring = Ring(fs, "qc", [128, 8, 128], BF16, 2)
            kring = Ring(fs, "kc", [128, 8, 128], BF16, 2)
            vring = Ring(fs, "vc", [128, 8, 128], BF16, 2)
            zring = Ring(fs, "zc", [128, 8, 128], BF16, 2)
            bgring = Ring(fs, "bgc", [128, 24], F32, 2)
            gcring = Ring(fs, "gcs", [128, 16], F32, 2)
            egring = Ring(fs, "eg", [128, 16], F32, 2)
            Dg = fsb("Dg", [128, 8, 128], F32)
            r_Dg = Res()
            U = fsb("U", [128, 2, 8, 128], F32)
            r_U = Res()
            E = fsb("E", [128, 2, 8, 128], F32)
            r_E = Res()
            kdec = fsb("kdec", [128, 8, 128], BF16)
            r_kdec = Res()
            vtok = fsb("vtok", [128, 8, 128], BF16)
            r_vtok = Res()
            Ap = [fsb("ApA", [128, 8, 128], F32), fsb("ApB", [128, 8, 128], F32)]
            Mp = [fsb("MpA", [128, 8, 128], F32), fsb("MpB", [128, 8, 128], F32)]
            r_Ap = [[Res(), Res()], [Res(), Res()]]
            r_Mp = [[Res(), Res()], [Res(), Res()]]
            X = fsb("X", [128, 8, 128], F32)
            r_X = [Res(), Res()]
            Ybf = fsb("Ybf", [128, 8, 128], BF16)
            r_Y = Res()
            qkT = fsb("qkT", [128, 8, 128], BF16)
            r_qkT = Res()
            rr = fsb("rr", [128, 8, 128], BF16)
            r_rr = Res()
            vnew = fsb("vnew", [128, 8, 128], BF16)
            r_vnew = Res()
            tmpf = fsb("tmpf", [128, 8, 128], F32)
            r_tmpf = Res()
            o_f = fsb("o_f", [128, 8, 128], F32)
            r_of = Res()
            sqo = fsb("sqo", [128, 8, 128], F32)
            r_sqo = Res()
            ssr = Ring(fs, "ss", [128, 8], F32, 2)
            onb = fsb("onb", [128, 8, 128], BF16)
            r_on = Res()
            ogring = Ring(fs, "ogc", [128, 8, 128], BF16, 2)

            def hview(dr, csl):
                return dr[:, :, csl]

            import os as _os
            _c0 = int(_os.environ.get('GDN_C0', '0'))
            _cut = int(_os.environ.get('GDN_CUT', '99'))
            _cut2 = int(_os.environ.get('GDN_CUT2', '99'))
            _c1 = int(_os.environ.get('GDN_C1', str(NCH)))
            for c in range(_c0, min(NCH, _c1)):
                csl = slice(c * 128, (c + 1) * 128)
                qc, r_qc = qring.get()
                kc, r_kc = kring.get()
                vc, r_vc = vring.get()
                zc, r_zc = zring.get()
                bgc, r_bgc = bgring.get()
                for h in range(8):
                    dma("sp", qc[:, h, :], qT_s[:, h, csl], r=[r_scr["q"]], w=[r_qc])
                    dma("sp", kc[:, h, :], kT_s[:, h, csl], r=[r_scr["k"]], w=[r_kc])
                    dma("sp", vc[:, h, :], vT_s[:, h, csl], r=[r_scr["v"]], w=[r_vc])
                    dma("sp", zc[:, h, :], zs_s[:, h, csl], r=[r_scr["z"]], w=[r_zc])
                dma("sp", bgc[:], bg_s[csl, :], r=[r_scr["bg"]], w=[r_bgc])
                if _cut <= 1:
                    continue
                gcs, r_gcs = gcring.get()
                bk = nb()
                mm(banks[bk][:, 0:8], tri_f, bgc[:, 8:16], True, True, r=[r_cst, r_bgc], w=[r_bank[bk]])
                cp("dve", gcs[:, 0:8], banks[bk][:, 0:8], r=[r_bank[bk]], w=[r_gcs])
                tt("dve", gcs[:, 8:16], gcs[:, 0:8], bgc[:, 16:24], ALU.subtract, r=[r_gcs, r_bgc], w=[r_gcs])
                if _cut2 <= 1:
                    continue
                tt("dve", Dg[:], bc_h(ident_f), bc_x(gcs[:, 0:8]), ALU.mult, r=[r_cst, r_gcs], w=[r_Dg])
                bR = [nb(), nb()]
                for hf in range(2):
                    mm(banks[bR[hf]][:], ones_f[:], Dg[:, 4 * hf:4 * hf + 4, :].rearrange("p h x -> p (h x)"), True, True,
                       r=[r_ones, r_Dg], w=[r_bank[bR[hf]]])
                if _cut2 <= 2:
                    continue
                eg, r_eg = egring.get()
                for hf in range(2):
                    hs = slice(4 * hf, 4 * hf + 4)
                    tt("dve", U[:, 0, hs, :], b4(bR[hf]), bc_x(gcs[:, 4 * hf:4 * hf + 4]), ALU.subtract,
                       r=[r_bank[bR[hf]], r_gcs], w=[r_U])
                    tt("dve", U[:, 1, hs, :], b4(bR[hf]), bc_x(gcs[:, 8 + 4 * hf:8 + 4 * hf + 4]), ALU.subtract,
                       r=[r_bank[bR[hf]], r_gcs], w=[r_U])
                    act(eg[:, 8 + 4 * hf:8 + 4 * hf + 4], b4(bR[hf])[:, :, 127], AF.Exp, r=[r_bank[bR[hf]]], w=[r_eg])
                if _os.environ.get('GDN_DBG'):
                    dma("sp", dbgU[c, :, :], U[:].rearrange("p a h x -> p (a h x)"), r=[r_U], w=[Res()], final=True)
                    dma("sp", dbgG[c, :, :], gcs[:], r=[r_gcs], w=[Res()], final=True)
                if _cut2 <= 3:
                    continue
                tt("dve", U[:, 0, :, :], U[:, 0, :, :], bc_h(maski), ALU.min, r=[r_U, r_cst], w=[r_U])
                tt("dve", U[:, 1, :, :], U[:, 1, :, :], bc_h(masks), ALU.min, r=[r_U, r_cst], w=[r_U])
                if _cut2 <= 4:
                    continue
                act(E[:], U[:], AF.Exp, r=[r_U], w=[r_E])
                act(eg[:, 0:8], gcs[:, 0:8], AF.Exp, r=[r_gcs], w=[r_eg])
                if _cut <= 2:
                    continue
                for h in range(8):
                    tr(bbank[0][:, h * 128:(h + 1) * 128], kc[:, h, :], ident_b, r=[r_kc, r_cbf], w=[r_bb[0]])
                for h in range(8):
                    tr(bbank[1][:, h * 128:(h + 1) * 128], vc[:, h, :], ident_b, r=[r_vc, r_cbf], w=[r_bb[1]])
                tt("dve", kdec[:], bbank[0][:].rearrange("p (h x) -> p h x", x=128), bc_x(E[:, 0, :, 127]), ALU.mult,
                   r=[r_bb[0], r_E], w=[r_kdec])
                act(vtok[:].rearrange("p h x -> p (h x)"), bbank[1][:], AF.Copy, r=[r_bb[1]], w=[r_vtok])
                if _cut <= 3:
                    continue
                bG = [nb(), nb()]
                for h in range(8):
                    mm(banks[bG[h // 4]][:, (h % 4) * 128:(h % 4 + 1) * 128], kc[:, h, :], kc[:, h, :], True, True,
                       r=[r_kc], w=[r_bank[bG[h // 4]]])
                for hf in range(2):
                    hs = slice(4 * hf, 4 * hf + 4)
                    tt("dve", Ap[0][:, hs, :], b4(bG[hf]), E[:, 1, hs, :], ALU.mult, r=[r_bank[bG[hf]], r_E], w=[r_Ap[0][hf]])
                bQ = [nb(), nb()]
                for h in range(8):
                    mm(banks[bQ[h // 4]][:, (h % 4) * 128:(h % 4 + 1) * 128], kc[:, h, :], qc[:, h, :], True, True,
                       r=[r_kc, r_qc], w=[r_bank[bQ[h // 4]]])
                for hf in range(2):
                    hs = slice(4 * hf, 4 * hf + 4)
                    stt("dve", qkT[:, hs, :], b4(bQ[hf]), QSCALE, E[:, 0, hs, :], ALU.mult, ALU.mult,
                        r=[r_bank[bQ[hf]], r_E], w=[r_qkT])
                if _cut <= 4:
                    continue
                for hf in range(2):
                    hs = slice(4 * hf, 4 * hf + 4)
                    bk = nb()
                    for hh in range(4):
                        tr(banks[bk][:, hh * 128:(hh + 1) * 128], Ap[0][:, 4 * hf + hh, :], ident_f, r=[r_Ap[0][hf], r_cst], w=[r_bank[bk]])
                    act(Mp[0][:, hs, :], b4(bk), AF.Copy, r=[r_bank[bk]], w=[r_Mp[0][hf]])
                    tt("dve", X[:, hs, :], bc_h4(ident_f), Ap[0][:, hs, :], ALU.subtract, r=[r_cst, r_Ap[0][hf]], w=[r_X[hf]])
                cur = 0
                for lev in range(6):
                    last = lev == 5
                    nxt = 1 - cur
                    bMs = [nb(), nb()]
                    for hf in range(2):
                        for hh in range(4):
                            h = 4 * hf + hh
                            mm(banks[bMs[hf]][:, hh * 128:(hh + 1) * 128], Ap[cur][:, h, :], Mp[cur][:, h, :], True, True,
                               r=[r_Ap[cur][hf], r_Mp[cur][hf]], w=[r_bank[bMs[hf]]])
                    if not last:
                        bAs = [nb(), nb()]
                        for hf in range(2):
                            for hh in range(4):
                                h = 4 * hf + hh
                                mm(banks[bAs[hf]][:, hh * 128:(hh + 1) * 128], Mp[cur][:, h, :], Ap[cur][:, h, :], True, True,
                                   r=[r_Ap[cur][hf], r_Mp[cur][hf]], w=[r_bank[bAs[hf]]])
                    for hf in range(2):
                        hs = slice(4 * hf, 4 * hf + 4)
                        act(Mp[nxt][:, hs, :], b4(bMs[hf]), AF.Copy, r=[r_bank[bMs[hf]]], w=[r_Mp[nxt][hf]])
                    if not last:
                        for hf in range(2):
                            hs = slice(4 * hf, 4 * hf + 4)
                            cp("dve", Ap[nxt][:, hs, :], b4(bAs[hf]), r=[r_bank[bAs[hf]]], w=[r_Ap[nxt][hf]])
                    bXs = [nb(), nb()]
                    for hf in range(2):
                        for hh in range(4):
                            h = 4 * hf + hh
                            mm(banks[bXs[hf]][:, hh * 128:(hh + 1) * 128], Mp[nxt][:, h, :], X[:, h, :], True, True,
                               r=[r_Mp[nxt][hf], r_X[hf]], w=[r_bank[bXs[hf]]])
                    for hf in range(2):
                        hs = slice(4 * hf, 4 * hf + 4)
                        tt("dve", X[:, hs, :], X[:, hs, :], b4(bXs[hf]), ALU.add, r=[r_X[hf], r_bank[bXs[hf]]], w=[r_X[hf]])
                    cur = nxt
                cp("dve", Ybf[:], X[:], r=r_X, w=[r_Y])
                if _cut <= 5:
                    continue
                bK = [nb(), nb()]
                for h in range(8):
                    mm(banks[bK[h // 4]][:, (h % 4) * 128:(h % 4 + 1) * 128], kc[:, h, :], S_b[:, h, :], True, True,
                       r=[r_kc, r_Sb[h // 4]], w=[r_bank[bK[h // 4]]])
                for hf in range(2):
                    hs = slice(4 * hf, 4 * hf + 4)
                    tt("dve", tmpf[:, hs, :], b4(bK[hf]), bc_x(eg[:, 4 * hf:4 * hf + 4]), ALU.mult, r=[r_bank[bK[hf]], r_eg], w=[r_tmpf])
                tt("dve", rr[:], vtok[:], tmpf[:], ALU.subtract, r=[r_vtok, r_tmpf], w=[r_rr])
                bV = [nb(), nb()]
                for h in range(8):
                    mm(banks[bV[h // 4]][:, (h % 4) * 128:(h % 4 + 1) * 128], Ybf[:, h, :], rr[:, h, :], True, True,
                       r=[r_Y, r_rr], w=[r_bank[bV[h // 4]]])
                for hf in range(2):
                    hs = slice(4 * hf, 4 * hf + 4)
                    tt("dve", vnew[:, hs, :], b4(bV[hf]), bc_x(bgc[:, 4 * hf:4 * hf + 4]), ALU.mult, r=[r_bank[bV[hf]], r_bgc], w=[r_vnew])
                bO1 = [nb(), nb()]
                for h in range(8):
                    mm(banks[bO1[h // 4]][:, (h % 4) * 128:(h % 4 + 1) * 128], qc[:, h, :], S_b[:, h, :], True, True,
                       r=[r_qc, r_Sb[h // 4]], w=[r_bank[bO1[h // 4]]])
                for hf in range(2):
                    hs = slice(4 * hf, 4 * hf + 4)
                    tt("dve", tmpf[:, hs, :], b4(bO1[hf]), bc_x(eg[:, 4 * hf:4 * hf + 4]), ALU.mult, r=[r_bank[bO1[hf]], r_eg, r_rr], w=[r_tmpf])
                bO2 = [nb(), nb()]
                for h in range(8):
                    mm(banks[bO2[h // 4]][:, (h % 4) * 128:(h % 4 + 1) * 128], qkT[:, h, :], vnew[:, h, :], True, True,
                       r=[r_qkT, r_vnew], w=[r_bank[bO2[h // 4]]])
                for hf in range(2):
                    hs = slice(4 * hf, 4 * hf + 4)
                    stt("dve", o_f[:, hs, :], tmpf[:, hs, :], QSCALE, b4(bO2[hf]), ALU.mult, ALU.add,
                        r=[r_tmpf, r_bank[bO2[hf]]], w=[r_of])
                bS = [nb(), nb()]
                for h in range(8):
                    mm(banks[bS[h // 4]][:, (h % 4) * 128:(h % 4 + 1) * 128], kdec[:, h, :], vnew[:, h, :], True, True,
                       r=[r_kdec, r_vnew], w=[r_bank[bS[h // 4]]])
                for hf in range(2):
                    hs = slice(4 * hf, 4 * hf + 4)
                    tt("dve", S_f[:, hs, :], S_f[:, hs, :], bc_x(eg[:, 8 + 4 * hf:8 + 4 * hf + 4]), ALU.mult, r=[r_S[hf], r_eg], w=[r_S[hf]])
                    tt("dve", S_f[:, hs, :], S_f[:, hs, :], b4(bS[hf]), ALU.add, r=[r_S[hf], r_bank[bS[hf]]], w=[r_S[hf]])
                    act(S_b[:, hs, :], S_f[:, hs, :], AF.Copy, r=[r_S[hf]], w=[r_Sb[hf]])
                if _cut <= 6:
                    continue
                tt("dve", sqo[:], o_f[:], o_f[:], ALU.mult, r=[r_of], w=[r_sqo])
                ss, r_ss = ssr.get()
                P.op("dve", lambda e, ss=ss: e.tensor_reduce(ss[:], sqo[:], AX.X, ALU.add), r=[r_sqo], w=[r_ss])
                act(ss[:], ss[:], AF.Ln, r=[r_ss, r_eps], w=[r_ss], bias=eps_t[:], scale=1.0 / 128)
                act(ss[:], ss[:], AF.Exp, r=[r_ss], w=[r_ss], scale=-0.5)
                tt("dve", onb[:], o_f[:], bc_x(ss[:]), ALU.mult, r=[r_of, r_ss], w=[r_on])
                for h in range(8):
                    tr(bbank[0][:, h * 128:(h + 1) * 128], onb[:, h, :], ident_b, r=[r_on, r_cbf], w=[r_bb[0]])
                ogc, r_ogc = ogring.get()
                stt("dve", ogc[:], bbank[0][:].rearrange("p (h x) -> p h x", x=128), pvec[:, l, 0:1], zc[:], ALU.mult, ALU.mult,
                    r=[r_bb[0], r_hv, r_zc], w=[r_ogc])
                for h in range(8):
                    dma("sp", og_s[:, h, csl], ogc[:, h, :], r=[r_ogc], w=[r_scr["og"]])
            P.barrier()

    def mix_attn(l):
        with ExitStack() as fs:
            def fsb(name, shape, dt):
                return fs.enter_context(nc.sbuf_tensor(un(name), list(shape), dt))
            knr = Ring(fs, "kn", [128, S], BF16, 2)
            krr = Ring(fs, "kr", [64, S], BF16, 2)
            qnr = Ring(fs, "qn", [128, S], BF16, 2)
            qrr = Ring(fs, "qr", [64, S], BF16, 2)
            vhr = Ring(fs, "vh", [128, NCH, 128], BF16, 2)
            ptr = Ring(fs, "pT", [128, TT], BF16, 3)
            rdr = Ring(fs, "rden", [128, TT], F32, 2)
            otr = Ring(fs, "oT", [128, TT], BF16, 2)
            amask = cbf[:, 192:192 + 2048]
            qcount = 0
            for h in range(8):
                kn, r_kn = knr.get()
                kr, r_kr = krr.get()
                qn_, r_qn = qnr.get()
                qr, r_qr = qrr.get()
                vh, r_vh = vhr.get()
                dma("sp", kn[:], km_s[h, 0:128, :], r=[r_scr["km"]], w=[r_kn])
                dma("sp", kr[:], km_s[h, 128:192, :], r=[r_scr["km"]], w=[r_kr])
                dma("sp", qn_[:], qm_s[h, 0:128, :], r=[r_scr["qm"]], w=[r_qn])
                dma("sp", qr[:], qm_s[h, 128:192, :], r=[r_scr["qm"]], w=[r_qr])
                for c0 in range(0, NCH, 4):
                    c1 = min(NCH, c0 + 4)
                    dma("sp", vh[:, c0:c1, :], vm_s[c0 * 128:c1 * 128, h * 128:(h + 1) * 128].rearrange("(c p) d -> p c d", p=128),
                        r=[r_scr["vm"]], w=[r_vh])
                for qi in range(NT):
                    qsl = slice(qi * TT, (qi + 1) * TT)
                    nk = 4 * (qi + 1)
                    bO = qcount % 2
                    bD = 2 + qcount % 2
                    qcount += 1
                    def s_mm(kt):
                        bS_ = 4 + (kt % 2)
                        ksl = slice(kt * 128, (kt + 1) * 128)
                        dg = kt >= 4 * qi
                        mm(banks[bS_][:], kn[:, ksl], qn_[:, qsl], True, False, r=[r_kn, r_qn], w=[r_bank[bS_]])
                        mm(banks[bS_][:], kr[:, ksl], qr[:, qsl], False, not dg, r=[r_kr, r_qr], w=[r_bank[bS_]])
                        if dg:
                            dd = kt - 4 * qi
                            mm(banks[bS_][:], ident_b, amask[:, dd * 512:(dd + 1) * 512], False, True, r=[r_cbf], w=[r_bank[bS_]])
                    s_mm(0)
                    for kt in range(nk):
                        bS_ = 4 + (kt % 2)
                        pT, r_pT = ptr.get()
                        act(pT[:], banks[bS_][:], AF.Exp, r=[r_bank[bS_]], w=[r_pT], scale=ASCALE)
                        if kt + 1 < nk:
                            s_mm(kt + 1)
                        mm(banks[bO][:], vh[:, kt, :], pT[:], kt == 0, kt == nk - 1, r=[r_vh, r_pT], w=[r_bank[bO]])
                        mm(banks[bD][:], ones_bf[:], pT[:], kt == 0, kt == nk - 1, r=[r_ones, r_pT], w=[r_bank[bD]])
                    rden, r_rden = rdr.get()
                    act(rden[:], banks[bD][:], AF.Copy, r=[r_bank[bD]], w=[r_rden])
                    P.op("dve", lambda e, rden=rden: e.reciprocal(rden[:], rden[:]), r=[r_rden], w=[r_rden])
                    oT, r_oT = otr.get()
                    tt("dve", oT[:], banks[bO][:], rden[:], ALU.mult, r=[r_bank[bO], r_rden], w=[r_oT])
                    dma("sp", om_s[:, h, qsl], oT[:], r=[r_oT], w=[r_scr["om"]])
            P.barrier()
            bstate["i"] = 0

    def mix_out(l):
        with ExitStack() as fs:
            def fsb(name, shape, dt):
                return fs.enter_context(nc.sbuf_tensor(un(name), list(shape), dt))
            wa = fsb("wa", [128, KD, D], BF16)
            wb = fsb("wb", [128, KD, D], BF16)
            wo = fsb("wo", [128, KD, D], BF16)
            r_wa, r_wb, r_wo = Res(), Res(), Res()
            for k in range(KD):
                dma("pool", wa[:, k, :], wa_d[l, k * 128:(k + 1) * 128, :], r=[], w=[r_wa])
                dma("pool", wb[:, k, :], wb_d[l, k * 128:(k + 1) * 128, :], r=[], w=[r_wb])
                dma("pool", wo[:, k, :], wo_d[l, k * 128:(k + 1) * 128, :], r=[], w=[r_wo])
            xt = fsb("xt", [128, KD, TT], F32)
            r_xt = [Res() for _ in range(KD)]
            ogr = Ring(fs, "ogt", [128, KD, TT], BF16, 2)
            omr = Ring(fs, "omt", [128, KD, TT], BF16, 2)
            gar = Ring(fs, "gat", [128, KD, TT], BF16, 2)
            gbr = Ring(fs, "gbt", [128, KD, TT], BF16, 2)
            yb = fsb("yb", [128, KD, TT], BF16)
            r_yb = [Res() for _ in range(KD)]
            t1r = Ring(fs, "t1", [128, TT], F32, 2)
            t2r = Ring(fs, "t2", [128, TT], F32, 2)

            def kview(dr, tsl):
                return dr[:, :, tsl]
            for t in range(NT):
                tsl = slice(t * TT, (t + 1) * TT)
                load_xt(t, xt, r_xt)
                ogt, r_ogt = ogr.get()
                omt, r_omt = omr.get()
                gat, r_gat = gar.get()
                gbt, r_gbt = gbr.get()
                dma("sp", ogt[:], kview(og_s, tsl), r=[r_scr["og"]], w=[r_ogt])
                dma("sp", omt[:], kview(om_s, tsl), r=[r_scr["om"]], w=[r_omt])
                dma("sp", gat[:], kview(ga_s, tsl), r=[r_scr["ga"]], w=[r_gat])
                dma("sp", gbt[:], kview(gb_s, tsl), r=[r_scr["gb"]], w=[r_gbt])
                for d in range(KD):
                    ds_ = slice(d * 128, (d + 1) * 128)
                    bA = nb()
                    for k in range(KD):
                        mm(banks[bA][:], wa[:, k, ds_], ogt[:, k, :], k == 0, k == KD - 1, r=[r_wa, r_ogt], w=[r_bank[bA]])
                    bB = nb()
                    for k in range(KD):
                        mm(banks[bB][:], wb[:, k, ds_], omt[:, k, :], k == 0, k == KD - 1, r=[r_wb, r_omt], w=[r_bank[bB]])
                    t1, r_t1 = t1r.get()
                    t2, r_t2 = t2r.get()
                    tt("dve", t1[:], banks[bA][:], gat[:, d, :], ALU.mult, r=[r_bank[bA], r_gat], w=[r_t1])
                    tt("dve", t2[:], banks[bB][:], gbt[:, d, :], ALU.mult, r=[r_bank[bB], r_gbt], w=[r_t2])
                    tt("pool", yb[:, d, :], t1[:], t2[:], ALU.add, r=[r_t1, r_t2], w=[r_yb[d]])
                for d in range(KD):
                    ds_ = slice(d * 128, (d + 1) * 128)
                    bk = nb()
                    for k in range(KD):
                        mm(banks[bk][:], wo[:, k, ds_], yb[:, k, :], k == 0, k == KD - 1, r=[r_wo, r_yb[k]], w=[r_bank[bk]])
                    stt("dve", xt[:, d, :], banks[bk][:], vG(l, 1, d), xt[:, d, :], ALU.mult, ALU.add,
                        r=[r_bank[bk], r_vec, r_xt[d]], w=[r_xt[d]])
                    dma("sp", out_d[ds_, tsl], xt[:, d, :], r=[r_xt[d]], w=[r_res[t]], final=True)
            P.barrier()
        state["first"] = False

    stages = ("ffn1", "proj", "gdn", "attn", "mixout", "full")
    si = stages.index(stage)
    for l in range(L):
        ffn(l, 0, 0)
        if si >= 1:
            mix_proj(l, 0)
            mix_proj(l, 1)
        if si >= 2:
            mix_gdn(l)
        if si >= 3:
            mix_attn(l)
        if si >= 4:
            mix_out(l)
        if si >= 5:
            ffn(l, 2, 1)

    nwait = P.emit()
    es.close()
    return nc, dict(n_instr=len(P.instrs), nwait=nwait)


def prep_inputs(inputs, S):
    f = np.float32
    sh = {}
    sh["consts"] = make_consts()
    sh["ada_w"] = np.ascontiguousarray(inputs["ada_w"], dtype=f)
    sh["ada_bT"] = np.ascontiguousarray(np.asarray(inputs["ada_b"]).reshape(4, 72, 128).transpose(0, 2, 1), dtype=f)
    g = np.stack([np.asarray(inputs["norm_ffn1"]), np.asarray(inputs["norm_mix"]), np.asarray(inputs["norm_ffn2"])], axis=1)
    sh["gainsT"] = np.ascontiguousarray(g.reshape(4, 3, KD, 128).transpose(0, 1, 3, 2), dtype=f)
    for n in ("ffn1_w1", "ffn1_w3", "ffn1_w2", "ffn2_w1", "ffn2_w3", "ffn2_w2", "w_in", "mla_w_q_up", "mla_w_kv_up",
              "w_branch_a", "w_branch_b", "w_out"):
        sh[n] = np.ascontiguousarray(inputs[n], dtype=f)
    conv = np.asarray(inputs["gdn_conv"], dtype=f)
    sh["convT"] = np.ascontiguousarray(conv.reshape(4, 4, 24, 128).transpose(0, 3, 2, 1))
    hv = np.zeros((4, 128, 16), f)
    hv[:, :, 0:8] = np.asarray(inputs["gdn_a_log"], dtype=f)[:, None, :]
    hv[:, :, 8:16] = np.asarray(inputs["gdn_dt_bias"], dtype=f)[:, None, :]
    sh["headvec"] = hv
    pv = np.zeros((4, 128, 16), f)
    pv[:, :, 0] = np.asarray(inputs["gdn_out_gain"], dtype=f)
    pv[:, :, 1:4] = np.asarray(inputs["mla_q_lat_gain"], dtype=f).reshape(4, 3, 128).transpose(0, 2, 1)
    pv[:, :, 4:6] = np.asarray(inputs["mla_kv_lat_gain"], dtype=f).reshape(4, 2, 128).transpose(0, 2, 1)
    qn = np.asarray(inputs["mla_q_norm"], dtype=f)
    kn = np.asarray(inputs["mla_k_norm"], dtype=f)
    pv[:, :, 6] = qn[:, 0:128]
    pv[:, 0:64, 7] = qn[:, 128:192]
    pv[:, :, 8] = kn[:, 0:128]
    pv[:, 0:64, 9] = kn[:, 128:192]
    sh["partvec"] = pv
    maps = []
    B = inputs["x"].shape[0]
    for b in range(B):
        m = dict(sh)
        m["xT"] = np.ascontiguousarray(np.asarray(inputs["x"])[b, :S].T, dtype=f)
        m["cT"] = np.ascontiguousarray(np.asarray(inputs["c"])[b].reshape(KD, 128).T, dtype=f)
        m["pos"] = np.ascontiguousarray(np.asarray(inputs["positions"])[b, :S].reshape(1, S), dtype=np.int32)
        maps.append(m)
    return maps


def kernel(**inputs):
    S = inputs["x"].shape[1]
    B = inputs["x"].shape[0]
    nc, info = build_program(S, 4)
    maps = prep_inputs(inputs, S)
    res = run_bass_kernel_spmd(nc, maps, core_ids=list(range(B)))
    out = np.stack([np.ascontiguousarray(r["outT"].T) for r in res.results], axis=0)
    return out.astype(np.float32)
```

```python
import numpy as np
from contextlib import ExitStack
import concourse.bass as bass
import concourse.mybir as mybir
from concourse.bass_utils import run_bass_kernel_spmd

F32 = mybir.dt.float32
BF16 = mybir.dt.bfloat16
I32 = mybir.dt.int32
AF = mybir.ActivationFunctionType
ALU = mybir.AluOpType
AX = mybir.AxisListType

D = 1024
KD = 8
DFF = 2816
KF = 22
NIN = 6864
EPS = 1e-6
TT = 512


class Res:
    __slots__ = ("w", "r", "name", "excl")

    def __init__(self, name="", excl=False):
        self.w = None
        self.r = {}
        self.name = name
        self.excl = excl


class Instr:
    __slots__ = ("eng", "fn", "deps", "idx", "signal", "sem", "semval", "dma", "gidx")

    def __init__(self, eng, fn, dma):
        self.eng = eng
        self.fn = fn
        self.dma = dma
        self.deps = set()
        self.signal = False
        self.sem = None
        self.semval = 0


NDMA_SEM = 12


class Prog:
    ENGS = ("pe", "act", "dve", "pool", "sp")

    def __init__(self, nc):
        self.nc = nc
        self.instrs = []
        self.cnt = {e: 0 for e in self.ENGS}
        self.last = {e: None for e in self.ENGS}
        self.dma_hist = {e: [] for e in self.ENGS}
        self.out_dmas = []

    def op(self, eng, fn, r=(), w=(), dma=False):
        ins = Instr(eng, fn, dma)
        ins.idx = self.cnt[eng]
        self.cnt[eng] += 1
        ins.gidx = len(self.instrs)
        if any(x.excl for x in r):
            w = list(w) + [x for x in r if x.excl]
            r = [x for x in r if not x.excl]
        deps = ins.deps
        for x in r:
            if x.w is not None:
                deps.add(x.w)
        for x in w:
            if x.w is not None:
                deps.add(x.w)
            for v in x.r.values():
                if isinstance(v, list):
                    deps.update(v)
                else:
                    deps.add(v)
        for x in r:
            if dma:
                x.r.setdefault("dma", []).append(ins)
            else:
                x.r[eng] = ins
        for x in w:
            x.w = ins
            x.r = {}
        deps.discard(ins)
        if dma:
            h = self.dma_hist[eng]
            if len(h) >= NDMA_SEM:
                deps.add(h[-NDMA_SEM])
            h.append(ins)
        self.instrs.append(ins)
        self.last[eng] = ins
        return ins

    def barrier(self):
        lasts = [v for v in self.last.values() if v is not None]
        alld = []
        for e in self.ENGS:
            alld.extend(self.dma_hist[e][-NDMA_SEM:])
        for e in self.ENGS:
            eobj = e
            ins = self.op(e, (lambda en: en.nop()), dma=False)
            for d in lasts + alld:
                if d is not ins:
                    ins.deps.add(d)

    def emit(self):
        nc = self.nc
        eobj = {"pe": nc.tensor, "act": nc.scalar, "dve": nc.vector, "pool": nc.gpsimd, "sp": nc.sync}
        esem = {e: nc.alloc_semaphore("cs_" + e) for e in self.ENGS}
        dsem = {e: [nc.alloc_semaphore("ds_%s_%d" % (e, i)) for i in range(NDMA_SEM)] for e in ("sp", "pool", "act")}
        for ins in self.instrs:
            keep = set()
            for d in ins.deps:
                if (not d.dma) and d.eng == ins.eng:
                    if ins.eng == "pe":
                        continue
                    if ins.dma:
                        continue
                    if ins.idx - d.idx > 3:
                        continue
                keep.add(d)
            if ins.dma:
                for d in ins.deps:
                    if (not d.dma) and d.eng == ins.eng:
                        keep.add(d)
            ins.deps = keep
            for d in keep:
                d.signal = True
        cnt = {e: 0 for e in self.ENGS}
        dcnt = {e: 0 for e in self.ENGS}
        for ins in self.instrs:
            if ins.dma:
                k = dcnt[ins.eng]
                dcnt[ins.eng] += 1
                ins.sem = dsem[ins.eng][k % NDMA_SEM]
                ins.semval = 16 * (k // NDMA_SEM + 1)
            elif ins.signal:
                cnt[ins.eng] += 1
                ins.sem = esem[ins.eng]
                ins.semval = cnt[ins.eng]
        waited = {e: {} for e in self.ENGS}
        nwait = 0
        for ins in self.instrs:
            e = eobj[ins.eng]
            wd = waited[ins.eng]
            waits = {}
            for d in ins.deps:
                key = id(d.sem)
                if wd.get(key, 0) >= d.semval:
                    continue
                cur = waits.get(key)
                if cur is None or cur[1] < d.semval:
                    waits[key] = (d.sem, d.semval)
            wl = list(waits.values())
            for (s, v) in wl[:-1]:
                e.wait_ge(s, v)
                nwait += 1
            bi = ins.fn(e)
            if wl:
                bi._wait_ge(wl[-1][0], wl[-1][1])
            for key, (s, v) in waits.items():
                wd[key] = v
            if ins.dma:
                bi.then_inc(ins.sem, 16)
            elif ins.signal:
                bi.then_inc(ins.sem, 1)
        fin = {}
        for d in self.out_dmas:
            key = id(d.sem)
            if key not in fin or fin[key][1] < d.semval:
                fin[key] = (d.sem, d.semval)
        for (s, v) in fin.values():
            nc.sync.wait_ge(s, v)
        return nwait


NCONST = 640 + 4 * 512
BIG = 30000.0
QSCALE = 128.0 ** -0.5
ASCALE = 192.0 ** -0.5


def make_consts():
    c = np.zeros((128, NCONST), np.float32)
    c[:, 0:128] = np.eye(128, dtype=np.float32)
    k = np.arange(128)[:, None]
    i = np.arange(128)[None, :]
    c[:, 128:256] = (k <= i).astype(np.float32)
    c[:, 256:384] = np.where(i >= k, 0.0, -BIG)
    c[:, 384:512] = np.where(i > k, 0.0, -BIG)
    rot = np.zeros((128, 64), np.float32)
    for m in range(32):
        rot[m + 32, m] = -1.0
        rot[m, m + 32] = 1.0
    c[:, 512:576] = rot
    half = 32
    invf = (10000.0 ** (-(np.arange(half, dtype=np.float32)) / half)).astype(np.float32)
    c[0:64, 576] = np.concatenate([invf, invf])
    q = np.arange(512)[None, :]
    for d in range(4):
        c[:, 640 + d * 512:640 + (d + 1) * 512] = np.where(d * 128 + k <= q, 0.0, -BIG)
    return c


def build_program(S, L, stage="full", dbg=()):
    NT = S // TT
    NCH = S // 128
    nc = bass.Bass("TRN2", target_bir_lowering=False)
    P = Prog(nc)
    es = ExitStack()

    def dram_in(name, shape, dt=F32):
        return nc.dram_tensor(name, list(shape), dt, kind="ExternalInput").ap()

    def scratch(name, shape, dt):
        kind = "ExternalOutput" if name in dbg else "Internal"
        return nc.dram_tensor(name, list(shape), dt, kind=kind).ap()

    xT_d = dram_in("xT", [D, S])
    cT_d = dram_in("cT", [128, KD])
    pos_d = dram_in("pos", [1, S], I32)
    consts_d = dram_in("consts", [128, NCONST])
    adaw_d = dram_in("ada_w", [4, D, 9 * D])
    adab_d = dram_in("ada_bT", [4, 128, 72])
    gains_d = dram_in("gainsT", [4, 3, 128, KD])
    w1_d = [dram_in("ffn1_w1", [4, D, DFF]), dram_in("ffn2_w1", [4, D, DFF])]
    w3_d = [dram_in("ffn1_w3", [4, D, DFF]), dram_in("ffn2_w3", [4, D, DFF])]
    w2_d = [dram_in("ffn1_w2", [4, DFF, D]), dram_in("ffn2_w2", [4, DFF, D])]
    win_d = dram_in("w_in", [4, D, NIN])
    convT_d = dram_in("convT", [4, 128, 24, 4])
    hv_d = dram_in("headvec", [4, 128, 16])
    pv_d = dram_in("partvec", [4, 128, 16])
    wq_d = dram_in("mla_w_q_up", [4, 384, 1536])
    wkv_d = dram_in("mla_w_kv_up", [4, 256, 2048])
    wa_d = dram_in("w_branch_a", [4, D, D])
    wb_d = dram_in("w_branch_b", [4, D, D])
    wo_d = dram_in("w_out", [4, D, D])
    out_d = nc.dram_tensor("outT", [D, S], F32, kind="ExternalOutput").ap()

    qT_s = scratch("qT_s", [128, KD, S], BF16)
    kT_s = scratch("kT_s", [128, KD, S], BF16)
    vT_s = scratch("vT_s", [128, KD, S], BF16)
    zs_s = scratch("zs_s", [128, KD, S], BF16)
    ga_s = scratch("ga_s", [128, KD, S], BF16)
    gb_s = scratch("gb_s", [128, KD, S], BF16)
    bg_s = scratch("bg_s", [S, 24], F32)
    qm_s = scratch("qm_s", [8, 192, S], BF16)
    km_s = scratch("km_s", [8, 192, S], BF16)
    vm_s = scratch("vm_s", [S, D], BF16)
    og_s = scratch("og_s", [128, KD, S], BF16)
    om_s = scratch("om_s", [128, KD, S], BF16)
    import os as _os0
    if _os0.environ.get('GDN_DBG'):
        dbgU = nc.dram_tensor("dbgU", [NCH, 128, 2048], F32, kind="ExternalOutput").ap()
        dbgG = nc.dram_tensor("dbgG", [NCH, 128, 16], F32, kind="ExternalOutput").ap()
    r_scr = {n: Res(n) for n in ("q", "k", "v", "z", "ga", "gb", "bg", "qm", "km", "vm", "og", "om")}

    uid = [0]

    def un(name):
        uid[0] += 1
        return "%s_u%d" % (name, uid[0])

    def sb(name, shape, dt):
        return es.enter_context(nc.sbuf_tensor(name, list(shape), dt))

    def mm(out, lhsT, rhs, start, stop, r, w):
        return P.op("pe", lambda e: e.matmul(out, lhsT, rhs, start=start, stop=stop), r=r, w=w)

    def tr(out, in_, ident, r, w):
        return P.op("pe", lambda e: e.transpose(out, in_, ident), r=r, w=w)

    def act(out, in_, func, r, w, bias=None, scale=1.0, accum=None):
        def f(e):
            kw = {}
            if bias is not None:
                kw["bias"] = bias
            if accum is not None:
                kw["accum_out"] = accum
            return e.activation(out=out, in_=in_, func=func, scale=scale, **kw)
        return P.op("act", f, r=r, w=w)

    def tt(eng, out, in0, in1, op, r, w):
        return P.op(eng, lambda e: e.tensor_tensor(out, in0, in1, op), r=r, w=w)

    def stt(eng, out, in0, scalar, in1, op0, op1, r, w):
        return P.op(eng, lambda e: e.scalar_tensor_tensor(out=out, in0=in0, scalar=scalar, in1=in1, op0=op0, op1=op1), r=r, w=w)

    def tsc(eng, out, in0, s1, s2, op0, op1, r, w):
        if s2 is None:
            return P.op(eng, lambda e: e.tensor_scalar(out, in0, s1, None, op0), r=r, w=w)
        return P.op(eng, lambda e: e.tensor_scalar(out, in0, s1, s2, op0, op1), r=r, w=w)

    def cp(eng, out, in_, r, w):
        return P.op(eng, lambda e: e.tensor_copy(out, in_), r=r, w=w)

    def dma(eng, out, in_, r, w, final=False):
        ins = P.op(eng, lambda e: e.dma_start(out=out, in_=in_), r=r, w=w, dma=True)
        if final:
            P.out_dmas.append(ins)
        return ins

    class Ring:
        def __init__(self, ctx, name, shape, dt, n):
            self.t = [ctx.enter_context(nc.sbuf_tensor(un("%s_%d" % (name, i)), list(shape), dt)) for i in range(n)]
            self.r = [Res("%s_%d" % (name, i)) for i in range(n)]
            self.i = 0
            self.n = n

        def get(self):
            k = self.i % self.n
            self.i += 1
            return self.t[k], self.r[k]

    cst = sb("cst", [128, 640], F32)
    r_cst = Res("cst")
    dma("sp", cst[:], consts_d[:, 0:640], r=[], w=[r_cst])
    ident_f = cst[:, 0:128]
    tri_f = cst[:, 128:256]
    maski = cst[:, 256:384]
    masks = cst[:, 384:512]
    cbf = sb("cbf", [128, 192 + 2048], BF16)
    r_cbf = Res("cbf")
    dma("pool", cbf[:, 0:128], consts_d[:, 0:128], r=[], w=[r_cbf])
    dma("pool", cbf[:, 128:192], consts_d[:, 512:576], r=[], w=[r_cbf])
    dma("pool", cbf[:, 192:192 + 2048], consts_d[:, 640:640 + 2048], r=[], w=[r_cbf])
    ident_b = cbf[:, 0:128]
    rot_b = cbf[0:64, 128:192]
    ones_bf = sb("ones_bf", [128, 128], BF16)
    ones_f = sb("ones_f", [128, 128], F32)
    r_ones = Res("ones")
    P.op("dve", lambda e: e.memset(ones_bf[:], 1.0), w=[r_ones])
    P.op("dve", lambda e: e.memset(ones_f[:], 1.0), w=[r_ones])
    eps_t = sb("eps_t", [128, 1], F32)
    r_eps = Res("eps")
    P.op("dve", lambda e: e.memset(eps_t[:], EPS), w=[r_eps])

    NBANK = 6
    banks = [es.enter_context(nc.psum_tensor("bank%d" % i, [128, 512], F32)) for i in range(NBANK)]
    r_bank = [Res("bank%d" % i, excl=True) for i in range(NBANK)]
    bbank = [es.enter_context(nc.psum_tensor("bbank%d" % i, [128, 1024], BF16)) for i in range(2)]
    r_bb = [Res("bb0", excl=True), Res("bb1", excl=True)]
    bstate = {"i": 0}

    def nb():
        k = bstate["i"] % NBANK
        bstate["i"] += 1
        return k

    cs_s = scratch("cs_s", [2, 64, S], F32)
    r_rope = Res("rope")
    r_cs = Res("cs")
    with ExitStack() as es2:
        cos2 = es2.enter_context(nc.sbuf_tensor("cos2", [64, S], F32))
        sin2 = es2.enter_context(nc.sbuf_tensor("sin2", [64, S], F32))
        posi = es2.enter_context(nc.sbuf_tensor("posi", [64, S], I32))
        ang = es2.enter_context(nc.sbuf_tensor("ang", [64, S], F32))
        nn = es2.enter_context(nc.sbuf_tensor("nn", [64, S], F32))
        r_p = Res("posi")
        r_a = Res("ang")
        r_n = Res("nn")
        dma("sp", posi[:], pos_d[0:1, :].partition_broadcast(64), r=[], w=[r_p])
        cp("dve", ang[:], posi[:], r=[r_p], w=[r_a])
        tsc("dve", ang[:], ang[:], cst[0:64, 576:577], None, ALU.mult, ALU.bypass, r=[r_a, r_cst], w=[r_a])
        TWO_PI = 2.0 * np.pi
        C1 = 6.28125
        C2 = float(TWO_PI - C1)
        MAGIC = 12582912.0
        tsc("dve", nn[:], ang[:], float(1.0 / TWO_PI), MAGIC, ALU.mult, ALU.add, r=[r_a], w=[r_n])
        tsc("dve", nn[:], nn[:], -MAGIC, None, ALU.add, ALU.bypass, r=[r_n], w=[r_n])
        stt("dve", ang[:], nn[:], -C1, ang[:], ALU.mult, ALU.add, r=[r_n, r_a], w=[r_a])
        stt("dve", ang[:], nn[:], -C2, ang[:], ALU.mult, ALU.add, r=[r_n, r_a], w=[r_a])
        tsc("dve", ang[:], ang[:], float(np.pi), float(-np.pi), ALU.min, ALU.max, r=[r_a], w=[r_a])
        act(sin2[:], ang[:], AF.Sin, r=[r_a], w=[r_rope])
        tsc("dve", nn[:], ang[:], float(np.pi / 2), None, ALU.add, ALU.bypass, r=[r_a], w=[r_n])
        tsc("dve", ang[:], nn[:], float(np.pi), float(-TWO_PI), ALU.is_gt, ALU.mult, r=[r_n], w=[r_a])
        tt("dve", nn[:], nn[:], ang[:], ALU.add, r=[r_n, r_a], w=[r_n])
        tsc("dve", nn[:], nn[:], float(np.pi), float(-np.pi), ALU.min, ALU.max, r=[r_n], w=[r_n])
        act(cos2[:], nn[:], AF.Sin, r=[r_n], w=[r_rope])
        dma("sp", cs_s[0, :, :], cos2[:], r=[r_rope], w=[r_cs])
        dma("sp", cs_s[1, :, :], sin2[:], r=[r_rope], w=[r_cs])
        P.barrier()

    condT = sb("condT", [128, KD], F32)
    r_cond = Res("cond")
    dma("sp", condT[:], cT_d[:, :], r=[], w=[r_cond])
    act(condT[:], condT[:], AF.Silu, r=[r_cond], w=[r_cond])
    modT = sb("modT", [128, 4, 72], F32)
    r_mod = Res("mod")
    adab = sb("adab", [128, 4, 72], F32)
    r_adab = Res("adab")
    for l in range(4):
        dma("sp", adab[:, l, :], adab_d[l, :, :], r=[], w=[r_adab])
    gains = sb("gains", [128, 4, 3, KD], F32)
    r_gains = Res("gains")
    for l in range(4):
        for j in range(3):
            dma("sp", gains[:, l, j, :], gains_d[l, j, :, :], r=[], w=[r_gains])
    hvec = sb("hvec", [128, 4, 16], F32)
    pvec = sb("pvec", [128, 4, 16], F32)
    r_hv = Res("hv")
    for l in range(4):
        dma("sp", hvec[:, l, :], hv_d[l, :, :], r=[], w=[r_hv])
        dma("sp", pvec[:, l, :], pv_d[l, :, :], r=[], w=[r_hv])
    for l in range(4):
        act(hvec[:, l, 0:8], hvec[:, l, 0:8], AF.Exp, r=[r_hv], w=[r_hv])
        tsc("dve", hvec[:, l, 0:8], hvec[:, l, 0:8], -1.0, None, ALU.mult, ALU.bypass, r=[r_hv], w=[r_hv])
    NG = 8
    GW = 9 * D // NG
    with ExitStack() as es2:
        awb = [es2.enter_context(nc.sbuf_tensor(un("awb%d" % i), [128, KD, GW], F32)) for i in range(2)]
        r_awb = [Res("awb0"), Res("awb1")]
        gi = 0
        for l in range(L):
            for g in range(NG):
                bsel = gi % 2
                for k in range(KD):
                    dma("sp", awb[bsel][:, k, :], adaw_d[l, k * 128:(k + 1) * 128, g * GW:(g + 1) * GW], r=[], w=[r_awb[bsel]])
                bk = nb()
                nj = GW // 128
                for jj in range(nj):
                    for k in range(KD):
                        mm(banks[bk][:, jj:jj + 1], awb[bsel][:, k, jj * 128:(jj + 1) * 128], condT[:, k:k + 1],
                           k == 0, k == KD - 1, r=[r_awb[bsel], r_cond], w=[r_bank[bk]])
                tt("dve", modT[:, l, g * nj:(g + 1) * nj], banks[bk][:, 0:nj], adab[:, l, g * nj:(g + 1) * nj], ALU.add,
                   r=[r_bank[bk], r_adab], w=[r_mod])
                gi += 1
        P.barrier()
    vecA = sb("vecA", [128, 4, 3, KD], F32)
    vecG = sb("vecG", [128, 4, 3, KD], F32)
    r_vec = Res("vec")
    for l in range(L):
        for j in range(3):
            sc = modT[:, l, (3 * j + 1) * KD:(3 * j + 2) * KD]
            gg = modT[:, l, (3 * j + 2) * KD:(3 * j + 3) * KD]
            stt("dve", vecA[:, l, j, :], sc, 1.0, gains[:, l, j, :], ALU.add, ALU.mult, r=[r_mod, r_gains], w=[r_vec])
            gs = 1.0 if j == 1 else 0.5
            tsc("dve", vecG[:, l, j, :], gg, 1.0, gs, ALU.add, ALU.mult, r=[r_mod], w=[r_vec])

    def vA(l, j, k):
        return vecA[:, l, j, k:k + 1]

    def vB(l, j, k):
        return modT[:, l, 3 * j * KD + k:3 * j * KD + k + 1]

    def vG(l, j, k):
        return vecG[:, l, j, k:k + 1]

    P.barrier()

    r_res = [Res("res%d" % t) for t in range(NT)]
    state = {"first": True}

    def norm_tile(l, j, xt, r_xt, hb, r_hb, sq, r_sq, rstd, r_rstd, tmpring):
        act(sq[:], xt[:], AF.Square, r=r_xt, w=[r_sq])
        bk = nb()
        for k in range(KD):
            mm(banks[bk][:], ones_bf[:], sq[:, k, :], k == 0, k == KD - 1, r=[r_sq, r_ones], w=[r_bank[bk]])
        act(rstd[:], banks[bk][:], AF.Ln, r=[r_bank[bk], r_eps], w=[r_rstd], bias=eps_t[:], scale=1.0 / D)
        act(rstd[:], rstd[:], AF.Exp, r=[r_rstd], w=[r_rstd], scale=-0.5)
        for k in range(KD):
            tm, r_tm = tmpring.get()
            stt("dve", tm[:], xt[:, k, :], vA(l, j, k), rstd[:], ALU.mult, ALU.mult, r=[r_xt[k], r_rstd, r_vec], w=[r_tm])
            act(hb[:, k, :], tm[:], AF.Identity, r=[r_tm, r_mod], w=[r_hb[k]], bias=vB(l, j, k))

    def load_xt(t, xt, r_xt, eng="sp"):
        src = xT_d if state["first"] else out_d
        ts_ = slice(t * TT, (t + 1) * TT)
        for k in range(KD):
            dma(eng, xt[:, k, :], src[k * 128:(k + 1) * 128, ts_], r=[r_res[t]], w=[r_xt[k]])

    def ffn(l, j, which):
        with ExitStack() as fs:
            def fsb(name, shape, dt):
                return fs.enter_context(nc.sbuf_tensor(un(name), list(shape), dt))
            w1 = fsb("w1", [128, KD, DFF], BF16)
            w3 = fsb("w3", [128, KD, DFF], BF16)
            w2 = fsb("w2", [128, KF, D], BF16)
            xt = fsb("xt", [128, KD, TT], F32)
            hb = fsb("hb", [128, KD, TT], BF16)
            ub = fsb("ub", [128, KF, TT], BF16)
            sq = fsb("sq", [128, KD, TT], BF16)
            rstd = fsb("rstd", [128, TT], F32)
            tmpring = Ring(fs, "tmp", [128, TT], F32, 2)
            r_w1 = [Res() for _ in range(KD)]
            r_w3 = [Res() for _ in range(KD)]
            r_w2 = [Res() for _ in range(KF)]
            r_xt = [Res() for _ in range(KD)]
            r_hb = [Res() for _ in range(KD)]
            r_ub = [Res() for _ in range(KF)]
            r_sq = Res()
            r_rstd = Res()
            for k in range(KD):
                dma("pool", w1[:, k, :], w1_d[which][l, k * 128:(k + 1) * 128, :], r=[], w=[r_w1[k]])
                dma("pool", w3[:, k, :], w3_d[which][l, k * 128:(k + 1) * 128, :], r=[], w=[r_w3[k]])
            for f in range(KF):
                dma("pool", w2[:, f, :], w2_d[which][l, f * 128:(f + 1) * 128, :], r=[], w=[r_w2[f]])
            for t in range(NT):
                ts_ = slice(t * TT, (t + 1) * TT)
                load_xt(t, xt, r_xt)
                norm_tile(l, j, xt, r_xt, hb, r_hb, sq, r_sq, rstd, r_rstd, tmpring)
                for f in range(KF):
                    b1 = nb()
                    b3 = nb()
                    fs_ = slice(f * 128, (f + 1) * 128)
                    for k in range(KD):
                        mm(banks[b1][:], w1[:, k, fs_], hb[:, k, :], k == 0, k == KD - 1, r=[r_w1[k], r_hb[k]], w=[r_bank[b1]])
                    for k in range(KD):
                        mm(banks[b3][:], w3[:, k, fs_], hb[:, k, :], k == 0, k == KD - 1, r=[r_w3[k], r_hb[k]], w=[r_bank[b3]])
                    tm, r_tm = tmpring.get()
                    act(tm[:], banks[b1][:], AF.Silu, r=[r_bank[b1]], w=[r_tm])
                    tt("dve", ub[:, f, :], tm[:], banks[b3][:], ALU.mult, r=[r_tm, r_bank[b3]], w=[r_ub[f]])
                for d in range(KD):
                    bk = nb()
                    ds_ = slice(d * 128, (d + 1) * 128)
                    for f in range(KF):
                        mm(banks[bk][:], w2[:, f, ds_], ub[:, f, :], f == 0, f == KF - 1, r=[r_w2[f], r_ub[f]], w=[r_bank[bk]])
                    stt("dve", xt[:, d, :], banks[bk][:], vG(l, j, d), xt[:, d, :], ALU.mult, ALU.add,
                        r=[r_bank[bk], r_vec, r_xt[d]], w=[r_xt[d]])
                    dma("sp", out_d[ds_, ts_], xt[:, d, :], r=[r_xt[d]], w=[r_res[t]], final=True)
            P.barrier()
        state["first"] = False

    def mix_proj(l, part):
        with ExitStack() as fs:
            def fsb(name, shape, dt):
                return fs.enter_context(nc.sbuf_tensor(un(name), list(shape), dt))
            cbase = 0 if part == 0 else 4112
            NC_ = 4112 if part == 0 else NIN - 4112
            win = fsb("win", [128, KD, NC_], BF16)
            cstr = Ring(fs, "cst_t", [64, 2, TT], F32, 2)
            wq = fsb("wq", [128, 3, 1536 if part == 1 else 2], BF16)
            wkv = fsb("wkv", [128, 2, 2048 if part == 1 else 2], BF16)
            cw = fsb("cw", [128, 24, 4], F32)
            xt = fsb("xt", [128, KD, TT], F32)
            hb = fsb("hb", [128, KD, TT], BF16)
            sq = fsb("sq", [128, KD, TT], BF16)
            rstd = fsb("rstd", [128, TT], F32)
            halo = fsb("halo", [128, 24, 3], F32)
            ql = fsb("ql", [128, 3, TT], F32)
            qn = fsb("qn", [128, 3, TT], BF16)
            kvl = fsb("kvl", [128, 2, TT], F32)
            kvn = fsb("kvn", [128, 2, TT], BF16)
            kpe = fsb("kpe", [64, TT], F32)
            sqpe = fsb("sqpe", [64, TT], BF16)
            tmpring = Ring(fs, "tmp", [128, TT], F32, 3)
            prering = Ring(fs, "pre", [128, TT + 3], F32, 4)
            accring = Ring(fs, "acc", [128, TT], F32, 4)
            sring = Ring(fs, "s", [128, TT], F32, 8)
            sqring = Ring(fs, "sqb", [128, TT], BF16, 8)
            rsring = Ring(fs, "rs", [128, TT], F32, 3)
            obring = Ring(fs, "ob", [128, TT], BF16, 8)
            pending = []
            pendA = []
            pendB = []
            smring = Ring(fs, "sm", [128, 32], F32, 4)
            bgring = Ring(fs, "bgt", [128, 24], F32, 2)
            vtring = Ring(fs, "vt", [128, 512], BF16, 2)
            r_win = [Res() for _ in range(KD)]
            r_wq, r_wkv, r_cw, r_halo = Res(), Res(), Res(), Res()
            r_xt = [Res() for _ in range(KD)]
            r_hb = [Res() for _ in range(KD)]
            r_sq, r_rstd = Res(), Res()
            r_ql, r_qn, r_kvl, r_kvn, r_kpe, r_sqpe = Res(), Res(), Res(), Res(), Res(), Res()
            for k in range(KD):
                dma("pool", win[:, k, :], win_d[l, k * 128:(k + 1) * 128, cbase:cbase + NC_], r=[], w=[r_win[k]])
            for c in (range(3) if part == 1 else ()):
                dma("pool", wq[:, c, :], wq_d[l, c * 128:(c + 1) * 128, :], r=[], w=[r_wq])
            for c in (range(2) if part == 1 else ()):
                dma("pool", wkv[:, c, :], wkv_d[l, c * 128:(c + 1) * 128, :], r=[], w=[r_wkv])
            dma("sp", cw[:], convT_d[l, :, :, :], r=[], w=[r_cw])
            P.op("dve", lambda e: e.memset(halo[:], 0.0), w=[r_halo])
            pv = lambda c: pvec[:, l, c:c + 1]

            def proj(col0, M, bk, rows=128):
                for k in range(KD):
                    mm(banks[bk][0:M, :], win[:, k, col0 - cbase:col0 - cbase + M], hb[:, k, :], k == 0, k == KD - 1,
                       r=[r_win[k], r_hb[k]], w=[r_bank[bk]])

            def rsq_from_bank(bk, scale, rows=128):
                rs, r_rs = rsring.get()
                act(rs[0:rows, :], banks[bk][0:rows, :], AF.Ln, r=[r_bank[bk], r_eps], w=[r_rs], bias=eps_t[0:rows, :], scale=scale)
                act(rs[0:rows, :], rs[0:rows, :], AF.Exp, r=[r_rs], w=[r_rs], scale=-0.5)
                return rs, r_rs

            def rope_out(src, r_src, dst_dram, r_dst):
                bk = 5
                mm(banks[bk][0:64, :], rot_b, src, True, True, r=[r_src, r_cbf], w=[r_bank[bk]])
                t1, r_t1 = tmpring.get()
                tt("dve", t1[0:64, :], banks[bk][0:64, :], cst_t[:, 1, :], ALU.mult, r=[r_bank[bk], r_cst_t], w=[r_t1])
                t2, r_t2 = tmpring.get()
                tt("dve", t2[0:64, :], src, cst_t[:, 0, :], ALU.mult, r=[r_src, r_cst_t], w=[r_t2])
                ob, r_ob = obring.get()
                tt("dve", ob[0:64, :], t1[0:64, :], t2[0:64, :], ALU.add, r=[r_t1, r_t2], w=[r_ob])
                dma("sp", dst_dram, ob[0:64, :], r=[r_ob], w=[r_dst])

            load_xt(0, xt, r_xt, "pool")
            for t in range(NT):
                tsl = slice(t * TT, (t + 1) * TT)
                norm_tile(l, 1, xt, r_xt, hb, r_hb, sq, r_sq, rstd, r_rstd, tmpring)
                if t + 1 < NT:
                    load_xt(t + 1, xt, r_xt, "pool")
                if part == 1:
                    cst_t, r_cst_t = cstr.get()
                    dma("sp", cst_t[:], cs_s[:, :, tsl].rearrange("a p t -> p a t"), r=[r_cs], w=[r_cst_t])
                def stage1pair(c0):
                    cs = (c0, c0 + 1)
                    bks = [nb(), nb()]
                    pres = [prering.get(), prering.get()]
                    accs = [accring.get(), accring.get()]
                    for i, c in enumerate(cs):
                        proj(c * 128, 128, bks[i])
                    for i, c in enumerate(cs):
                        pre, r_pre = pres[i]
                        cp("dve", pre[:, 0:3], halo[:, c, :], r=[r_halo], w=[r_pre])
                        act(pre[:, 3:TT + 3], banks[bks[i]][:], AF.Copy, r=[r_bank[bks[i]]], w=[r_pre])
                        cp("dve", halo[:, c, :], pre[:, TT:TT + 3], r=[r_pre], w=[r_halo])
                    for i, c in enumerate(cs):
                        pre, r_pre = pres[i]
                        acc, r_acc = accs[i]
                        tsc("dve", acc[:], pre[:, 3:TT + 3], cw[:, c, 3:4], None, ALU.mult, ALU.bypass, r=[r_pre, r_cw], w=[r_acc])
                    for jj in (2, 1, 0):
                        for i, c in enumerate(cs):
                            pre, r_pre = pres[i]
                            acc, r_acc = accs[i]
                            stt("dve", acc[:], pre[:, jj:jj + TT], cw[:, c, jj:jj + 1], acc[:], ALU.mult, ALU.add,
                                r=[r_pre, r_cw, r_acc], w=[r_acc])
                    return accs

                def stage2(c, acc, r_acc):
                    if c >= 16:
                        ob, r_ob = obring.get()
                        act(ob[:], acc[:], AF.Silu, r=[r_acc], w=[r_ob])
                        dma("sp", vT_s[:, c - 16, tsl], ob[:], r=[r_ob], w=[r_scr["v"]])
                    else:
                        s_, r_s = sring.get()
                        act(s_[:], acc[:], AF.Silu, r=[r_acc], w=[r_s])
                        sqb, r_sqb = sqring.get()
                        act(sqb[:], s_[:], AF.Square, r=[r_s], w=[r_sqb])

                        def tail(c=c, s_=s_, r_s=r_s, sqb=sqb, r_sqb=r_sqb, tsl=tsl):
                            b2 = nb()
                            mm(banks[b2][:], ones_bf[:], sqb[:], True, True, r=[r_sqb, r_ones], w=[r_bank[b2]])
                            rs, r_rs = rsq_from_bank(b2, 1.0)
                            ob, r_ob = obring.get()
                            tt("dve", ob[:], s_[:], rs[:], ALU.mult, r=[r_s, r_rs], w=[r_ob])
                            if c < 8:
                                dma("sp", qT_s[:, c, tsl], ob[:], r=[r_ob], w=[r_scr["q"]])
                            else:
                                dma("sp", kT_s[:, c - 8, tsl], ob[:], r=[r_ob], w=[r_scr["k"]])
                        pending.append(tail)
                    if len(pending) >= 6:
                        for _ in range(4):
                            pending.pop(0)()

                if part == 0:
                    prev = None
                    for c0 in range(0, 24, 2):
                        cur_ = stage1pair(c0)
                        if prev is not None:
                            stage2(c0 - 2, *prev[0])
                            stage2(c0 - 1, *prev[1])
                        prev = cur_
                    stage2(22, *prev[0])
                    stage2(23, *prev[1])
                while pending:
                    pending.pop(0)()
                for c in (range(8) if part == 0 else ()):
                    bk = nb()
                    proj(3072 + c * 128, 128, bk)
                    ob, r_ob = obring.get()
                    act(ob[:], banks[bk][:], AF.Silu, r=[r_bank[bk]], w=[r_ob])
                    dma("sp", zs_s[:, c, tsl], ob[:], r=[r_ob], w=[r_scr["z"]])
                for c in (range(16) if part == 1 else ()):
                    bk = nb()
                    proj(4816 + c * 128, 128, bk)
                    ob, r_ob = obring.get()
                    act(ob[:], banks[bk][:], AF.Sigmoid, r=[r_bank[bk]], w=[r_ob])
                    if c < 8:
                        dma("sp", ga_s[:, c, tsl], ob[:], r=[r_ob], w=[r_scr["ga"]])
                    else:
                        dma("sp", gb_s[:, c - 8, tsl], ob[:], r=[r_ob], w=[r_scr["gb"]])
                for sub in (range(4) if part == 0 else ()):
                    bk = nb()
                    ssl = slice(sub * 128, (sub + 1) * 128)
                    for k in range(KD):
                        mm(banks[bk][:, 0:16], hb[:, k, ssl], win[:, k, 4096:4112], k == 0, k == KD - 1,
                           r=[r_win[k], r_hb[k]], w=[r_bank[bk]])
                    sm, r_sm = smring.get()
                    bgt, r_bgt = bgring.get()
                    act(sm[:, 0:8], banks[bk][:, 0:8], AF.Exp, r=[r_bank[bk]], w=[r_sm], scale=-1.0)
                    act(sm[:, 8:16], sm[:, 0:8], AF.Ln, r=[r_sm], w=[r_sm], bias=1.0)
                    tsc("dve", bgt[:, 16:24], sm[:, 8:16], -1.0, None, ALU.mult, ALU.bypass, r=[r_sm], w=[r_bgt])
                    act(bgt[:, 0:8], bgt[:, 16:24], AF.Exp, r=[r_bgt], w=[r_bgt])
                    tt("dve", sm[:, 16:24], banks[bk][:, 8:16], hvec[:, l, 8:16], ALU.add, r=[r_bank[bk], r_hv, r_sm], w=[r_sm])
                    act(sm[:, 24:32], sm[:, 16:24], AF.Exp, r=[r_sm], w=[r_sm])
                    act(sm[:, 16:24], sm[:, 24:32], AF.Ln, r=[r_sm], w=[r_sm], bias=1.0)
                    tt("dve", bgt[:, 8:16], sm[:, 16:24], hvec[:, l, 0:8], ALU.mult, r=[r_sm, r_hv, r_bgt], w=[r_bgt])
                    dma("sp", bg_s[t * TT + sub * 128:t * TT + (sub + 1) * 128, :], bgt[:], r=[r_bgt], w=[r_scr["bg"]])
                if part == 0:
                    continue
                for c in range(3):
                    bk = nb()
                    proj(4112 + c * 128, 128, bk)
                    act(ql[:, c, :], banks[bk][:], AF.Copy, r=[r_bank[bk]], w=[r_ql])
                for c in range(2):
                    bk = nb()
                    proj(4496 + c * 128, 128, bk)
                    act(kvl[:, c, :], banks[bk][:], AF.Copy, r=[r_bank[bk]], w=[r_kvl])
                bk = nb()
                proj(4752, 64, bk)
                act(kpe[:], banks[bk][0:64, :], AF.Copy, r=[r_bank[bk]], w=[r_kpe])
                act(sqpe[:], kpe[:], AF.Square, r=[r_kpe], w=[r_sqpe])
                act(sq[:, 0:3, :], ql[:], AF.Square, r=[r_ql], w=[r_sq])
                bk = nb()
                for c in range(3):
                    mm(banks[bk][:], ones_bf[:], sq[:, c, :], c == 0, c == 2, r=[r_sq, r_ones], w=[r_bank[bk]])
                rs, r_rs = rsq_from_bank(bk, 1.0 / 384)
                for c in range(3):
                    stt("dve", qn[:, c, :], ql[:, c, :], pv(1 + c), rs[:], ALU.mult, ALU.mult, r=[r_ql, r_rs, r_hv], w=[r_qn])
                act(sq[:, 4:6, :], kvl[:], AF.Square, r=[r_kvl], w=[r_sq])
                bk = nb()
                for c in range(2):
                    mm(banks[bk][:], ones_bf[:], sq[:, 4 + c, :], c == 0, c == 1, r=[r_sq, r_ones], w=[r_bank[bk]])
                rs, r_rs = rsq_from_bank(bk, 1.0 / 256)
                for c in range(2):
                    stt("dve", kvn[:, c, :], kvl[:, c, :], pv(4 + c), rs[:], ALU.mult, ALU.mult, r=[r_kvl, r_rs, r_hv], w=[r_kvn])
                hcount = 0
                for isk in (0, 1):
                    for h in range(8):
                        bn = hcount % 2
                        br = 2 + hcount % 2
                        hcount += 1
                        if isk == 0:
                            for c in range(3):
                                mm(banks[bn][:], wq[:, c, h * 192:h * 192 + 128], qn[:, c, :], c == 0, c == 2, r=[r_wq, r_qn], w=[r_bank[bn]])
                            for c in range(3):
                                mm(banks[br][0:64, :], wq[:, c, h * 192 + 128:h * 192 + 192], qn[:, c, :], c == 0, c == 2, r=[r_wq, r_qn], w=[r_bank[br]])
                            sqr, r_sqr = sqring.get()
                            act(sqr[0:64, :], banks[br][0:64, :], AF.Square, r=[r_bank[br]], w=[r_sqr])
                            ropesrc, r_ropesrc = banks[br][0:64, :], r_bank[br]
                            gcol = 6
                        else:
                            for c in range(2):
                                mm(banks[bn][:], wkv[:, c, h * 256:h * 256 + 128], kvn[:, c, :], c == 0, c == 1, r=[r_wkv, r_kvn], w=[r_bank[bn]])
                            sqr, r_sqr = sqpe, r_sqpe
                            ropesrc, r_ropesrc = kpe[:], r_kpe
                            gcol = 8
                        sqn, r_sqn = sqring.get()
                        act(sqn[:], banks[bn][:], AF.Square, r=[r_bank[bn]], w=[r_sqn])

                        def tailA(h=h, isk=isk, bn=bn, sqn=sqn, r_sqn=r_sqn, sqr=sqr, r_sqr=r_sqr, ropesrc=ropesrc,
                                  r_ropesrc=r_ropesrc, gcol=gcol, tsl=tsl):
                            b2 = 4
                            mm(banks[b2][:], ones_bf[:], sqn[:], True, False, r=[r_sqn, r_ones], w=[r_bank[b2]])
                            mm(banks[b2][:], ones_bf[0:64, :], sqr[0:64, :], False, True, r=[r_sqr, r_ones], w=[r_bank[b2]])
                            rs, r_rs = rsq_from_bank(b2, 1.0 / 192)
                            ob, r_ob = obring.get()
                            stt("dve", ob[:], banks[bn][:], pv(gcol), rs[:], ALU.mult, ALU.mult, r=[r_bank[bn], r_rs, r_hv], w=[r_ob])
                            dst = qm_s if isk == 0 else km_s
                            r_dst = r_scr["qm"] if isk == 0 else r_scr["km"]
                            dma("sp", dst[h, 0:128, tsl], ob[:], r=[r_ob], w=[r_dst])
                            rb, r_rb = obring.get()
                            stt("dve", rb[0:64, :], ropesrc, pvec[0:64, l, gcol + 1:gcol + 2], rs[0:64, :], ALU.mult, ALU.mult,
                                r=[r_ropesrc, r_rs, r_hv], w=[r_rb])

                            def tailB():
                                rope_out(rb[0:64, :], r_rb, dst[h, 128:192, tsl], r_dst)
                            pendB.append(tailB)
                        pendA.append(tailA)
                        while len(pendA) > 1:
                            pendA.pop(0)()
                        while len(pendB) > 1:
                            pendB.pop(0)()
                while pendA:
                    pendA.pop(0)()
                while pendB:
                    pendB.pop(0)()
                for sub in range(4):
                    ssl = slice(sub * 128, (sub + 1) * 128)
                    for g in range(2):
                        bk = nb()
                        for c in range(2):
                            rhs = wkv[:, c, g * 1024:(g + 1) * 1024].rearrange("p (h x) -> p h x", x=256)[:, :, 128:256]
                            mm(banks[bk][:].rearrange("p (h x) -> p h x", x=128), kvn[:, c, ssl], rhs, c == 0, c == 1,
                               r=[r_wkv, r_kvn], w=[r_bank[bk]])
                        vt, r_vt = vtring.get()
                        act(vt[:], banks[bk][:], AF.Copy, r=[r_bank[bk]], w=[r_vt])
                        dma("sp", vm_s[t * TT + sub * 128:t * TT + (sub + 1) * 128, g * 512:(g + 1) * 512], vt[:], r=[r_vt], w=[r_scr["vm"]])
            P.barrier()
    def bc_h(ap2d):
        return ap2d.unsqueeze(1).to_broadcast([128, 8, ap2d.shape[1]])

    def bc_h4(ap2d):
        return ap2d.unsqueeze(1).to_broadcast([128, 4, ap2d.shape[1]])

    def bc_x(ap2d, n=128):
        return ap2d.unsqueeze(2).to_broadcast([128, ap2d.shape[1], n])

    def b4(bk):
        return banks[bk][:].rearrange("p (h x) -> p h x", x=128)

    def mix_gdn(l):
        with ExitStack() as fs:
            def fsb(name, shape, dt):
                return fs.enter_context(nc.sbuf_tensor(un(name), list(shape), dt))
            S_f = fsb("S_f", [128, 8, 128], F32)
            S_b = fsb("S_b", [128, 8, 128], BF16)
            r_S = [Res(), Res()]
            r_Sb = [Res(), Res()]
            P.op("dve", lambda e: e.memset(S_f[:], 0.0), w=r_S)
            P.op("dve", lambda e: e.memset(S_b[:], 0.0), w=r_Sb)
            qring = Ring(fs, "qc", [128, 8, 128], BF16, 2)
            kring = Ring(fs, "kc", [128, 8, 128], BF16, 2)
            vring = Ring(fs, "vc", [128, 8, 128], BF16, 2)
            zring = Ring(fs, "zc", [128, 8, 128], BF16, 2)
            bgring = Ring(fs, "bgc", [128, 24], F32, 2)
            gcring = Ring(fs, "gcs", [128, 16], F32, 2)
            egring = Ring(fs, "eg", [128, 16], F32, 2)
            Dg = fsb("Dg", [128, 8, 128], F32)
            r_Dg = Res()
            U = fsb("U", [128, 2, 8, 128], F32)
            r_U = Res()
            E = fsb("E", [128, 2, 8, 128], F32)
            r_E = Res()
            kdec = fsb("kdec", [128, 8, 128], BF16)
            r_kdec = Res()
            vtok = fsb("vtok", [128, 8, 128], BF16)
            r_vtok = Res()
            Ap = [fsb("ApA", [128, 8, 128], F32), fsb("ApB", [128, 8, 128], F32)]
            Mp = [fsb("MpA", [128, 8, 128], F32), fsb("MpB", [128, 8, 128], F32)]
            r_Ap = [[Res(), Res()], [Res(), Res()]]
            r_Mp = [[Res(), Res()], [Res(), Res()]]
            X = fsb("X", [128, 8, 128], F32)
            r_X = [Res(), Res()]
            Ybf = fsb("Ybf", [128, 8, 128], BF16)
            r_Y = Res()
            qkT = fsb("qkT", [128, 8, 128], BF16)
            r_qkT = Res()
            rr = fsb("rr", [128, 8, 128], BF16)
            r_rr = Res()
            vnew = fsb("vnew", [128, 8, 128], BF16)
            r_vnew = Res()
            tmpf = fsb("tmpf", [128, 8, 128], F32)
            r_tmpf = Res()
            o_f = fsb("o_f", [128, 8, 128], F32)
            r_of = Res()
            sqo = fsb("sqo", [128, 8, 128], F32)
            r_sqo = Res()
            ssr = Ring(fs, "ss", [128, 8], F32, 2)
            onb = fsb("onb", [128, 8, 128], BF16)
            r_on = Res()
            ogring = Ring(fs, "ogc", [128, 8, 128], BF16, 2)

            def hview(dr, csl):
                return dr[:, :, csl]

            import os as _os
            _c0 = int(_os.environ.get('GDN_C0', '0'))
            _cut = int(_os.environ.get('GDN_CUT', '99'))
            _cut2 = int(_os.environ.get('GDN_CUT2', '99'))
            _c1 = int(_os.environ.get('GDN_C1', str(NCH)))
            for c in range(_c0, min(NCH, _c1)):
                csl = slice(c * 128, (c + 1) * 128)
                qc, r_qc = qring.get()
                kc, r_kc = kring.get()
                vc, r_vc = vring.get()
                zc, r_zc = zring.get()
                bgc, r_bgc = bgring.get()
                for h in range(8):
                    dma("sp", qc[:, h, :], qT_s[:, h, csl], r=[r_scr["q"]], w=[r_qc])
                    dma("sp", kc[:, h, :], kT_s[:, h, csl], r=[r_scr["k"]], w=[r_kc])
                    dma("sp", vc[:, h, :], vT_s[:, h, csl], r=[r_scr["v"]], w=[r_vc])
                    dma("sp", zc[:, h, :], zs_s[:, h, csl], r=[r_scr["z"]], w=[r_zc])
                dma("sp", bgc[:], bg_s[csl, :], r=[r_scr["bg"]], w=[r_bgc])
                if _cut <= 1:
                    continue
                gcs, r_gcs = gcring.get()
                bk = nb()
                mm(banks[bk][:, 0:8], tri_f, bgc[:, 8:16], True, True, r=[r_cst, r_bgc], w=[r_bank[bk]])
                cp("dve", gcs[:, 0:8], banks[bk][:, 0:8], r=[r_bank[bk]], w=[r_gcs])
                tt("dve", gcs[:, 8:16], gcs[:, 0:8], bgc[:, 16:24], ALU.subtract, r=[r_gcs, r_bgc], w=[r_gcs])
                if _cut2 <= 1:
                    continue
                tt("dve", Dg[:], bc_h(ident_f), bc_x(gcs[:, 0:8]), ALU.mult, r=[r_cst, r_gcs], w=[r_Dg])
                bR = [nb(), nb()]
                for hf in range(2):
                    mm(banks[bR[hf]][:], ones_f[:], Dg[:, 4 * hf:4 * hf + 4, :].rearrange("p h x -> p (h x)"), True, True,
                       r=[r_ones, r_Dg], w=[r_bank[bR[hf]]])
                if _cut2 <= 2:
                    continue
                eg, r_eg = egring.get()
                for hf in range(2):
                    hs = slice(4 * hf, 4 * hf + 4)
                    tt("dve", U[:, 0, hs, :], b4(bR[hf]), bc_x(gcs[:, 4 * hf:4 * hf + 4]), ALU.subtract,
                       r=[r_bank[bR[hf]], r_gcs], w=[r_U])
                    tt("dve", U[:, 1, hs, :], b4(bR[hf]), bc_x(gcs[:, 8 + 4 * hf:8 + 4 * hf + 4]), ALU.subtract,
                       r=[r_bank[bR[hf]], r_gcs], w=[r_U])
                    act(eg[:, 8 + 4 * hf:8 + 4 * hf + 4], b4(bR[hf])[:, :, 127], AF.Exp, r=[r_bank[bR[hf]]], w=[r_eg])
                if _os.environ.get('GDN_DBG'):
                    dma("sp", dbgU[c, :, :], U[:].rearrange("p a h x -> p (a h x)"), r=[r_U], w=[Res()], final=True)
                    dma("sp", dbgG[c, :, :], gcs[:], r=[r_gcs], w=[Res()], final=True)
                if _cut2 <= 3:
                    continue
                tt("dve", U[:, 0, :, :], U[:, 0, :, :], bc_h(maski), ALU.min, r=[r_U, r_cst], w=[r_U])
                tt("dve", U[:, 1, :, :], U[:, 1, :, :], bc_h(masks), ALU.min, r=[r_U, r_cst], w=[r_U])
                if _cut2 <= 4:
                    continue
                act(E[:], U[:], AF.Exp, r=[r_U], w=[r_E])
                act(eg[:, 0:8], gcs[:, 0:8], AF.Exp, r=[r_gcs], w=[r_eg])
                if _cut <= 2:
                    continue
                for h in range(8):
                    tr(bbank[0][:, h * 128:(h + 1) * 128], kc[:, h, :], ident_b, r=[r_kc, r_cbf], w=[r_bb[0]])
                for h in range(8):
                    tr(bbank[1][:, h * 128:(h + 1) * 128], vc[:, h, :], ident_b, r=[r_vc, r_cbf], w=[r_bb[1]])
                tt("dve", kdec[:], bbank[0][:].rearrange("p (h x) -> p h x", x=128), bc_x(E[:, 0, :, 127]), ALU.mult,
                   r=[r_bb[0], r_E], w=[r_kdec])
                act(vtok[:].rearrange("p h x -> p (h x)"), bbank[1][:], AF.Copy, r=[r_bb[1]], w=[r_vtok])
                if _cut <= 3:
                    continue
                bG = [nb(), nb()]
                for h in range(8):
                    mm(banks[bG[h // 4]][:, (h % 4) * 128:(h % 4 + 1) * 128], kc[:, h, :], kc[:, h, :], True, True,
                       r=[r_kc], w=[r_bank[bG[h // 4]]])
                for hf in range(2):
                    hs = slice(4 * hf, 4 * hf + 4)
                    tt("dve", Ap[0][:, hs, :], b4(bG[hf]), E[:, 1, hs, :], ALU.mult, r=[r_bank[bG[hf]], r_E], w=[r_Ap[0][hf]])
                bQ = [nb(), nb()]
                for h in range(8):
                    mm(banks[bQ[h // 4]][:, (h % 4) * 128:(h % 4 + 1) * 128], kc[:, h, :], qc[:, h, :], True, True,
                       r=[r_kc, r_qc], w=[r_bank[bQ[h // 4]]])
                for hf in range(2):
                    hs = slice(4 * hf, 4 * hf + 4)
                    stt("dve", qkT[:, hs, :], b4(bQ[hf]), QSCALE, E[:, 0, hs, :], ALU.mult, ALU.mult,
                        r=[r_bank[bQ[hf]], r_E], w=[r_qkT])
                if _cut <= 4:
                    continue
                for hf in range(2):
                    hs = slice(4 * hf, 4 * hf + 4)
                    bk = nb()
                    for hh in range(4):
                        tr(banks[bk][:, hh * 128:(hh + 1) * 128], Ap[0][:, 4 * hf + hh, :], ident_f, r=[r_Ap[0][hf], r_cst], w=[r_bank[bk]])
                    act(Mp[0][:, hs, :], b4(bk), AF.Copy, r=[r_bank[bk]], w=[r_Mp[0][hf]])
                    tt("dve", X[:, hs, :], bc_h4(ident_f), Ap[0][:, hs, :], ALU.subtract, r=[r_cst, r_Ap[0][hf]], w=[r_X[hf]])
                cur = 0
                for lev in range(6):
                    last = lev == 5
                    nxt = 1 - cur
                    bMs = [nb(), nb()]
                    for hf in range(2):
                        for hh in range(4):
                            h = 4 * hf + hh
                            mm(banks[bMs[hf]][:, hh * 128:(hh + 1) * 128], Ap[cur][:, h, :], Mp[cur][:, h, :], True, True,
                               r=[r_Ap[cur][hf], r_Mp[cur][hf]], w=[r_bank[bMs[hf]]])
                    if not last:
                        bAs = [nb(), nb()]
                        for hf in range(2):
                            for hh in range(4):
                                h = 4 * hf + hh
                                mm(banks[bAs[hf]][:, hh * 128:(hh + 1) * 128], Mp[cur][:, h, :], Ap[cur][:, h, :], True, True,
                                   r=[r_Ap[cur][hf], r_Mp[cur][hf]], w=[r_bank[bAs[hf]]])
                    for hf in range(2):
                        hs = slice(4 * hf, 4 * hf + 4)
                        act(Mp[nxt][:, hs, :], b4(bMs[hf]), AF.Copy, r=[r_bank[bMs[hf]]], w=[r_Mp[nxt][hf]])
                    if not last:
                        for hf in range(2):
                            hs = slice(4 * hf, 4 * hf + 4)
                            cp("dve", Ap[nxt][:, hs, :], b4(bAs[hf]), r=[r_bank[bAs[hf]]], w=[r_Ap[nxt][hf]])
                    bXs = [nb(), nb()]
                    for hf in range(2):
                        for hh in range(4):
                            h = 4 * hf + hh
                            mm(banks[bXs[hf]][:, hh * 128:(hh + 1) * 128], Mp[nxt][:, h, :], X[:, h, :], True, True,
                               r=[r_Mp[nxt][hf], r_X[hf]], w=[r_bank[bXs[hf]]])
                    for hf in range(2):
                        hs = slice(4 * hf, 4 * hf + 4)
                        tt("dve", X[:, hs, :], X[:, hs, :], b4(bXs[hf]), ALU.add, r=[r_X[hf], r_bank[bXs[hf]]], w=[r_X[hf]])
                    cur = nxt
                cp("dve", Ybf[:], X[:], r=r_X, w=[r_Y])
                if _cut <= 5:
                    continue
                bK = [nb(), nb()]
                for h in range(8):
                    mm(banks[bK[h // 4]][:, (h % 4) * 128:(h % 4 + 1) * 128], kc[:, h, :], S_b[:, h, :], True, True,
                       r=[r_kc, r_Sb[h // 4]], w=[r_bank[bK[h // 4]]])
                for hf in range(2):
                    hs = slice(4 * hf, 4 * hf + 4)
                    tt("dve", tmpf[:, hs, :], b4(bK[hf]), bc_x(eg[:, 4 * hf:4 * hf + 4]), ALU.mult, r=[r_bank[bK[hf]], r_eg], w=[r_tmpf])
                tt("dve", rr[:], vtok[:], tmpf[:], ALU.subtract, r=[r_vtok, r_tmpf], w=[r_rr])
                bV = [nb(), nb()]
                for h in range(8):
                    mm(banks[bV[h // 4]][:, (h % 4) * 128:(h % 4 + 1) * 128], Ybf[:, h, :], rr[:, h, :], True, True,
                       r=[r_Y, r_rr], w=[r_bank[bV[h // 4]]])
                for hf in range(2):
                    hs = slice(4 * hf, 4 * hf + 4)
                    tt("dve", vnew[:, hs, :], b4(bV[hf]), bc_x(bgc[:, 4 * hf:4 * hf + 4]), ALU.mult, r=[r_bank[bV[hf]], r_bgc], w=[r_vnew])
                bO1 = [nb(), nb()]
                for h in range(8):
                    mm(banks[bO1[h // 4]][:, (h % 4) * 128:(h % 4 + 1) * 128], qc[:, h, :], S_b[:, h, :], True, True,
                       r=[r_qc, r_Sb[h // 4]], w=[r_bank[bO1[h // 4]]])
                for hf in range(2):
                    hs = slice(4 * hf, 4 * hf + 4)
                    tt("dve", tmpf[:, hs, :], b4(bO1[hf]), bc_x(eg[:, 4 * hf:4 * hf + 4]), ALU.mult, r=[r_bank[bO1[hf]], r_eg, r_rr], w=[r_tmpf])
                bO2 = [nb(), nb()]
                for h in range(8):
                    mm(banks[bO2[h // 4]][:, (h % 4) * 128:(h % 4 + 1) * 128], qkT[:, h, :], vnew[:, h, :], True, True,
                       r=[r_qkT, r_vnew], w=[r_bank[bO2[h // 4]]])
                for hf in range(2):
                    hs = slice(4 * hf, 4 * hf + 4)
                    stt("dve", o_f[:, hs, :], tmpf[:, hs, :], QSCALE, b4(bO2[hf]), ALU.mult, ALU.add,
                        r=[r_tmpf, r_bank[bO2[hf]]], w=[r_of])
                bS = [nb(), nb()]
                for h in range(8):
                    mm(banks[bS[h // 4]][:, (h % 4) * 128:(h % 4 + 1) * 128], kdec[:, h, :], vnew[:, h, :], True, True,
                       r=[r_kdec, r_vnew], w=[r_bank[bS[h // 4]]])
                for hf in range(2):
                    hs = slice(4 * hf, 4 * hf + 4)
                    tt("dve", S_f[:, hs, :], S_f[:, hs, :], bc_x(eg[:, 8 + 4 * hf:8 + 4 * hf + 4]), ALU.mult, r=[r_S[hf], r_eg], w=[r_S[hf]])
                    tt("dve", S_f[:, hs, :], S_f[:, hs, :], b4(bS[hf]), ALU.add, r=[r_S[hf], r_bank[bS[hf]]], w=[r_S[hf]])
                    act(S_b[:, hs, :], S_f[:, hs, :], AF.Copy, r=[r_S[hf]], w=[r_Sb[hf]])
                if _cut <= 6:
                    continue
                tt("dve", sqo[:], o_f[:], o_f[:], ALU.mult, r=[r_of], w=[r_sqo])
                ss, r_ss = ssr.get()
                P.op("dve", lambda e, ss=ss: e.tensor_reduce(ss[:], sqo[:], AX.X, ALU.add), r=[r_sqo], w=[r_ss])
                act(ss[:], ss[:], AF.Ln, r=[r_ss, r_eps], w=[r_ss], bias=eps_t[:], scale=1.0 / 128)
                act(ss[:], ss[:], AF.Exp, r=[r_ss], w=[r_ss], scale=-0.5)
                tt("dve", onb[:], o_f[:], bc_x(ss[:]), ALU.mult, r=[r_of, r_ss], w=[r_on])
                for h in range(8):
                    tr(bbank[0][:, h * 128:(h + 1) * 128], onb[:, h, :], ident_b, r=[r_on, r_cbf], w=[r_bb[0]])
                ogc, r_ogc = ogring.get()
                stt("dve", ogc[:], bbank[0][:].rearrange("p (h x) -> p h x", x=128), pvec[:, l, 0:1], zc[:], ALU.mult, ALU.mult,
                    r=[r_bb[0], r_hv, r_zc], w=[r_ogc])
                for h in range(8):
                    dma("sp", og_s[:, h, csl], ogc[:, h, :], r=[r_ogc], w=[r_scr["og"]])
            P.barrier()

    def mix_attn(l):
        with ExitStack() as fs:
            def fsb(name, shape, dt):
                return fs.enter_context(nc.sbuf_tensor(un(name), list(shape), dt))
            knr = Ring(fs, "kn", [128, S], BF16, 2)
            krr = Ring(fs, "kr", [64, S], BF16, 2)
            qnr = Ring(fs, "qn", [128, S], BF16, 2)
            qrr = Ring(fs, "qr", [64, S], BF16, 2)
            vhr = Ring(fs, "vh", [128, NCH, 128], BF16, 2)
            ptr = Ring(fs, "pT", [128, TT], BF16, 3)
            rdr = Ring(fs, "rden", [128, TT], F32, 2)
            otr = Ring(fs, "oT", [128, TT], BF16, 2)
            amask = cbf[:, 192:192 + 2048]
            qcount = 0
            for h in range(8):
                kn, r_kn = knr.get()
                kr, r_kr = krr.get()
                qn_, r_qn = qnr.get()
                qr, r_qr = qrr.get()
                vh, r_vh = vhr.get()
                dma("sp", kn[:], km_s[h, 0:128, :], r=[r_scr["km"]], w=[r_kn])
                dma("sp", kr[:], km_s[h, 128:192, :], r=[r_scr["km"]], w=[r_kr])
                dma("sp", qn_[:], qm_s[h, 0:128, :], r=[r_scr["qm"]], w=[r_qn])
                dma("sp", qr[:], qm_s[h, 128:192, :], r=[r_scr["qm"]], w=[r_qr])
                for c0 in range(0, NCH, 4):
                    c1 = min(NCH, c0 + 4)
                    dma("sp", vh[:, c0:c1, :], vm_s[c0 * 128:c1 * 128, h * 128:(h + 1) * 128].rearrange("(c p) d -> p c d", p=128),
                        r=[r_scr["vm"]], w=[r_vh])
                for qi in range(NT):
                    qsl = slice(qi * TT, (qi + 1) * TT)
                    nk = 4 * (qi + 1)
                    bO = qcount % 2
                    bD = 2 + qcount % 2
                    qcount += 1
                    def s_mm(kt):
                        bS_ = 4 + (kt % 2)
                        ksl = slice(kt * 128, (kt + 1) * 128)
                        dg = kt >= 4 * qi
                        mm(banks[bS_][:], kn[:, ksl], qn_[:, qsl], True, False, r=[r_kn, r_qn], w=[r_bank[bS_]])
                        mm(banks[bS_][:], kr[:, ksl], qr[:, qsl], False, not dg, r=[r_kr, r_qr], w=[r_bank[bS_]])
                        if dg:
                            dd = kt - 4 * qi
                            mm(banks[bS_][:], ident_b, amask[:, dd * 512:(dd + 1) * 512], False, True, r=[r_cbf], w=[r_bank[bS_]])
                    s_mm(0)
                    for kt in range(nk):
                        bS_ = 4 + (kt % 2)
                        pT, r_pT = ptr.get()
                        act(pT[:], banks[bS_][:], AF.Exp, r=[r_bank[bS_]], w=[r_pT], scale=ASCALE)
                        if kt + 1 < nk:
                            s_mm(kt + 1)
                        mm(banks[bO][:], vh[:, kt, :], pT[:], kt == 0, kt == nk - 1, r=[r_vh, r_pT], w=[r_bank[bO]])
                        mm(banks[bD][:], ones_bf[:], pT[:], kt == 0, kt == nk - 1, r=[r_ones, r_pT], w=[r_bank[bD]])
                    rden, r_rden = rdr.get()
                    act(rden[:], banks[bD][:], AF.Copy, r=[r_bank[bD]], w=[r_rden])
                    P.op("dve", lambda e, rden=rden: e.reciprocal(rden[:], rden[:]), r=[r_rden], w=[r_rden])
                    oT, r_oT = otr.get()
                    tt("dve", oT[:], banks[bO][:], rden[:], ALU.mult, r=[r_bank[bO], r_rden], w=[r_oT])
                    dma("sp", om_s[:, h, qsl], oT[:], r=[r_oT], w=[r_scr["om"]])
            P.barrier()
            bstate["i"] = 0

    def mix_out(l):
        with ExitStack() as fs:
            def fsb(name, shape, dt):
                return fs.enter_context(nc.sbuf_tensor(un(name), list(shape), dt))
            wa = fsb("wa", [128, KD, D], BF16)
            wb = fsb("wb", [128, KD, D], BF16)
            wo = fsb("wo", [128, KD, D], BF16)
            r_wa, r_wb, r_wo = Res(), Res(), Res()
            for k in range(KD):
                dma("pool", wa[:, k, :], wa_d[l, k * 128:(k + 1) * 128, :], r=[], w=[r_wa])
                dma("pool", wb[:, k, :], wb_d[l, k * 128:(k + 1) * 128, :], r=[], w=[r_wb])
                dma("pool", wo[:, k, :], wo_d[l, k * 128:(k + 1) * 128, :], r=[], w=[r_wo])
            xt = fsb("xt", [128, KD, TT], F32)
            r_xt = [Res() for _ in range(KD)]
            ogr = Ring(fs, "ogt", [128, KD, TT], BF16, 2)
            omr = Ring(fs, "omt", [128, KD, TT], BF16, 2)
            gar = Ring(fs, "gat", [128, KD, TT], BF16, 2)
            gbr = Ring(fs, "gbt", [128, KD, TT], BF16, 2)
            yb = fsb("yb", [128, KD, TT], BF16)
            r_yb = [Res() for _ in range(KD)]
            t1r = Ring(fs, "t1", [128, TT], F32, 2)
            t2r = Ring(fs, "t2", [128, TT], F32, 2)

            def kview(dr, tsl):
                return dr[:, :, tsl]
            for t in range(NT):
                tsl = slice(t * TT, (t + 1) * TT)
                load_xt(t, xt, r_xt)
                ogt, r_ogt = ogr.get()
                omt, r_omt = omr.get()
                gat, r_gat = gar.get()
                gbt, r_gbt = gbr.get()
                dma("sp", ogt[:], kview(og_s, tsl), r=[r_scr["og"]], w=[r_ogt])
                dma("sp", omt[:], kview(om_s, tsl), r=[r_scr["om"]], w=[r_omt])
                dma("sp", gat[:], kview(ga_s, tsl), r=[r_scr["ga"]], w=[r_gat])
                dma("sp", gbt[:], kview(gb_s, tsl), r=[r_scr["gb"]], w=[r_gbt])
                for d in range(KD):
                    ds_ = slice(d * 128, (d + 1) * 128)
                    bA = nb()
                    for k in range(KD):
                        mm(banks[bA][:], wa[:, k, ds_], ogt[:, k, :], k == 0, k == KD - 1, r=[r_wa, r_ogt], w=[r_bank[bA]])
                    bB = nb()
                    for k in range(KD):
                        mm(banks[bB][:], wb[:, k, ds_], omt[:, k, :], k == 0, k == KD - 1, r=[r_wb, r_omt], w=[r_bank[bB]])
                    t1, r_t1 = t1r.get()
                    t2, r_t2 = t2r.get()
                    tt("dve", t1[:], banks[bA][:], gat[:, d, :], ALU.mult, r=[r_bank[bA], r_gat], w=[r_t1])
                    tt("dve", t2[:], banks[bB][:], gbt[:, d, :], ALU.mult, r=[r_bank[bB], r_gbt], w=[r_t2])
                    tt("pool", yb[:, d, :], t1[:], t2[:], ALU.add, r=[r_t1, r_t2], w=[r_yb[d]])
                for d in range(KD):
                    ds_ = slice(d * 128, (d + 1) * 128)
                    bk = nb()
                    for k in range(KD):
                        mm(banks[bk][:], wo[:, k, ds_], yb[:, k, :], k == 0, k == KD - 1, r=[r_wo, r_yb[k]], w=[r_bank[bk]])
                    stt("dve", xt[:, d, :], banks[bk][:], vG(l, 1, d), xt[:, d, :], ALU.mult, ALU.add,
                        r=[r_bank[bk], r_vec, r_xt[d]], w=[r_xt[d]])
                    dma("sp", out_d[ds_, tsl], xt[:, d, :], r=[r_xt[d]], w=[r_res[t]], final=True)
            P.barrier()
        state["first"] = False

    stages = ("ffn1", "proj", "gdn", "attn", "mixout", "full")
    si = stages.index(stage)
    for l in range(L):
        ffn(l, 0, 0)
        if si >= 1:
            mix_proj(l, 0)
            mix_proj(l, 1)
        if si >= 2:
            mix_gdn(l)
        if si >= 3:
            mix_attn(l)
        if si >= 4:
            mix_out(l)
        if si >= 5:
            ffn(l, 2, 1)

    nwait = P.emit()
    es.close()
    return nc, dict(n_instr=len(P.instrs), nwait=nwait)


def prep_inputs(inputs, S):
    f = np.float32
    sh = {}
    sh["consts"] = make_consts()
    sh["ada_w"] = np.ascontiguousarray(inputs["ada_w"], dtype=f)
    sh["ada_bT"] = np.ascontiguousarray(np.asarray(inputs["ada_b"]).reshape(4, 72, 128).transpose(0, 2, 1), dtype=f)
    g = np.stack([np.asarray(inputs["norm_ffn1"]), np.asarray(inputs["norm_mix"]), np.asarray(inputs["norm_ffn2"])], axis=1)
    sh["gainsT"] = np.ascontiguousarray(g.reshape(4, 3, KD, 128).transpose(0, 1, 3, 2), dtype=f)
    for n in ("ffn1_w1", "ffn1_w3", "ffn1_w2", "ffn2_w1", "ffn2_w3", "ffn2_w2", "w_in", "mla_w_q_up", "mla_w_kv_up",
              "w_branch_a", "w_branch_b", "w_out"):
        sh[n] = np.ascontiguousarray(inputs[n], dtype=f)
    conv = np.asarray(inputs["gdn_conv"], dtype=f)
    sh["convT"] = np.ascontiguousarray(conv.reshape(4, 4, 24, 128).transpose(0, 3, 2, 1))
    hv = np.zeros((4, 128, 16), f)
    hv[:, :, 0:8] = np.asarray(inputs["gdn_a_log"], dtype=f)[:, None, :]
    hv[:, :, 8:16] = np.asarray(inputs["gdn_dt_bias"], dtype=f)[:, None, :]
    sh["headvec"] = hv
    pv = np.zeros((4, 128, 16), f)
    pv[:, :, 0] = np.asarray(inputs["gdn_out_gain"], dtype=f)
    pv[:, :, 1:4] = np.asarray(inputs["mla_q_lat_gain"], dtype=f).reshape(4, 3, 128).transpose(0, 2, 1)
    pv[:, :, 4:6] = np.asarray(inputs["mla_kv_lat_gain"], dtype=f).reshape(4, 2, 128).transpose(0, 2, 1)
    qn = np.asarray(inputs["mla_q_norm"], dtype=f)
    kn = np.asarray(inputs["mla_k_norm"], dtype=f)
    pv[:, :, 6] = qn[:, 0:128]
    pv[:, 0:64, 7] = qn[:, 128:192]
    pv[:, :, 8] = kn[:, 0:128]
    pv[:, 0:64, 9] = kn[:, 128:192]
    sh["partvec"] = pv
    maps = []
    B = inputs["x"].shape[0]
    for b in range(B):
        m = dict(sh)
        m["xT"] = np.ascontiguousarray(np.asarray(inputs["x"])[b, :S].T, dtype=f)
        m["cT"] = np.ascontiguousarray(np.asarray(inputs["c"])[b].reshape(KD, 128).T, dtype=f)
        m["pos"] = np.ascontiguousarray(np.asarray(inputs["positions"])[b, :S].reshape(1, S), dtype=np.int32)
        maps.append(m)
    return maps


def kernel(**inputs):
    S = inputs["x"].shape[1]
    B = inputs["x"].shape[0]
    nc, info = build_program(S, 4)
    maps = prep_inputs(inputs, S)
    res = run_bass_kernel_spmd(nc, maps, core_ids=list(range(B)))
    out = np.stack([np.ascontiguousarray(r["outT"].T) for r in res.results], axis=0)
    return out.astype(np.float32)
```

```python
import numpy as np
from contextlib import ExitStack
import concourse.bass as bass
import concourse.mybir as mybir
from concourse.bass_utils import run_bass_kernel_spmd

F32 = mybir.dt.float32
BF16 = mybir.dt.bfloat16
I32 = mybir.dt.int32
AF = mybir.ActivationFunctionType
ALU = mybir.AluOpType
AX = mybir.AxisListType

D = 1024
KD = 8
DFF = 2816
KF = 22
NIN = 6864
EPS = 1e-6
TT = 512


class Res:
    __slots__ = ("w", "r", "name", "excl")

    def __init__(self, name="", excl=False):
        self.w = None
        self.r = {}
        self.name = name
        self.excl = excl


class Instr:
    __slots__ = ("eng", "fn", "deps", "idx", "signal", "sem", "semval", "dma", "gidx")

    def __init__(self, eng, fn, dma):
        self.eng = eng
        self.fn = fn
        self.dma = dma
        self.deps = set()
        self.signal = False
        self.sem = None
        self.semval = 0


NDMA_SEM = 12


class Prog:
    ENGS = ("pe", "act", "dve", "pool", "sp")

    def __init__(self, nc):
        self.nc = nc
        self.instrs = []
        self.cnt = {e: 0 for e in self.ENGS}
        self.last = {e: None for e in self.ENGS}
        self.dma_hist = {e: [] for e in self.ENGS}
        self.out_dmas = []

    def op(self, eng, fn, r=(), w=(), dma=False):
        ins = Instr(eng, fn, dma)
        ins.idx = self.cnt[eng]
        self.cnt[eng] += 1
        ins.gidx = len(self.instrs)
        if any(x.excl for x in r):
            w = list(w) + [x for x in r if x.excl]
            r = [x for x in r if not x.excl]
        deps = ins.deps
        for x in r:
            if x.w is not None:
                deps.add(x.w)
        for x in w:
            if x.w is not None:
                deps.add(x.w)
            for v in x.r.values():
                if isinstance(v, list):
                    deps.update(v)
                else:
                    deps.add(v)
        for x in r:
            if dma:
                x.r.setdefault("dma", []).append(ins)
            else:
                x.r[eng] = ins
        for x in w:
            x.w = ins
            x.r = {}
        deps.discard(ins)
        if dma:
            h = self.dma_hist[eng]
            if len(h) >= NDMA_SEM:
                deps.add(h[-NDMA_SEM])
            h.append(ins)
        self.instrs.append(ins)
        self.last[eng] = ins
        return ins

    def barrier(self):
        lasts = [v for v in self.last.values() if v is not None]
        alld = []
        for e in self.ENGS:
            alld.extend(self.dma_hist[e][-NDMA_SEM:])
        for e in self.ENGS:
            eobj = e
            ins = self.op(e, (lambda en: en.nop()), dma=False)
            for d in lasts + alld:
                if d is not ins:
                    ins.deps.add(d)

    def emit(self):
        nc = self.nc
        eobj = {"pe": nc.tensor, "act": nc.scalar, "dve": nc.vector, "pool": nc.gpsimd, "sp": nc.sync}
        esem = {e: nc.alloc_semaphore("cs_" + e) for e in self.ENGS}
        dsem = {e: [nc.alloc_semaphore("ds_%s_%d" % (e, i)) for i in range(NDMA_SEM)] for e in ("sp", "pool", "act")}
        for ins in self.instrs:
            keep = set()
            for d in ins.deps:
                if (not d.dma) and d.eng == ins.eng:
                    if ins.eng == "pe":
                        continue
                    if ins.dma:
                        continue
                    if ins.idx - d.idx > 3:
                        continue
                keep.add(d)
            if ins.dma:
                for d in ins.deps:
                    if (not d.dma) and d.eng == ins.eng:
                        keep.add(d)
            ins.deps = keep
            for d in keep:
                d.signal = True
        cnt = {e: 0 for e in self.ENGS}
        dcnt = {e: 0 for e in self.ENGS}
        for ins in self.instrs:
            if ins.dma:
                k = dcnt[ins.eng]
                dcnt[ins.eng] += 1
                ins.sem = dsem[ins.eng][k % NDMA_SEM]
                ins.semval = 16 * (k // NDMA_SEM + 1)
            elif ins.signal:
                cnt[ins.eng] += 1
                ins.sem = esem[ins.eng]
                ins.semval = cnt[ins.eng]
        waited = {e: {} for e in self.ENGS}
        nwait = 0
        for ins in self.instrs:
            e = eobj[ins.eng]
            wd = waited[ins.eng]
            waits = {}
            for d in ins.deps:
                key = id(d.sem)
                if wd.get(key, 0) >= d.semval:
                    continue
                cur = waits.get(key)
                if cur is None or cur[1] < d.semval:
                    waits[key] = (d.sem, d.semval)
            wl = list(waits.values())
            for (s, v) in wl[:-1]:
                e.wait_ge(s, v)
                nwait += 1
            bi = ins.fn(e)
            if wl:
                bi._wait_ge(wl[-1][0], wl[-1][1])
            for key, (s, v) in waits.items():
                wd[key] = v
            if ins.dma:
                bi.then_inc(ins.sem, 16)
            elif ins.signal:
                bi.then_inc(ins.sem, 1)
        fin = {}
        for d in self.out_dmas:
            key = id(d.sem)
            if key not in fin or fin[key][1] < d.semval:
                fin[key] = (d.sem, d.semval)
        for (s, v) in fin.values():
            nc.sync.wait_ge(s, v)
        return nwait


NCONST = 640 + 4 * 512
BIG = 30000.0
QSCALE = 128.0 ** -0.5
ASCALE = 192.0 ** -0.5


def make_consts():
    c = np.zeros((128, NCONST), np.float32)
    c[:, 0:128] = np.eye(128, dtype=np.float32)
    k = np.arange(128)[:, None]
    i = np.arange(128)[None, :]
    c[:, 128:256] = (k <= i).astype(np.float32)
    c[:, 256:384] = np.where(i >= k, 0.0, -BIG)
    c[:, 384:512] = np.where(i > k, 0.0, -BIG)
    rot = np.zeros((128, 64), np.float32)
    for m in range(32):
        rot[m + 32, m] = -1.0
        rot[m, m + 32] = 1.0
    c[:, 512:576] = rot
    half = 32
    invf = (10000.0 ** (-(np.arange(half, dtype=np.float32)) / half)).astype(np.float32)
    c[0:64, 576] = np.concatenate([invf, invf])
    q = np.arange(512)[None, :]
    for d in range(4):
        c[:, 640 + d * 512:640 + (d + 1) * 512] = np.where(d * 128 + k <= q, 0.0, -BIG)
    return c


def build_program(S, L, stage="full", dbg=()):
    NT = S // TT
    NCH = S // 128
    nc = bass.Bass("TRN2", target_bir_lowering=False)
    P = Prog(nc)
    es = ExitStack()

    def dram_in(name, shape, dt=F32):
        return nc.dram_tensor(name, list(shape), dt, kind="ExternalInput").ap()

    def scratch(name, shape, dt):
        kind = "ExternalOutput" if name in dbg else "Internal"
        return nc.dram_tensor(name, list(shape), dt, kind=kind).ap()

    xT_d = dram_in("xT", [D, S])
    cT_d = dram_in("cT", [128, KD])
    pos_d = dram_in("pos", [1, S], I32)
    consts_d = dram_in("consts", [128, NCONST])
    adaw_d = dram_in("ada_w", [4, D, 9 * D])
    adab_d = dram_in("ada_bT", [4, 128, 72])
    gains_d = dram_in("gainsT", [4, 3, 128, KD])
    w1_d = [dram_in("ffn1_w1", [4, D, DFF]), dram_in("ffn2_w1", [4, D, DFF])]
    w3_d = [dram_in("ffn1_w3", [4, D, DFF]), dram_in("ffn2_w3", [4, D, DFF])]
    w2_d = [dram_in("ffn1_w2", [4, DFF, D]), dram_in("ffn2_w2", [4, DFF, D])]
    win_d = dram_in("w_in", [4, D, NIN])
    convT_d = dram_in("convT", [4, 128, 24, 4])
    hv_d = dram_in("headvec", [4, 128, 16])
    pv_d = dram_in("partvec", [4, 128, 16])
    wq_d = dram_in("mla_w_q_up", [4, 384, 1536])
    wkv_d = dram_in("mla_w_kv_up", [4, 256, 2048])
    wa_d = dram_in("w_branch_a", [4, D, D])
    wb_d = dram_in("w_branch_b", [4, D, D])
    wo_d = dram_in("w_out", [4, D, D])
    out_d = nc.dram_tensor("outT", [D, S], F32, kind="ExternalOutput").ap()

    qT_s = scratch("qT_s", [128, KD, S], BF16)
    kT_s = scratch("kT_s", [128, KD, S], BF16)
    vT_s = scratch("vT_s", [128, KD, S], BF16)
    zs_s = scratch("zs_s", [128, KD, S], BF16)
    ga_s = scratch("ga_s", [128, KD, S], BF16)
    gb_s = scratch("gb_s", [128, KD, S], BF16)
    bg_s = scratch("bg_s", [S, 24], F32)
    qm_s = scratch("qm_s", [8, 192, S], BF16)
    km_s = scratch("km_s", [8, 192, S], BF16)
    vm_s = scratch("vm_s", [S, D], BF16)
    og_s = scratch("og_s", [128, KD, S], BF16)
    om_s = scratch("om_s", [128, KD, S], BF16)
    import os as _os0
    if _os0.environ.get('GDN_DBG'):
        dbgU = nc.dram_tensor("dbgU", [NCH, 128, 2048], F32, kind="ExternalOutput").ap()
        dbgG = nc.dram_tensor("dbgG", [NCH, 128, 16], F32, kind="ExternalOutput").ap()
    r_scr = {n: Res(n) for n in ("q", "k", "v", "z", "ga", "gb", "bg", "qm", "km", "vm", "og", "om")}

    uid = [0]

    def un(name):
        uid[0] += 1
        return "%s_u%d" % (name, uid[0])

    def sb(name, shape, dt):
        return es.enter_context(nc.sbuf_tensor(name, list(shape), dt))

    def mm(out, lhsT, rhs, start, stop, r, w):
        return P.op("pe", lambda e: e.matmul(out, lhsT, rhs, start=start, stop=stop), r=r, w=w)

    def tr(out, in_, ident, r, w):
        return P.op("pe", lambda e: e.transpose(out, in_, ident), r=r, w=w)

    def act(out, in_, func, r, w, bias=None, scale=1.0, accum=None):
        def f(e):
            kw = {}
            if bias is not None:
                kw["bias"] = bias
            if accum is not None:
                kw["accum_out"] = accum
            return e.activation(out=out, in_=in_, func=func, scale=scale, **kw)
        return P.op("act", f, r=r, w=w)

    def tt(eng, out, in0, in1, op, r, w):
        return P.op(eng, lambda e: e.tensor_tensor(out, in0, in1, op), r=r, w=w)

    def stt(eng, out, in0, scalar, in1, op0, op1, r, w):
        return P.op(eng, lambda e: e.scalar_tensor_tensor(out=out, in0=in0, scalar=scalar, in1=in1, op0=op0, op1=op1), r=r, w=w)

    def tsc(eng, out, in0, s1, s2, op0, op1, r, w):
        if s2 is None:
            return P.op(eng, lambda e: e.tensor_scalar(out, in0, s1, None, op0), r=r, w=w)
        return P.op(eng, lambda e: e.tensor_scalar(out, in0, s1, s2, op0, op1), r=r, w=w)

    def cp(eng, out, in_, r, w):
        return P.op(eng, lambda e: e.tensor_copy(out, in_), r=r, w=w)

    def dma(eng, out, in_, r, w, final=False):
        ins = P.op(eng, lambda e: e.dma_start(out=out, in_=in_), r=r, w=w, dma=True)
        if final:
            P.out_dmas.append(ins)
        return ins

    class Ring:
        def __init__(self, ctx, name, shape, dt, n):
            self.t = [ctx.enter_context(nc.sbuf_tensor(un("%s_%d" % (name, i)), list(shape), dt)) for i in range(n)]
            self.r = [Res("%s_%d" % (name, i)) for i in range(n)]
            self.i = 0
            self.n = n

        def get(self):
            k = self.i % self.n
            self.i += 1
            return self.t[k], self.r[k]

    cst = sb("cst", [128, 640], F32)
    r_cst = Res("cst")
    dma("sp", cst[:], consts_d[:, 0:640], r=[], w=[r_cst])
    ident_f = cst[:, 0:128]
    tri_f = cst[:, 128:256]
    maski = cst[:, 256:384]
    masks = cst[:, 384:512]
    cbf = sb("cbf", [128, 192 + 2048], BF16)
    r_cbf = Res("cbf")
    dma("pool", cbf[:, 0:128], consts_d[:, 0:128], r=[], w=[r_cbf])
    dma("pool", cbf[:, 128:192], consts_d[:, 512:576], r=[], w=[r_cbf])
    dma("pool", cbf[:, 192:192 + 2048], consts_d[:, 640:640 + 2048], r=[], w=[r_cbf])
    ident_b = cbf[:, 0:128]
    rot_b = cbf[0:64, 128:192]
    ones_bf = sb("ones_bf", [128, 128], BF16)
    ones_f = sb("ones_f", [128, 128], F32)
    r_ones = Res("ones")
    P.op("dve", lambda e: e.memset(ones_bf[:], 1.0), w=[r_ones])
    P.op("dve", lambda e: e.memset(ones_f[:], 1.0), w=[r_ones])
    eps_t = sb("eps_t", [128, 1], F32)
    r_eps = Res("eps")
    P.op("dve", lambda e: e.memset(eps_t[:], EPS), w=[r_eps])

    NBANK = 6
    banks = [es.enter_context(nc.psum_tensor("bank%d" % i, [128, 512], F32)) for i in range(NBANK)]
    r_bank = [Res("bank%d" % i, excl=True) for i in range(NBANK)]
    bbank = [es.enter_context(nc.psum_tensor("bbank%d" % i, [128, 1024], BF16)) for i in range(2)]
    r_bb = [Res("bb0", excl=True), Res("bb1", excl=True)]
    bstate = {"i": 0}

    def nb():
        k = bstate["i"] % NBANK
        bstate["i"] += 1
        return k

    cs_s = scratch("cs_s", [2, 64, S], F32)
    r_rope = Res("rope")
    r_cs = Res("cs")
    with ExitStack() as es2:
        cos2 = es2.enter_context(nc.sbuf_tensor("cos2", [64, S], F32))
        sin2 = es2.enter_context(nc.sbuf_tensor("sin2", [64, S], F32))
        posi = es2.enter_context(nc.sbuf_tensor("posi", [64, S], I32))
        ang = es2.enter_context(nc.sbuf_tensor("ang", [64, S], F32))
        nn = es2.enter_context(nc.sbuf_tensor("nn", [64, S], F32))
        r_p = Res("posi")
        r_a = Res("ang")
        r_n = Res("nn")
        dma("sp", posi[:], pos_d[0:1, :].partition_broadcast(64), r=[], w=[r_p])
        cp("dve", ang[:], posi[:], r=[r_p], w=[r_a])
        tsc("dve", ang[:], ang[:], cst[0:64, 576:577], None, ALU.mult, ALU.bypass, r=[r_a, r_cst], w=[r_a])
        TWO_PI = 2.0 * np.pi
        C1 = 6.28125
        C2 = float(TWO_PI - C1)
        MAGIC = 12582912.0
        tsc("dve", nn[:], ang[:], float(1.0 / TWO_PI), MAGIC, ALU.mult, ALU.add, r=[r_a], w=[r_n])
        tsc("dve", nn[:], nn[:], -MAGIC, None, ALU.add, ALU.bypass, r=[r_n], w=[r_n])
        stt("dve", ang[:], nn[:], -C1, ang[:], ALU.mult, ALU.add, r=[r_n, r_a], w=[r_a])
        stt("dve", ang[:], nn[:], -C2, ang[:], ALU.mult, ALU.add, r=[r_n, r_a], w=[r_a])
        tsc("dve", ang[:], ang[:], float(np.pi), float(-np.pi), ALU.min, ALU.max, r=[r_a], w=[r_a])
        act(sin2[:], ang[:], AF.Sin, r=[r_a], w=[r_rope])
        tsc("dve", nn[:], ang[:], float(np.pi / 2), None, ALU.add, ALU.bypass, r=[r_a], w=[r_n])
        tsc("dve", ang[:], nn[:], float(np.pi), float(-TWO_PI), ALU.is_gt, ALU.mult, r=[r_n], w=[r_a])
        tt("dve", nn[:], nn[:], ang[:], ALU.add, r=[r_n, r_a], w=[r_n])
        tsc("dve", nn[:], nn[:], float(np.pi), float(-np.pi), ALU.min, ALU.max, r=[r_n], w=[r_n])
        act(cos2[:], nn[:], AF.Sin, r=[r_n], w=[r_rope])
        dma("sp", cs_s[0, :, :], cos2[:], r=[r_rope], w=[r_cs])
        dma("sp", cs_s[1, :, :], sin2[:], r=[r_rope], w=[r_cs])
        P.barrier()

    condT = sb("condT", [128, KD], F32)
    r_cond = Res("cond")
    dma("sp", condT[:], cT_d[:, :], r=[], w=[r_cond])
    act(condT[:], condT[:], AF.Silu, r=[r_cond], w=[r_cond])
    modT = sb("modT", [128, 4, 72], F32)
    r_mod = Res("mod")
    adab = sb("adab", [128, 4, 72], F32)
    r_adab = Res("adab")
    for l in range(4):
        dma("sp", adab[:, l, :], adab_d[l, :, :], r=[], w=[r_adab])
    gains = sb("gains", [128, 4, 3, KD], F32)
    r_gains = Res("gains")
    for l in range(4):
        for j in range(3):
            dma("sp", gains[:, l, j, :], gains_d[l, j, :, :], r=[], w=[r_gains])
    hvec = sb("hvec", [128, 4, 16], F32)
    pvec = sb("pvec", [128, 4, 16], F32)
    r_hv = Res("hv")
    for l in range(4):
        dma("sp", hvec[:, l, :], hv_d[l, :, :], r=[], w=[r_hv])
        dma("sp", pvec[:, l, :], pv_d[l, :, :], r=[], w=[r_hv])
    for l in range(4):
        act(hvec[:, l, 0:8], hvec[:, l, 0:8], AF.Exp, r=[r_hv], w=[r_hv])
        tsc("dve", hvec[:, l, 0:8], hvec[:, l, 0:8], -1.0, None, ALU.mult, ALU.bypass, r=[r_hv], w=[r_hv])
    NG = 8
    GW = 9 * D // NG
    with ExitStack() as es2:
        awb = [es2.enter_context(nc.sbuf_tensor(un("awb%d" % i), [128, KD, GW], F32)) for i in range(2)]
        r_awb = [Res("awb0"), Res("awb1")]
        gi = 0
        for l in range(L):
            for g in range(NG):
                bsel = gi % 2
                for k in range(KD):
                    dma("sp", awb[bsel][:, k, :], adaw_d[l, k * 128:(k + 1) * 128, g * GW:(g + 1) * GW], r=[], w=[r_awb[bsel]])
                bk = nb()
                nj = GW // 128
                for jj in range(nj):
                    for k in range(KD):
                        mm(banks[bk][:, jj:jj + 1], awb[bsel][:, k, jj * 128:(jj + 1) * 128], condT[:, k:k + 1],
                           k == 0, k == KD - 1, r=[r_awb[bsel], r_cond], w=[r_bank[bk]])
                tt("dve", modT[:, l, g * nj:(g + 1) * nj], banks[bk][:, 0:nj], adab[:, l, g * nj:(g + 1) * nj], ALU.add,
                   r=[r_bank[bk], r_adab], w=[r_mod])
                gi += 1
        P.barrier()
    vecA = sb("vecA", [128, 4, 3, KD], F32)
    vecG = sb("vecG", [128, 4, 3, KD], F32)
    r_vec = Res("vec")
    for l in range(L):
        for j in range(3):
            sc = modT[:, l, (3 * j + 1) * KD:(3 * j + 2) * KD]
            gg = modT[:, l, (3 * j + 2) * KD:(3 * j + 3) * KD]
            stt("dve", vecA[:, l, j, :], sc, 1.0, gains[:, l, j, :], ALU.add, ALU.mult, r=[r_mod, r_gains], w=[r_vec])
            gs = 1.0 if j == 1 else 0.5
            tsc("dve", vecG[:, l, j, :], gg, 1.0, gs, ALU.add, ALU.mult, r=[r_mod], w=[r_vec])

    def vA(l, j, k):
        return vecA[:, l, j, k:k + 1]

    def vB(l, j, k):
        return modT[:, l, 3 * j * KD + k:3 * j * KD + k + 1]

    def vG(l, j, k):
        return vecG[:, l, j, k:k + 1]

    P.barrier()

    r_res = [Res("res%d" % t) for t in range(NT)]
    state = {"first": True}

    def norm_tile(l, j, xt, r_xt, hb, r_hb, sq, r_sq, rstd, r_rstd, tmpring):
        act(sq[:], xt[:], AF.Square, r=r_xt, w=[r_sq])
        bk = nb()
        for k in range(KD):
            mm(banks[bk][:], ones_bf[:], sq[:, k, :], k == 0, k == KD - 1, r=[r_sq, r_ones], w=[r_bank[bk]])
        act(rstd[:], banks[bk][:], AF.Ln, r=[r_bank[bk], r_eps], w=[r_rstd], bias=eps_t[:], scale=1.0 / D)
        act(rstd[:], rstd[:], AF.Exp, r=[r_rstd], w=[r_rstd], scale=-0.5)
        for k in range(KD):
            tm, r_tm = tmpring.get()
            stt("dve", tm[:], xt[:, k, :], vA(l, j, k), rstd[:], ALU.mult, ALU.mult, r=[r_xt[k], r_rstd, r_vec], w=[r_tm])
            act(hb[:, k, :], tm[:], AF.Identity, r=[r_tm, r_mod], w=[r_hb[k]], bias=vB(l, j, k))

    def load_xt(t, xt, r_xt, eng="sp"):
        src = xT_d if state["first"] else out_d
        ts_ = slice(t * TT, (t + 1) * TT)
        for k in range(KD):
            dma(eng, xt[:, k, :], src[k * 128:(k + 1) * 128, ts_], r=[r_res[t]], w=[r_xt[k]])

    def ffn(l, j, which):
        with ExitStack() as fs:
            def fsb(name, shape, dt):
                return fs.enter_context(nc.sbuf_tensor(un(name), list(shape), dt))
            w1 = fsb("w1", [128, KD, DFF], BF16)
            w3 = fsb("w3", [128, KD, DFF], BF16)
            w2 = fsb("w2", [128, KF, D], BF16)
            xt = fsb("xt", [128, KD, TT], F32)
            hb = fsb("hb", [128, KD, TT], BF16)
            ub = fsb("ub", [128, KF, TT], BF16)
            sq = fsb("sq", [128, KD, TT], BF16)
            rstd = fsb("rstd", [128, TT], F32)
            tmpring = Ring(fs, "tmp", [128, TT], F32, 2)
            r_w1 = [Res() for _ in range(KD)]
            r_w3 = [Res() for _ in range(KD)]
            r_w2 = [Res() for _ in range(KF)]
            r_xt = [Res() for _ in range(KD)]
            r_hb = [Res() for _ in range(KD)]
            r_ub = [Res() for _ in range(KF)]
            r_sq = Res()
            r_rstd = Res()
            for k in range(KD):
                dma("pool", w1[:, k, :], w1_d[which][l, k * 128:(k + 1) * 128, :], r=[], w=[r_w1[k]])
                dma("pool", w3[:, k, :], w3_d[which][l, k * 128:(k + 1) * 128, :], r=[], w=[r_w3[k]])
            for f in range(KF):
                dma("pool", w2[:, f, :], w2_d[which][l, f * 128:(f + 1) * 128, :], r=[], w=[r_w2[f]])
            for t in range(NT):
                ts_ = slice(t * TT, (t + 1) * TT)
                load_xt(t, xt, r_xt)
                norm_tile(l, j, xt, r_xt, hb, r_hb, sq, r_sq, rstd, r_rstd, tmpring)
                for f in range(KF):
                    b1 = nb()
                    b3 = nb()
                    fs_ = slice(f * 128, (f + 1) * 128)
                    for k in range(KD):
                        mm(banks[b1][:], w1[:, k, fs_], hb[:, k, :], k == 0, k == KD - 1, r=[r_w1[k], r_hb[k]], w=[r_bank[b1]])
                    for k in range(KD):
                        mm(banks[b3][:], w3[:, k, fs_], hb[:, k, :], k == 0, k == KD - 1, r=[r_w3[k], r_hb[k]], w=[r_bank[b3]])
                    tm, r_tm = tmpring.get()
                    act(tm[:], banks[b1][:], AF.Silu, r=[r_bank[b1]], w=[r_tm])
                    tt("dve", ub[:, f, :], tm[:], banks[b3][:], ALU.mult, r=[r_tm, r_bank[b3]], w=[r_ub[f]])
                for d in range(KD):
                    bk = nb()
                    ds_ = slice(d * 128, (d + 1) * 128)
                    for f in range(KF):
                        mm(banks[bk][:], w2[:, f, ds_], ub[:, f, :], f == 0, f == KF - 1, r=[r_w2[f], r_ub[f]], w=[r_bank[bk]])
                    stt("dve", xt[:, d, :], banks[bk][:], vG(l, j, d), xt[:, d, :], ALU.mult, ALU.add,
                        r=[r_bank[bk], r_vec, r_xt[d]], w=[r_xt[d]])
                    dma("sp", out_d[ds_, ts_], xt[:, d, :], r=[r_xt[d]], w=[r_res[t]], final=True)
            P.barrier()
        state["first"] = False

    def mix_proj(l, part):
        with ExitStack() as fs:
            def fsb(name, shape, dt):
                return fs.enter_context(nc.sbuf_tensor(un(name), list(shape), dt))
            cbase = 0 if part == 0 else 4112
            NC_ = 4112 if part == 0 else NIN - 4112
            win = fsb("win", [128, KD, NC_], BF16)
            cstr = Ring(fs, "cst_t", [64, 2, TT], F32, 2)
            wq = fsb("wq", [128, 3, 1536 if part == 1 else 2], BF16)
            wkv = fsb("wkv", [128, 2, 2048 if part == 1 else 2], BF16)
            cw = fsb("cw", [128, 24, 4], F32)
            xt = fsb("xt", [128, KD, TT], F32)
            hb = fsb("hb", [128, KD, TT], BF16)
            sq = fsb("sq", [128, KD, TT], BF16)
            rstd = fsb("rstd", [128, TT], F32)
            halo = fsb("halo", [128, 24, 3], F32)
            ql = fsb("ql", [128, 3, TT], F32)
            qn = fsb("qn", [128, 3, TT], BF16)
            kvl = fsb("kvl", [128, 2, TT], F32)
            kvn = fsb("kvn", [128, 2, TT], BF16)
            kpe = fsb("kpe", [64, TT], F32)
            sqpe = fsb("sqpe", [64, TT], BF16)
            tmpring = Ring(fs, "tmp", [128, TT], F32, 3)
            prering = Ring(fs, "pre", [128, TT + 3], F32, 3)
            accring = Ring(fs, "acc", [128, TT], F32, 3)
            sring = Ring(fs, "s", [128, TT], F32, 8)
            sqring = Ring(fs, "sqb", [128, TT], BF16, 8)
            rsring = Ring(fs, "rs", [128, TT], F32, 3)
            obring = Ring(fs, "ob", [128, TT], BF16, 8)
            pending = []
            pendA = []
            pendB = []
            smring = Ring(fs, "sm", [128, 32], F32, 4)
            bgring = Ring(fs, "bgt", [128, 24], F32, 2)
            vtring = Ring(fs, "vt", [128, 512], BF16, 2)
            r_win = [Res() for _ in range(KD)]
            r_wq, r_wkv, r_cw, r_halo = Res(), Res(), Res(), Res()
            r_xt = [Res() for _ in range(KD)]
            r_hb = [Res() for _ in range(KD)]
            r_sq, r_rstd = Res(), Res()
            r_ql, r_qn, r_kvl, r_kvn, r_kpe, r_sqpe = Res(), Res(), Res(), Res(), Res(), Res()
            for k in range(KD):
                dma("pool", win[:, k, :], win_d[l, k * 128:(k + 1) * 128, cbase:cbase + NC_], r=[], w=[r_win[k]])
            for c in (range(3) if part == 1 else ()):
                dma("pool", wq[:, c, :], wq_d[l, c * 128:(c + 1) * 128, :], r=[], w=[r_wq])
            for c in (range(2) if part == 1 else ()):
                dma("pool", wkv[:, c, :], wkv_d[l, c * 128:(c + 1) * 128, :], r=[], w=[r_wkv])
            dma("sp", cw[:], convT_d[l, :, :, :], r=[], w=[r_cw])
            P.op("dve", lambda e: e.memset(halo[:], 0.0), w=[r_halo])
            pv = lambda c: pvec[:, l, c:c + 1]

            def proj(col0, M, bk, rows=128):
                for k in range(KD):
                    mm(banks[bk][0:M, :], win[:, k, col0 - cbase:col0 - cbase + M], hb[:, k, :], k == 0, k == KD - 1,
                       r=[r_win[k], r_hb[k]], w=[r_bank[bk]])

            def rsq_from_bank(bk, scale, rows=128):
                rs, r_rs = rsring.get()
                act(rs[0:rows, :], banks[bk][0:rows, :], AF.Ln, r=[r_bank[bk], r_eps], w=[r_rs], bias=eps_t[0:rows, :], scale=scale)
                act(rs[0:rows, :], rs[0:rows, :], AF.Exp, r=[r_rs], w=[r_rs], scale=-0.5)
                return rs, r_rs

            def rope_out(src, r_src, dst_dram, r_dst):
                bk = 5
                mm(banks[bk][0:64, :], rot_b, src, True, True, r=[r_src, r_cbf], w=[r_bank[bk]])
                t1, r_t1 = tmpring.get()
                tt("dve", t1[0:64, :], banks[bk][0:64, :], cst_t[:, 1, :], ALU.mult, r=[r_bank[bk], r_cst_t], w=[r_t1])
                t2, r_t2 = tmpring.get()
                tt("dve", t2[0:64, :], src, cst_t[:, 0, :], ALU.mult, r=[r_src, r_cst_t], w=[r_t2])
                ob, r_ob = obring.get()
                tt("dve", ob[0:64, :], t1[0:64, :], t2[0:64, :], ALU.add, r=[r_t1, r_t2], w=[r_ob])
                dma("sp", dst_dram, ob[0:64, :], r=[r_ob], w=[r_dst])

            load_xt(0, xt, r_xt, "pool")
            for t in range(NT):
                tsl = slice(t * TT, (t + 1) * TT)
                norm_tile(l, 1, xt, r_xt, hb, r_hb, sq, r_sq, rstd, r_rstd, tmpring)
                if t + 1 < NT:
                    load_xt(t + 1, xt, r_xt, "pool")
                if part == 1:
                    cst_t, r_cst_t = cstr.get()
                    dma("sp", cst_t[:], cs_s[:, :, tsl].rearrange("a p t -> p a t"), r=[r_cs], w=[r_cst_t])
                def stage1(c):
                    bk = nb()
                    proj(c * 128, 128, bk)
                    pre, r_pre = prering.get()
                    cp("dve", pre[:, 0:3], halo[:, c, :], r=[r_halo], w=[r_pre])
                    act(pre[:, 3:TT + 3], banks[bk][:], AF.Copy, r=[r_bank[bk]], w=[r_pre])
                    cp("dve", halo[:, c, :], pre[:, TT:TT + 3], r=[r_pre], w=[r_halo])
                    acc, r_acc = accring.get()
                    tsc("dve", acc[:], pre[:, 3:TT + 3], cw[:, c, 3:4], None, ALU.mult, ALU.bypass, r=[r_pre, r_cw], w=[r_acc])
                    for jj in (2, 1, 0):
                        stt("dve", acc[:], pre[:, jj:jj + TT], cw[:, c, jj:jj + 1], acc[:], ALU.mult, ALU.add,
                            r=[r_pre, r_cw, r_acc], w=[r_acc])
                    return acc, r_acc

                def stage2(c, acc, r_acc):
                    if c >= 16:
                        ob, r_ob = obring.get()
                        act(ob[:], acc[:], AF.Silu, r=[r_acc], w=[r_ob])
                        dma("sp", vT_s[:, c - 16, tsl], ob[:], r=[r_ob], w=[r_scr["v"]])
                    else:
                        s_, r_s = sring.get()
                        act(s_[:], acc[:], AF.Silu, r=[r_acc], w=[r_s])
                        sqb, r_sqb = sqring.get()
                        act(sqb[:], s_[:], AF.Square, r=[r_s], w=[r_sqb])

                        def tail(c=c, s_=s_, r_s=r_s, sqb=sqb, r_sqb=r_sqb, tsl=tsl):
                            b2 = nb()
                            mm(banks[b2][:], ones_bf[:], sqb[:], True, True, r=[r_sqb, r_ones], w=[r_bank[b2]])
                            rs, r_rs = rsq_from_bank(b2, 1.0)
                            ob, r_ob = obring.get()
                            tt("dve", ob[:], s_[:], rs[:], ALU.mult, r=[r_s, r_rs], w=[r_ob])
                            if c < 8:
                                dma("sp", qT_s[:, c, tsl], ob[:], r=[r_ob], w=[r_scr["q"]])
                            else:
                                dma("sp", kT_s[:, c - 8, tsl], ob[:], r=[r_ob], w=[r_scr["k"]])
                        pending.append(tail)
                    if len(pending) >= 6:
                        for _ in range(4):
                            pending.pop(0)()

                if part == 0:
                    prev = None
                    for c in range(24):
                        cur_ = stage1(c)
                        if prev is not None:
                            stage2(c - 1, *prev)
                        prev = cur_
                    stage2(23, *prev)
                while pending:
                    pending.pop(0)()
                for c in (range(8) if part == 0 else ()):
                    bk = nb()
                    proj(3072 + c * 128, 128, bk)
                    ob, r_ob = obring.get()
                    act(ob[:], banks[bk][:], AF.Silu, r=[r_bank[bk]], w=[r_ob])
                    dma("sp", zs_s[:, c, tsl], ob[:], r=[r_ob], w=[r_scr["z"]])
                for c in (range(16) if part == 1 else ()):
                    bk = nb()
                    proj(4816 + c * 128, 128, bk)
                    ob, r_ob = obring.get()
                    act(ob[:], banks[bk][:], AF.Sigmoid, r=[r_bank[bk]], w=[r_ob])
                    if c < 8:
                        dma("sp", ga_s[:, c, tsl], ob[:], r=[r_ob], w=[r_scr["ga"]])
                    else:
                        dma("sp", gb_s[:, c - 8, tsl], ob[:], r=[r_ob], w=[r_scr["gb"]])
                for sub in (range(4) if part == 0 else ()):
                    bk = nb()
                    ssl = slice(sub * 128, (sub + 1) * 128)
                    for k in range(KD):
                        mm(banks[bk][:, 0:16], hb[:, k, ssl], win[:, k, 4096:4112], k == 0, k == KD - 1,
                           r=[r_win[k], r_hb[k]], w=[r_bank[bk]])
                    sm, r_sm = smring.get()
                    bgt, r_bgt = bgring.get()
                    act(sm[:, 0:8], banks[bk][:, 0:8], AF.Exp, r=[r_bank[bk]], w=[r_sm], scale=-1.0)
                    act(sm[:, 8:16], sm[:, 0:8], AF.Ln, r=[r_sm], w=[r_sm], bias=1.0)
                    tsc("dve", bgt[:, 16:24], sm[:, 8:16], -1.0, None, ALU.mult, ALU.bypass, r=[r_sm], w=[r_bgt])
                    act(bgt[:, 0:8], bgt[:, 16:24], AF.Exp, r=[r_bgt], w=[r_bgt])
                    tt("dve", sm[:, 16:24], banks[bk][:, 8:16], hvec[:, l, 8:16], ALU.add, r=[r_bank[bk], r_hv, r_sm], w=[r_sm])
                    act(sm[:, 24:32], sm[:, 16:24], AF.Exp, r=[r_sm], w=[r_sm])
                    act(sm[:, 16:24], sm[:, 24:32], AF.Ln, r=[r_sm], w=[r_sm], bias=1.0)
                    tt("dve", bgt[:, 8:16], sm[:, 16:24], hvec[:, l, 0:8], ALU.mult, r=[r_sm, r_hv, r_bgt], w=[r_bgt])
                    dma("sp", bg_s[t * TT + sub * 128:t * TT + (sub + 1) * 128, :], bgt[:], r=[r_bgt], w=[r_scr["bg"]])
                if part == 0:
                    continue
                for c in range(3):
                    bk = nb()
                    proj(4112 + c * 128, 128, bk)
                    act(ql[:, c, :], banks[bk][:], AF.Copy, r=[r_bank[bk]], w=[r_ql])
                for c in range(2):
                    bk = nb()
                    proj(4496 + c * 128, 128, bk)
                    act(kvl[:, c, :], banks[bk][:], AF.Copy, r=[r_bank[bk]], w=[r_kvl])
                bk = nb()
                proj(4752, 64, bk)
                act(kpe[:], banks[bk][0:64, :], AF.Copy, r=[r_bank[bk]], w=[r_kpe])
                act(sqpe[:], kpe[:], AF.Square, r=[r_kpe], w=[r_sqpe])
                act(sq[:, 0:3, :], ql[:], AF.Square, r=[r_ql], w=[r_sq])
                bk = nb()
                for c in range(3):
                    mm(banks[bk][:], ones_bf[:], sq[:, c, :], c == 0, c == 2, r=[r_sq, r_ones], w=[r_bank[bk]])
                rs, r_rs = rsq_from_bank(bk, 1.0 / 384)
                for c in range(3):
                    stt("dve", qn[:, c, :], ql[:, c, :], pv(1 + c), rs[:], ALU.mult, ALU.mult, r=[r_ql, r_rs, r_hv], w=[r_qn])
                act(sq[:, 4:6, :], kvl[:], AF.Square, r=[r_kvl], w=[r_sq])
                bk = nb()
                for c in range(2):
                    mm(banks[bk][:], ones_bf[:], sq[:, 4 + c, :], c == 0, c == 1, r=[r_sq, r_ones], w=[r_bank[bk]])
                rs, r_rs = rsq_from_bank(bk, 1.0 / 256)
                for c in range(2):
                    stt("dve", kvn[:, c, :], kvl[:, c, :], pv(4 + c), rs[:], ALU.mult, ALU.mult, r=[r_kvl, r_rs, r_hv], w=[r_kvn])
                hcount = 0
                for isk in (0, 1):
                    for h in range(8):
                        bn = hcount % 2
                        br = 2 + hcount % 2
                        hcount += 1
                        if isk == 0:
                            for c in range(3):
                                mm(banks[bn][:], wq[:, c, h * 192:h * 192 + 128], qn[:, c, :], c == 0, c == 2, r=[r_wq, r_qn], w=[r_bank[bn]])
                            for c in range(3):
                                mm(banks[br][0:64, :], wq[:, c, h * 192 + 128:h * 192 + 192], qn[:, c, :], c == 0, c == 2, r=[r_wq, r_qn], w=[r_bank[br]])
                            sqr, r_sqr = sqring.get()
                            act(sqr[0:64, :], banks[br][0:64, :], AF.Square, r=[r_bank[br]], w=[r_sqr])
                            ropesrc, r_ropesrc = banks[br][0:64, :], r_bank[br]
                            gcol = 6
                        else:
                            for c in range(2):
                                mm(banks[bn][:], wkv[:, c, h * 256:h * 256 + 128], kvn[:, c, :], c == 0, c == 1, r=[r_wkv, r_kvn], w=[r_bank[bn]])
                            sqr, r_sqr = sqpe, r_sqpe
                            ropesrc, r_ropesrc = kpe[:], r_kpe
                            gcol = 8
                        sqn, r_sqn = sqring.get()
                        act(sqn[:], banks[bn][:], AF.Square, r=[r_bank[bn]], w=[r_sqn])

                        def tailA(h=h, isk=isk, bn=bn, sqn=sqn, r_sqn=r_sqn, sqr=sqr, r_sqr=r_sqr, ropesrc=ropesrc,
                                  r_ropesrc=r_ropesrc, gcol=gcol, tsl=tsl):
                            b2 = 4
                            mm(banks[b2][:], ones_bf[:], sqn[:], True, False, r=[r_sqn, r_ones], w=[r_bank[b2]])
                            mm(banks[b2][:], ones_bf[0:64, :], sqr[0:64, :], False, True, r=[r_sqr, r_ones], w=[r_bank[b2]])
                            rs, r_rs = rsq_from_bank(b2, 1.0 / 192)
                            ob, r_ob = obring.get()
                            stt("dve", ob[:], banks[bn][:], pv(gcol), rs[:], ALU.mult, ALU.mult, r=[r_bank[bn], r_rs, r_hv], w=[r_ob])
                            dst = qm_s if isk == 0 else km_s
                            r_dst = r_scr["qm"] if isk == 0 else r_scr["km"]
                            dma("sp", dst[h, 0:128, tsl], ob[:], r=[r_ob], w=[r_dst])
                            rb, r_rb = obring.get()
                            stt("dve", rb[0:64, :], ropesrc, pvec[0:64, l, gcol + 1:gcol + 2], rs[0:64, :], ALU.mult, ALU.mult,
                                r=[r_ropesrc, r_rs, r_hv], w=[r_rb])

                            def tailB():
                                rope_out(rb[0:64, :], r_rb, dst[h, 128:192, tsl], r_dst)
                            pendB.append(tailB)
                        pendA.append(tailA)
                        while len(pendA) > 1:
                            pendA.pop(0)()
                        while len(pendB) > 1:
                            pendB.pop(0)()
                while pendA:
                    pendA.pop(0)()
                while pendB:
                    pendB.pop(0)()
                for sub in range(4):
                    ssl = slice(sub * 128, (sub + 1) * 128)
                    for g in range(2):
                        bk = nb()
                        for c in range(2):
                            rhs = wkv[:, c, g * 1024:(g + 1) * 1024].rearrange("p (h x) -> p h x", x=256)[:, :, 128:256]
                            mm(banks[bk][:].rearrange("p (h x) -> p h x", x=128), kvn[:, c, ssl], rhs, c == 0, c == 1,
                               r=[r_wkv, r_kvn], w=[r_bank[bk]])
                        vt, r_vt = vtring.get()
                        act(vt[:], banks[bk][:], AF.Copy, r=[r_bank[bk]], w=[r_vt])
                        dma("sp", vm_s[t * TT + sub * 128:t * TT + (sub + 1) * 128, g * 512:(g + 1) * 512], vt[:], r=[r_vt], w=[r_scr["vm"]])
            P.barrier()
    def bc_h(ap2d):
        return ap2d.unsqueeze(1).to_broadcast([128, 8, ap2d.shape[1]])

    def bc_h4(ap2d):
        return ap2d.unsqueeze(1).to_broadcast([128, 4, ap2d.shape[1]])

    def bc_x(ap2d, n=128):
        return ap2d.unsqueeze(2).to_broadcast([128, ap2d.shape[1], n])

    def b4(bk):
        return banks[bk][:].rearrange("p (h x) -> p h x", x=128)

    def mix_gdn(l):
        with ExitStack() as fs:
            def fsb(name, shape, dt):
                return fs.enter_context(nc.sbuf_tensor(un(name), list(shape), dt))
            S_f = fsb("S_f", [128, 8, 128], F32)
            S_b = fsb("S_b", [128, 8, 128], BF16)
            r_S = [Res(), Res()]
            r_Sb = [Res(), Res()]
            P.op("dve", lambda e: e.memset(S_f[:], 0.0), w=r_S)
            P.op("dve", lambda e: e.memset(S_b[:], 0.0), w=r_Sb)
            qring = Ring(fs, "qc", [128, 8, 128], BF16, 2)
            kring = Ring(fs, "kc", [128, 8, 128], BF16, 2)
            vring = Ring(fs, "vc", [128, 8, 128], BF16, 2)
            zring = Ring(fs, "zc", [128, 8, 128], BF16, 2)
            bgring = Ring(fs, "bgc", [128, 24], F32, 2)
            gcring = Ring(fs, "gcs", [128, 16], F32, 2)
            egring = Ring(fs, "eg", [128, 16], F32, 2)
            Dg = fsb("Dg", [128, 8, 128], F32)
            r_Dg = Res()
            U = fsb("U", [128, 2, 8, 128], F32)
            r_U = Res()
            E = fsb("E", [128, 2, 8, 128], F32)
            r_E = Res()
            kdec = fsb("kdec", [128, 8, 128], BF16)
            r_kdec = Res()
            vtok = fsb("vtok", [128, 8, 128], BF16)
            r_vtok = Res()
            Ap = [fsb("ApA", [128, 8, 128], F32), fsb("ApB", [128, 8, 128], F32)]
            Mp = [fsb("MpA", [128, 8, 128], F32), fsb("MpB", [128, 8, 128], F32)]
            r_Ap = [[Res(), Res()], [Res(), Res()]]
            r_Mp = [[Res(), Res()], [Res(), Res()]]
            X = fsb("X", [128, 8, 128], F32)
            r_X = [Res(), Res()]
            Ybf = fsb("Ybf", [128, 8, 128], BF16)
            r_Y = Res()
            qkT = fsb("qkT", [128, 8, 128], BF16)
            r_qkT = Res()
            rr = fsb("rr", [128, 8, 128], BF16)
            r_rr = Res()
            vnew = fsb("vnew", [128, 8, 128], BF16)
            r_vnew = Res()
            tmpf = fsb("tmpf", [128, 8, 128], F32)
            r_tmpf = Res()
            o_f = fsb("o_f", [128, 8, 128], F32)
            r_of = Res()
            sqo = fsb("sqo", [128, 8, 128], F32)
            r_sqo = Res()
            ssr = Ring(fs, "ss", [128, 8], F32, 2)
            onb = fsb("onb", [128, 8, 128], BF16)
            r_on = Res()
            ogring = Ring(fs, "ogc", [128, 8, 128], BF16, 2)

            def hview(dr, csl):
                return dr[:, :, csl]

            import os as _os
            _c0 = int(_os.environ.get('GDN_C0', '0'))
            _cut = int(_os.environ.get('GDN_CUT', '99'))
            _cut2 = int(_os.environ.get('GDN_CUT2', '99'))
            _c1 = int(_os.environ.get('GDN_C1', str(NCH)))
            def gload(c):
                csl = slice(c * 128, (c + 1) * 128)
                bufs = (qring.get(), kring.get(), vring.get(), zring.get(), bgring.get())
                (qc, r_qc), (kc, r_kc), (vc, r_vc), (zc, r_zc), (bgc, r_bgc) = bufs
                for h in range(8):
                    dma("sp", qc[:, h, :], qT_s[:, h, csl], r=[r_scr["q"]], w=[r_qc])
                    dma("sp", kc[:, h, :], kT_s[:, h, csl], r=[r_scr["k"]], w=[r_kc])
                    dma("sp", vc[:, h, :], vT_s[:, h, csl], r=[r_scr["v"]], w=[r_vc])
                    dma("sp", zc[:, h, :], zs_s[:, h, csl], r=[r_scr["z"]], w=[r_zc])
                dma("sp", bgc[:], bg_s[csl, :], r=[r_scr["bg"]], w=[r_bgc])
                return bufs
            _cend = min(NCH, _c1)
            gpref = {_c0: gload(_c0)} if _c0 < _cend else {}
            for c in range(_c0, _cend):
                csl = slice(c * 128, (c + 1) * 128)
                (qc, r_qc), (kc, r_kc), (vc, r_vc), (zc, r_zc), (bgc, r_bgc) = gpref.pop(c)
                if c + 1 < _cend:
                    gpref[c + 1] = gload(c + 1)
                if _cut <= 1:
                    continue
                gcs, r_gcs = gcring.get()
                bk = nb()
                mm(banks[bk][:, 0:8], tri_f, bgc[:, 8:16], True, True, r=[r_cst, r_bgc], w=[r_bank[bk]])
                cp("dve", gcs[:, 0:8], banks[bk][:, 0:8], r=[r_bank[bk]], w=[r_gcs])
                tt("dve", gcs[:, 8:16], gcs[:, 0:8], bgc[:, 16:24], ALU.subtract, r=[r_gcs, r_bgc], w=[r_gcs])
                if _cut2 <= 1:
                    continue
                tt("dve", Dg[:], bc_h(ident_f), bc_x(gcs[:, 0:8]), ALU.mult, r=[r_cst, r_gcs], w=[r_Dg])
                bR = [nb(), nb()]
                for hf in range(2):
                    mm(banks[bR[hf]][:], ones_f[:], Dg[:, 4 * hf:4 * hf + 4, :].rearrange("p h x -> p (h x)"), True, True,
                       r=[r_ones, r_Dg], w=[r_bank[bR[hf]]])
                if _cut2 <= 2:
                    continue
                eg, r_eg = egring.get()
                for hf in range(2):
                    hs = slice(4 * hf, 4 * hf + 4)
                    tt("dve", U[:, 0, hs, :], b4(bR[hf]), bc_x(gcs[:, 4 * hf:4 * hf + 4]), ALU.subtract,
                       r=[r_bank[bR[hf]], r_gcs], w=[r_U])
                    tt("dve", U[:, 1, hs, :], b4(bR[hf]), bc_x(gcs[:, 8 + 4 * hf:8 + 4 * hf + 4]), ALU.subtract,
                       r=[r_bank[bR[hf]], r_gcs], w=[r_U])
                    act(eg[:, 8 + 4 * hf:8 + 4 * hf + 4], b4(bR[hf])[:, :, 127], AF.Exp, r=[r_bank[bR[hf]]], w=[r_eg])
                if _os.environ.get('GDN_DBG'):
                    dma("sp", dbgU[c, :, :], U[:].rearrange("p a h x -> p (a h x)"), r=[r_U], w=[Res()], final=True)
                    dma("sp", dbgG[c, :, :], gcs[:], r=[r_gcs], w=[Res()], final=True)
                if _cut2 <= 3:
                    continue
                tt("dve", U[:, 0, :, :], U[:, 0, :, :], bc_h(maski), ALU.min, r=[r_U, r_cst], w=[r_U])
                tt("dve", U[:, 1, :, :], U[:, 1, :, :], bc_h(masks), ALU.min, r=[r_U, r_cst], w=[r_U])
                if _cut2 <= 4:
                    continue
                act(E[:], U[:], AF.Exp, r=[r_U], w=[r_E])
                act(eg[:, 0:8], gcs[:, 0:8], AF.Exp, r=[r_gcs], w=[r_eg])
                if _cut <= 2:
                    continue
                for h in range(8):
                    tr(bbank[0][:, h * 128:(h + 1) * 128], kc[:, h, :], ident_b, r=[r_kc, r_cbf], w=[r_bb[0]])
                for h in range(8):
                    tr(bbank[1][:, h * 128:(h + 1) * 128], vc[:, h, :], ident_b, r=[r_vc, r_cbf], w=[r_bb[1]])
                tt("dve", kdec[:], bbank[0][:].rearrange("p (h x) -> p h x", x=128), bc_x(E[:, 0, :, 127]), ALU.mult,
                   r=[r_bb[0], r_E], w=[r_kdec])
                act(vtok[:].rearrange("p h x -> p (h x)"), bbank[1][:], AF.Copy, r=[r_bb[1]], w=[r_vtok])
                if _cut <= 3:
                    continue
                bG = [nb(), nb()]
                for h in range(8):
                    mm(banks[bG[h // 4]][:, (h % 4) * 128:(h % 4 + 1) * 128], kc[:, h, :], kc[:, h, :], True, True,
                       r=[r_kc], w=[r_bank[bG[h // 4]]])
                for hf in range(2):
                    hs = slice(4 * hf, 4 * hf + 4)
                    tt("dve", Ap[0][:, hs, :], b4(bG[hf]), E[:, 1, hs, :], ALU.mult, r=[r_bank[bG[hf]], r_E], w=[r_Ap[0][hf]])
                bQ = [nb(), nb()]
                for h in range(8):
                    mm(banks[bQ[h // 4]][:, (h % 4) * 128:(h % 4 + 1) * 128], kc[:, h, :], qc[:, h, :], True, True,
                       r=[r_kc, r_qc], w=[r_bank[bQ[h // 4]]])
                for hf in range(2):
                    hs = slice(4 * hf, 4 * hf + 4)
                    stt("dve", qkT[:, hs, :], b4(bQ[hf]), QSCALE, E[:, 0, hs, :], ALU.mult, ALU.mult,
                        r=[r_bank[bQ[hf]], r_E], w=[r_qkT])
                if _cut <= 4:
                    continue
                for hf in range(2):
                    hs = slice(4 * hf, 4 * hf + 4)
                    bk = nb()
                    for hh in range(4):
                        tr(banks[bk][:, hh * 128:(hh + 1) * 128], Ap[0][:, 4 * hf + hh, :], ident_f, r=[r_Ap[0][hf], r_cst], w=[r_bank[bk]])
                    act(Mp[0][:, hs, :], b4(bk), AF.Copy, r=[r_bank[bk]], w=[r_Mp[0][hf]])
                    tt("dve", X[:, hs, :], bc_h4(ident_f), Ap[0][:, hs, :], ALU.subtract, r=[r_cst, r_Ap[0][hf]], w=[r_X[hf]])
                cur = 0
                for lev in range(6):
                    last = lev == 5
                    nxt = 1 - cur
                    bMs = [nb(), nb()]
                    for hf in range(2):
                        for hh in range(4):
                            h = 4 * hf + hh
                            mm(banks[bMs[hf]][:, hh * 128:(hh + 1) * 128], Ap[cur][:, h, :], Mp[cur][:, h, :], True, True,
                               r=[r_Ap[cur][hf], r_Mp[cur][hf]], w=[r_bank[bMs[hf]]])
                    if not last:
                        bAs = [nb(), nb()]
                        for hf in range(2):
                            for hh in range(4):
                                h = 4 * hf + hh
                                mm(banks[bAs[hf]][:, hh * 128:(hh + 1) * 128], Mp[cur][:, h, :], Ap[cur][:, h, :], True, True,
                                   r=[r_Ap[cur][hf], r_Mp[cur][hf]], w=[r_bank[bAs[hf]]])
                    for hf in range(2):
                        hs = slice(4 * hf, 4 * hf + 4)
                        act(Mp[nxt][:, hs, :], b4(bMs[hf]), AF.Copy, r=[r_bank[bMs[hf]]], w=[r_Mp[nxt][hf]])
                    if not last:
                        for hf in range(2):
                            hs = slice(4 * hf, 4 * hf + 4)
                            cp("dve", Ap[nxt][:, hs, :], b4(bAs[hf]), r=[r_bank[bAs[hf]]], w=[r_Ap[nxt][hf]])
                    bXs = [nb(), nb()]
                    for hf in range(2):
                        for hh in range(4):
                            h = 4 * hf + hh
                            mm(banks[bXs[hf]][:, hh * 128:(hh + 1) * 128], Mp[nxt][:, h, :], X[:, h, :], True, True,
                               r=[r_Mp[nxt][hf], r_X[hf]], w=[r_bank[bXs[hf]]])
                    for hf in range(2):
                        hs = slice(4 * hf, 4 * hf + 4)
                        tt("dve", X[:, hs, :], X[:, hs, :], b4(bXs[hf]), ALU.add, r=[r_X[hf], r_bank[bXs[hf]]], w=[r_X[hf]])
                    cur = nxt
                cp("dve", Ybf[:], X[:], r=r_X, w=[r_Y])
                if _cut <= 5:
                    continue
                bK = [nb(), nb()]
                for h in range(8):
                    mm(banks[bK[h // 4]][:, (h % 4) * 128:(h % 4 + 1) * 128], kc[:, h, :], S_b[:, h, :], True, True,
                       r=[r_kc, r_Sb[h // 4]], w=[r_bank[bK[h // 4]]])
                for hf in range(2):
                    hs = slice(4 * hf, 4 * hf + 4)
                    tt("dve", tmpf[:, hs, :], b4(bK[hf]), bc_x(eg[:, 4 * hf:4 * hf + 4]), ALU.mult, r=[r_bank[bK[hf]], r_eg], w=[r_tmpf])
                tt("dve", rr[:], vtok[:], tmpf[:], ALU.subtract, r=[r_vtok, r_tmpf], w=[r_rr])
                bV = [nb(), nb()]
                for h in range(8):
                    mm(banks[bV[h // 4]][:, (h % 4) * 128:(h % 4 + 1) * 128], Ybf[:, h, :], rr[:, h, :], True, True,
                       r=[r_Y, r_rr], w=[r_bank[bV[h // 4]]])
                for hf in range(2):
                    hs = slice(4 * hf, 4 * hf + 4)
                    tt("dve", vnew[:, hs, :], b4(bV[hf]), bc_x(bgc[:, 4 * hf:4 * hf + 4]), ALU.mult, r=[r_bank[bV[hf]], r_bgc], w=[r_vnew])
                bO1 = [nb(), nb()]
                for h in range(8):
                    mm(banks[bO1[h // 4]][:, (h % 4) * 128:(h % 4 + 1) * 128], qc[:, h, :], S_b[:, h, :], True, True,
                       r=[r_qc, r_Sb[h // 4]], w=[r_bank[bO1[h // 4]]])
                for hf in range(2):
                    hs = slice(4 * hf, 4 * hf + 4)
                    tt("dve", tmpf[:, hs, :], b4(bO1[hf]), bc_x(eg[:, 4 * hf:4 * hf + 4]), ALU.mult, r=[r_bank[bO1[hf]], r_eg, r_rr], w=[r_tmpf])
                bO2 = [nb(), nb()]
                for h in range(8):
                    mm(banks[bO2[h // 4]][:, (h % 4) * 128:(h % 4 + 1) * 128], qkT[:, h, :], vnew[:, h, :], True, True,
                       r=[r_qkT, r_vnew], w=[r_bank[bO2[h // 4]]])
                for hf in range(2):
                    hs = slice(4 * hf, 4 * hf + 4)
                    stt("dve", o_f[:, hs, :], tmpf[:, hs, :], QSCALE, b4(bO2[hf]), ALU.mult, ALU.add,
                        r=[r_tmpf, r_bank[bO2[hf]]], w=[r_of])
                bS = [nb(), nb()]
                for h in range(8):
                    mm(banks[bS[h // 4]][:, (h % 4) * 128:(h % 4 + 1) * 128], kdec[:, h, :], vnew[:, h, :], True, True,
                       r=[r_kdec, r_vnew], w=[r_bank[bS[h // 4]]])
                for hf in range(2):
                    hs = slice(4 * hf, 4 * hf + 4)
                    tt("dve", S_f[:, hs, :], S_f[:, hs, :], bc_x(eg[:, 8 + 4 * hf:8 + 4 * hf + 4]), ALU.mult, r=[r_S[hf], r_eg], w=[r_S[hf]])
                    tt("dve", S_f[:, hs, :], S_f[:, hs, :], b4(bS[hf]), ALU.add, r=[r_S[hf], r_bank[bS[hf]]], w=[r_S[hf]])
                    act(S_b[:, hs, :], S_f[:, hs, :], AF.Copy, r=[r_S[hf]], w=[r_Sb[hf]])
                if _cut <= 6:
                    continue
                tt("dve", sqo[:], o_f[:], o_f[:], ALU.mult, r=[r_of], w=[r_sqo])
                ss, r_ss = ssr.get()
                P.op("dve", lambda e, ss=ss: e.tensor_reduce(ss[:], sqo[:], AX.X, ALU.add), r=[r_sqo], w=[r_ss])
                act(ss[:], ss[:], AF.Ln, r=[r_ss, r_eps], w=[r_ss], bias=eps_t[:], scale=1.0 / 128)
                act(ss[:], ss[:], AF.Exp, r=[r_ss], w=[r_ss], scale=-0.5)
                tt("dve", onb[:], o_f[:], bc_x(ss[:]), ALU.mult, r=[r_of, r_ss], w=[r_on])
                for h in range(8):
                    tr(bbank[0][:, h * 128:(h + 1) * 128], onb[:, h, :], ident_b, r=[r_on, r_cbf], w=[r_bb[0]])
                ogc, r_ogc = ogring.get()
                stt("dve", ogc[:], bbank[0][:].rearrange("p (h x) -> p h x", x=128), pvec[:, l, 0:1], zc[:], ALU.mult, ALU.mult,
                    r=[r_bb[0], r_hv, r_zc], w=[r_ogc])
                for h in range(8):
                    dma("sp", og_s[:, h, csl], ogc[:, h, :], r=[r_ogc], w=[r_scr["og"]])
            P.barrier()

    def mix_attn(l):
        with ExitStack() as fs:
            def fsb(name, shape, dt):
                return fs.enter_context(nc.sbuf_tensor(un(name), list(shape), dt))
            knr = Ring(fs, "kn", [128, S], BF16, 2)
            krr = Ring(fs, "kr", [64, S], BF16, 2)
            qnr = Ring(fs, "qn", [128, S], BF16, 2)
            qrr = Ring(fs, "qr", [64, S], BF16, 2)
            vhr = Ring(fs, "vh", [128, NCH, 128], BF16, 2)
            ptr = Ring(fs, "pT", [128, TT], BF16, 3)
            rdr = Ring(fs, "rden", [128, TT], F32, 2)
            otr = Ring(fs, "oT", [128, TT], BF16, 2)
            amask = cbf[:, 192:192 + 2048]
            qcount = 0
            def aload(h):
                bufs = (knr.get(), krr.get(), qnr.get(), qrr.get(), vhr.get())
                (kn, r_kn), (kr, r_kr), (qn_, r_qn), (qr, r_qr), (vh, r_vh) = bufs
                dma("sp", kn[:], km_s[h, 0:128, :], r=[r_scr["km"]], w=[r_kn])
                dma("sp", kr[:], km_s[h, 128:192, :], r=[r_scr["km"]], w=[r_kr])
                dma("sp", qn_[:], qm_s[h, 0:128, :], r=[r_scr["qm"]], w=[r_qn])
                dma("sp", qr[:], qm_s[h, 128:192, :], r=[r_scr["qm"]], w=[r_qr])
                for c0 in range(0, NCH, 4):
                    c1 = min(NCH, c0 + 4)
                    dma("sp", vh[:, c0:c1, :], vm_s[c0 * 128:c1 * 128, h * 128:(h + 1) * 128].rearrange("(c p) d -> p c d", p=128),
                        r=[r_scr["vm"]], w=[r_vh])
                return bufs
            apref = {0: aload(0)}
            for h in range(8):
                (kn, r_kn), (kr, r_kr), (qn_, r_qn), (qr, r_qr), (vh, r_vh) = apref.pop(h)
                if h + 1 < 8:
                    apref[h + 1] = aload(h + 1)
                for qi in range(NT):
                    qsl = slice(qi * TT, (qi + 1) * TT)
                    nk = 4 * (qi + 1)
                    bO = qcount % 2
                    bD = 2 + qcount % 2
                    qcount += 1
                    def s_mm(kt):
                        bS_ = 4 + (kt % 2)
                        ksl = slice(kt * 128, (kt + 1) * 128)
                        dg = kt >= 4 * qi
                        mm(banks[bS_][:], kn[:, ksl], qn_[:, qsl], True, False, r=[r_kn, r_qn], w=[r_bank[bS_]])
                        mm(banks[bS_][:], kr[:, ksl], qr[:, qsl], False, not dg, r=[r_kr, r_qr], w=[r_bank[bS_]])
                        if dg:
                            dd = kt - 4 * qi
                            mm(banks[bS_][:], ident_b, amask[:, dd * 512:(dd + 1) * 512], False, True, r=[r_cbf], w=[r_bank[bS_]])
                    s_mm(0)
                    for kt in range(nk):
                        bS_ = 4 + (kt % 2)
                        pT, r_pT = ptr.get()
                        act(pT[:], banks[bS_][:], AF.Exp, r=[r_bank[bS_]], w=[r_pT], scale=ASCALE)
                        if kt + 1 < nk:
                            s_mm(kt + 1)
                        mm(banks[bO][:], vh[:, kt, :], pT[:], kt == 0, kt == nk - 1, r=[r_vh, r_pT], w=[r_bank[bO]])
                        mm(banks[bD][:], ones_bf[:], pT[:], kt == 0, kt == nk - 1, r=[r_ones, r_pT], w=[r_bank[bD]])
                    rden, r_rden = rdr.get()
                    act(rden[:], banks[bD][:], AF.Copy, r=[r_bank[bD]], w=[r_rden])
                    P.op("dve", lambda e, rden=rden: e.reciprocal(rden[:], rden[:]), r=[r_rden], w=[r_rden])
                    oT, r_oT = otr.get()
                    tt("dve", oT[:], banks[bO][:], rden[:], ALU.mult, r=[r_bank[bO], r_rden], w=[r_oT])
                    dma("sp", om_s[:, h, qsl], oT[:], r=[r_oT], w=[r_scr["om"]])
            P.barrier()
            bstate["i"] = 0

    def mix_out(l):
        with ExitStack() as fs:
            def fsb(name, shape, dt):
                return fs.enter_context(nc.sbuf_tensor(un(name), list(shape), dt))
            wa = fsb("wa", [128, KD, D], BF16)
            wb = fsb("wb", [128, KD, D], BF16)
            wo = fsb("wo", [128, KD, D], BF16)
            r_wa, r_wb, r_wo = Res(), Res(), Res()
            for k in range(KD):
                dma("pool", wa[:, k, :], wa_d[l, k * 128:(k + 1) * 128, :], r=[], w=[r_wa])
                dma("pool", wb[:, k, :], wb_d[l, k * 128:(k + 1) * 128, :], r=[], w=[r_wb])
                dma("pool", wo[:, k, :], wo_d[l, k * 128:(k + 1) * 128, :], r=[], w=[r_wo])
            xt = fsb("xt", [128, KD, TT], F32)
            r_xt = [Res() for _ in range(KD)]
            ogr = Ring(fs, "ogt", [128, KD, TT], BF16, 2)
            omr = Ring(fs, "omt", [128, KD, TT], BF16, 2)
            gar = Ring(fs, "gat", [128, KD, TT], BF16, 2)
            gbr = Ring(fs, "gbt", [128, KD, TT], BF16, 2)
            yb = fsb("yb", [128, KD, TT], BF16)
            r_yb = [Res() for _ in range(KD)]
            t1r = Ring(fs, "t1", [128, TT], F32, 2)
            t2r = Ring(fs, "t2", [128, TT], F32, 2)

            def kview(dr, tsl):
                return dr[:, :, tsl]
            for t in range(NT):
                tsl = slice(t * TT, (t + 1) * TT)
                load_xt(t, xt, r_xt)
                ogt, r_ogt = ogr.get()
                omt, r_omt = omr.get()
                gat, r_gat = gar.get()
                gbt, r_gbt = gbr.get()
                dma("sp", ogt[:], kview(og_s, tsl), r=[r_scr["og"]], w=[r_ogt])
                dma("sp", omt[:], kview(om_s, tsl), r=[r_scr["om"]], w=[r_omt])
                dma("sp", gat[:], kview(ga_s, tsl), r=[r_scr["ga"]], w=[r_gat])
                dma("sp", gbt[:], kview(gb_s, tsl), r=[r_scr["gb"]], w=[r_gbt])
                for d in range(KD):
                    ds_ = slice(d * 128, (d + 1) * 128)
                    bA = nb()
                    for k in range(KD):
                        mm(banks[bA][:], wa[:, k, ds_], ogt[:, k, :], k == 0, k == KD - 1, r=[r_wa, r_ogt], w=[r_bank[bA]])
                    bB = nb()
                    for k in range(KD):
                        mm(banks[bB][:], wb[:, k, ds_], omt[:, k, :], k == 0, k == KD - 1, r=[r_wb, r_omt], w=[r_bank[bB]])
                    t1, r_t1 = t1r.get()
                    t2, r_t2 = t2r.get()
                    tt("dve", t1[:], banks[bA][:], gat[:, d, :], ALU.mult, r=[r_bank[bA], r_gat], w=[r_t1])
                    tt("dve", t2[:], banks[bB][:], gbt[:, d, :], ALU.mult, r=[r_bank[bB], r_gbt], w=[r_t2])
                    tt("pool", yb[:, d, :], t1[:], t2[:], ALU.add, r=[r_t1, r_t2], w=[r_yb[d]])
                for d in range(KD):
                    ds_ = slice(d * 128, (d + 1) * 128)
                    bk = nb()
                    for k in range(KD):
                        mm(banks[bk][:], wo[:, k, ds_], yb[:, k, :], k == 0, k == KD - 1, r=[r_wo, r_yb[k]], w=[r_bank[bk]])
                    stt("dve", xt[:, d, :], banks[bk][:], vG(l, 1, d), xt[:, d, :], ALU.mult, ALU.add,
                        r=[r_bank[bk], r_vec, r_xt[d]], w=[r_xt[d]])
                    dma("sp", out_d[ds_, tsl], xt[:, d, :], r=[r_xt[d]], w=[r_res[t]], final=True)
            P.barrier()
        state["first"] = False

    stages = ("ffn1", "proj", "gdn", "attn", "mixout", "full")
    si = stages.index(stage)
    for l in range(L):
        ffn(l, 0, 0)
        if si >= 1:
            mix_proj(l, 0)
            mix_proj(l, 1)
        if si >= 2:
            mix_gdn(l)
        if si >= 3:
            mix_attn(l)
        if si >= 4:
            mix_out(l)
        if si >= 5:
            ffn(l, 2, 1)

    nwait = P.emit()
    es.close()
    return nc, dict(n_instr=len(P.instrs), nwait=nwait)


def prep_inputs(inputs, S):
    f = np.float32
    sh = {}
    sh["consts"] = make_consts()
    sh["ada_w"] = np.ascontiguousarray(inputs["ada_w"], dtype=f)
    sh["ada_bT"] = np.ascontiguousarray(np.asarray(inputs["ada_b"]).reshape(4, 72, 128).transpose(0, 2, 1), dtype=f)
    g = np.stack([np.asarray(inputs["norm_ffn1"]), np.asarray(inputs["norm_mix"]), np.asarray(inputs["norm_ffn2"])], axis=1)
    sh["gainsT"] = np.ascontiguousarray(g.reshape(4, 3, KD, 128).transpose(0, 1, 3, 2), dtype=f)
    for n in ("ffn1_w1", "ffn1_w3", "ffn1_w2", "ffn2_w1", "ffn2_w3", "ffn2_w2", "w_in", "mla_w_q_up", "mla_w_kv_up",
              "w_branch_a", "w_branch_b", "w_out"):
        sh[n] = np.ascontiguousarray(inputs[n], dtype=f)
    conv = np.asarray(inputs["gdn_conv"], dtype=f)
    sh["convT"] = np.ascontiguousarray(conv.reshape(4, 4, 24, 128).transpose(0, 3, 2, 1))
    hv = np.zeros((4, 128, 16), f)
    hv[:, :, 0:8] = np.asarray(inputs["gdn_a_log"], dtype=f)[:, None, :]
    hv[:, :, 8:16] = np.asarray(inputs["gdn_dt_bias"], dtype=f)[:, None, :]
    sh["headvec"] = hv
    pv = np.zeros((4, 128, 16), f)
    pv[:, :, 0] = np.asarray(inputs["gdn_out_gain"], dtype=f)
    pv[:, :, 1:4] = np.asarray(inputs["mla_q_lat_gain"], dtype=f).reshape(4, 3, 128).transpose(0, 2, 1)
    pv[:, :, 4:6] = np.asarray(inputs["mla_kv_lat_gain"], dtype=f).reshape(4, 2, 128).transpose(0, 2, 1)
    qn = np.asarray(inputs["mla_q_norm"], dtype=f)
    kn = np.asarray(inputs["mla_k_norm"], dtype=f)
    pv[:, :, 6] = qn[:, 0:128]
    pv[:, 0:64, 7] = qn[:, 128:192]
    pv[:, :, 8] = kn[:, 0:128]
    pv[:, 0:64, 9] = kn[:, 128:192]
    sh["partvec"] = pv
    maps = []
    B = inputs["x"].shape[0]
    for b in range(B):
        m = dict(sh)
        m["xT"] = np.ascontiguousarray(np.asarray(inputs["x"])[b, :S].T, dtype=f)
        m["cT"] = np.ascontiguousarray(np.asarray(inputs["c"])[b].reshape(KD, 128).T, dtype=f)
        m["pos"] = np.ascontiguousarray(np.asarray(inputs["positions"])[b, :S].reshape(1, S), dtype=np.int32)
        maps.append(m)
    return maps


def kernel(**inputs):
    S = inputs["x"].shape[1]
    B = inputs["x"].shape[0]
    nc, info = build_program(S, 4)
    maps = prep_inputs(inputs, S)
    res = run_bass_kernel_spmd(nc, maps, core_ids=list(range(B)))
    out = np.stack([np.ascontiguousarray(r["outT"].T) for r in res.results], axis=0)
    return out.astype(np.float32)
```
